# Optimizing a Trainium2 kernel written in Bass

```python
import math
import jax, jax.numpy as jnp
from jax import lax
import numpy as np

D_MODEL = 1024
BATCH = 32
SEQ = 2048
DEPTH = 1
DEC_BATCH = 8
DEC_SEQ = 64
PAST_LEN = 4096

CHUNK = 64
N_MOD = 9
D_FF = 2816
GMLP_CHUNK = 128
D_A = D_MODEL
GMLP_GROUPS = 8
GMLP_GROUP_DIM = D_A // GMLP_GROUPS
MLA_HEADS = 8
QK_NOPE = 64
QK_ROPE = 32
QK_HEAD = QK_NOPE + QK_ROPE
V_HEAD = 64
D_B = MLA_HEADS * V_HEAD
Q_RANK = 384
KV_RANK = 256
ROPE_THETA = 10000.0
Q_BLOCK = 128
EPS = 1e-6
NEG = -1e30
IN_SPLITS = (D_A, D_A, Q_RANK, KV_RANK, QK_ROPE, 2 * D_MODEL)
N_IN = 2 * D_A + Q_RANK + KV_RANK + QK_ROPE + 2 * D_MODEL

kernel_name = 'hybrid_gmlp_mla_macaron_streaming_step'


def rms_norm(x, g):
    x32 = x.astype(jnp.float32)
    y = x32 * lax.rsqrt(jnp.mean(x32 * x32, axis=-1, keepdims=True) + EPS)
    return (y * g.astype(jnp.float32)).astype(x.dtype)


def modulate(x, g, shift, scale):
    return rms_norm(x, g) * (1 + scale) + shift


def rope(x, pos):
    half = x.shape[-1] // 2
    freqs = ROPE_THETA ** (-jnp.arange(half, dtype=jnp.float32) / half)
    ang = pos[:, None] * freqs[None, :]
    cos = jnp.cos(ang)[None, :, None, :]
    sin = jnp.sin(ang)[None, :, None, :]
    x32 = x.astype(jnp.float32)
    x1, x2 = x32[..., :half], x32[..., half:]
    return jnp.concatenate([x1 * cos - x2 * sin, x1 * sin + x2 * cos], axis=-1).astype(x.dtype)


def split_cols(z, sizes):
    parts = []
    o = 0
    for s in sizes:
        parts.append(z[..., o:o + s])
        o += s
    return parts


def swiglu(h, w_up, w_down):
    gu = h @ w_up
    return (jax.nn.silu(gu[..., :D_FF]) * gu[..., D_FF:]) @ w_down


def gmlp_branch(z_u, z_v, g_v, ws, b):
    u = jax.nn.gelu(z_u)
    v = rms_norm(jax.nn.gelu(z_v), g_v)
    B, T, _ = v.shape
    L = min(T, GMLP_CHUNK)
    n = T // L
    mask = jnp.tril(jnp.ones((L, L), dtype=bool))
    w = jnp.where(mask[None], ws[:, :L, :L], 0).astype(v.dtype)
    vg = v.reshape(B, n, L, GMLP_GROUPS, GMLP_GROUP_DIM)
    mixed = jnp.einsum('gts,bnsgc->bntgc', w, vg) + b[:, :L].T[None, None, :, :, None]
    return u * mixed.reshape(B, T, D_A), v


def mla_queries(z_q, pos, p):
    B, T, _ = z_q.shape
    q = (rms_norm(z_q, p['g_q_lat']) @ p['w_uq']).reshape(B, T, MLA_HEADS, QK_HEAD)
    q = jnp.concatenate([q[..., :QK_NOPE], rope(q[..., QK_NOPE:], pos)], axis=-1)
    return rms_norm(q, p['g_qnorm'])


def mla_keys_values(ckv, krope, p):
    B, T, _ = ckv.shape
    k_nope = jnp.einsum('btr,rhd->bthd', ckv, p['w_uk'].reshape(KV_RANK, MLA_HEADS, QK_NOPE))
    k_rope = jnp.broadcast_to(krope[:, :, None, :], (B, T, MLA_HEADS, QK_ROPE))
    k = rms_norm(jnp.concatenate([k_nope, k_rope], axis=-1), p['g_knorm'])
    v = jnp.einsum('btr,rhd->bthd', ckv, p['w_uv'].reshape(KV_RANK, MLA_HEADS, V_HEAD))
    return k, v


def attend_block_causal(q, k, v):
    B, T, H, _ = q.shape
    nq = T // Q_BLOCK
    scale = QK_HEAD ** -0.5
    qb = q.reshape(B, nq, Q_BLOCK, H, QK_HEAD).transpose(1, 0, 2, 3, 4)
    kchunk = jnp.arange(T) // CHUNK
    qchunk = kchunk.reshape(nq, Q_BLOCK)

    def block(args):
        qi, qc = args
        s = jnp.einsum('bqhd,bkhd->bhqk', qi, k, preferred_element_type=jnp.float32) * scale
        s = jnp.where(kchunk[None, None, None, :] <= qc[None, None, :, None], s, NEG)
        pr = jax.nn.softmax(s, axis=-1).astype(v.dtype)
        return jnp.einsum('bhqk,bkhd->bqhd', pr, v)

    o = lax.map(block, (qb, qchunk))
    return o.transpose(1, 0, 2, 3, 4).reshape(B, T, H * V_HEAD)


def attend_all(q, k, v):
    B, T, H, _ = q.shape
    s = jnp.einsum('bqhd,bkhd->bhqk', q, k, preferred_element_type=jnp.float32) * (QK_HEAD ** -0.5)
    pr = jax.nn.softmax(s, axis=-1).astype(v.dtype)
    return jnp.einsum('bhqk,bkhd->bqhd', pr, v).reshape(B, T, H * V_HEAD)


def layer(x, c, pos, past_ckv, past_krope, p):
    B, T, _ = x.shape
    m = (jax.nn.silu(c) @ p['w_mod'] + p['b_mod']).reshape(B, N_MOD, D_MODEL)[:, :, None, :]
    sh1, sc1, gt1, sh2, sc2, gt2, sh3, sc3, gt3 = [m[:, i] for i in range(N_MOD)]
    x = x + 0.5 * gt1 * swiglu(modulate(x, p['g_ffn1'], sh1, sc1), p['w_ffn1_up'], p['w_ffn1_down'])
    h = modulate(x, p['g_mix'], sh2, sc2)
    z = h @ p['w_in']
    z_u, z_v, z_q, z_kv, z_kr, z_g = split_cols(z, IN_SPLITS)
    o_a, v_a = gmlp_branch(z_u, z_v, p['g_gmlp_v'], p['gmlp_ws'], p['gmlp_b'])
    q = mla_queries(z_q, pos, p)
    ckv = rms_norm(z_kv, p['g_kv_lat'])
    krope = rope(z_kr[:, :, None, :], pos)[:, :, 0, :]
    if past_ckv is None:
        k, v = mla_keys_values(ckv, krope, p)
        o_b = attend_block_causal(q, k, v)
    else:
        k, v = mla_keys_values(jnp.concatenate([past_ckv, ckv], axis=1),
                               jnp.concatenate([past_krope, krope], axis=1), p)
        o_b = attend_all(q, k, v)
    gates = jax.nn.sigmoid(z_g + p['b_gate'])
    merged = gates[..., :D_MODEL] * (o_a @ p['w_branch_a']) + gates[..., D_MODEL:] * (o_b @ p['w_branch_b'])
    x = x + gt2 * (merged @ p['w_out'])
    x = x + 0.5 * gt3 * swiglu(modulate(x, p['g_ffn2'], sh3, sc3), p['w_ffn2_up'], p['w_ffn2_down'])
    return x, ckv, krope, v_a


def setup_inputs(seed: int = 0) -> dict:
    key = jax.random.key(seed)
    ks = jax.random.split(key, 32)

    def nrm(i, shape, scale):
        return jax.random.normal(ks[i], shape, jnp.float32) * scale

    def w(i, fan_in, fan_out, mult=1.0):
        return nrm(i, (DEPTH, fan_in, fan_out), mult * fan_in ** -0.5)

    def gain(i, n):
        return 1.0 + nrm(i, (DEPTH, n), 0.02)

    return {
        'x_prompt': nrm(0, (BATCH, SEQ, D_MODEL), 1.0),
        'x_sample': nrm(1, (DEC_BATCH, DEC_SEQ, D_MODEL), 1.0),
        'c_prompt': nrm(2, (BATCH, D_MODEL), 1.0),
        'c_sample': nrm(3, (DEC_BATCH, D_MODEL), 1.0),
        'cache_ckv': nrm(4, (DEPTH, DEC_BATCH, PAST_LEN, KV_RANK), 1.0),
        'cache_krope': nrm(5, (DEPTH, DEC_BATCH, PAST_LEN, QK_ROPE), 1.0),
        'w_mod': w(6, D_MODEL, N_MOD * D_MODEL, 0.5),
        'b_mod': nrm(7, (DEPTH, N_MOD * D_MODEL), 0.01),
        'g_ffn1': gain(8, D_MODEL),
        'w_ffn1_up': w(9, D_MODEL, 2 * D_FF),
        'w_ffn1_down': w(10, D_FF, D_MODEL),
        'g_mix': gain(11, D_MODEL),
        'w_in': w(12, D_MODEL, N_IN),
        'g_gmlp_v': gain(13, D_A),
        'gmlp_ws': nrm(14, (DEPTH, GMLP_GROUPS, GMLP_CHUNK, GMLP_CHUNK), GMLP_CHUNK ** -0.5),
        'gmlp_b': 1.0 + nrm(15, (DEPTH, GMLP_GROUPS, GMLP_CHUNK), 0.02),
        'g_q_lat': gain(16, Q_RANK),
        'w_uq': w(17, Q_RANK, MLA_HEADS * QK_HEAD),
        'g_kv_lat': gain(18, KV_RANK),
        'w_uk': w(19, KV_RANK, MLA_HEADS * QK_NOPE),
        'w_uv': w(20, KV_RANK, MLA_HEADS * V_HEAD),
        'g_qnorm': gain(21, QK_HEAD),
        'g_knorm': gain(22, QK_HEAD),
        'b_gate': nrm(23, (DEPTH, 2 * D_MODEL), 0.01),
        'w_branch_a': w(24, D_A, D_MODEL),
        'w_branch_b': w(25, D_B, D_MODEL),
        'w_out': w(26, D_MODEL, D_MODEL),
        'g_ffn2': gain(27, D_MODEL),
        'w_ffn2_up': w(28, D_MODEL, 2 * D_FF),
        'w_ffn2_down': w(29, D_FF, D_MODEL),
    }


def reference(x_prompt, x_sample, c_prompt, c_sample, cache_ckv, cache_krope,
              w_mod, b_mod, g_ffn1, w_ffn1_up, w_ffn1_down, g_mix, w_in, g_gmlp_v, gmlp_ws, gmlp_b,
              g_q_lat, w_uq, g_kv_lat, w_uk, w_uv, g_qnorm, g_knorm, b_gate, w_branch_a, w_branch_b,
              w_out, g_ffn2, w_ffn2_up, w_ffn2_down):
    t_p = x_prompt.shape[1]
    t_s = x_sample.shape[1]
    past = cache_ckv.shape[2]
    pos_p = jnp.arange(t_p, dtype=jnp.float32)
    pos_s = jnp.arange(t_s, dtype=jnp.float32) + jnp.float32(past)
    xp, xs = x_prompt, x_sample
    ckv_p_l, kr_p_l, ckv_s_l, kr_s_l, vg_s_l = [], [], [], [], []
    for l in range(DEPTH):
        lp = {
            'w_mod': w_mod[l], 'b_mod': b_mod[l], 'g_ffn1': g_ffn1[l], 'w_ffn1_up': w_ffn1_up[l],
            'w_ffn1_down': w_ffn1_down[l], 'g_mix': g_mix[l], 'w_in': w_in[l], 'g_gmlp_v': g_gmlp_v[l],
            'gmlp_ws': gmlp_ws[l], 'gmlp_b': gmlp_b[l], 'g_q_lat': g_q_lat[l], 'w_uq': w_uq[l],
            'g_kv_lat': g_kv_lat[l], 'w_uk': w_uk[l], 'w_uv': w_uv[l], 'g_qnorm': g_qnorm[l],
            'g_knorm': g_knorm[l], 'b_gate': b_gate[l], 'w_branch_a': w_branch_a[l],
            'w_branch_b': w_branch_b[l], 'w_out': w_out[l], 'g_ffn2': g_ffn2[l],
            'w_ffn2_up': w_ffn2_up[l], 'w_ffn2_down': w_ffn2_down[l],
        }
        xp, ckv_p, kr_p, _ = layer(xp, c_prompt, pos_p, None, None, lp)
        xs, ckv_s, kr_s, vg_s = layer(xs, c_sample, pos_s, cache_ckv[l], cache_krope[l], lp)
        ckv_p_l.append(ckv_p)
        kr_p_l.append(kr_p)
        ckv_s_l.append(ckv_s)
        kr_s_l.append(kr_s)
        vg_s_l.append(vg_s)
    new_ckv_prompt = jnp.stack(ckv_p_l, axis=0)
    new_krope_prompt = jnp.stack(kr_p_l, axis=0)
    new_ckv_sample = jnp.stack(ckv_s_l, axis=0)
    new_krope_sample = jnp.stack(kr_s_l, axis=0)
    new_gmlp_v_sample = jnp.stack(vg_s_l, axis=0)
    return (xp, xs, new_ckv_prompt, new_krope_prompt, new_ckv_sample, new_krope_sample, new_gmlp_v_sample)
```

```python
import numpy as np
import concourse.bass as bass
import concourse.mybir as mybir
from concourse.bass_utils import run_bass_kernel_spmd

F32 = mybir.dt.float32
BF16 = mybir.dt.bfloat16
U8 = mybir.dt.uint8
AF = mybir.ActivationFunctionType
ALU = mybir.AluOpType
ESZ = {F32: 4, BF16: 2, U8: 1}

D = 1024
DFF = 2816
NJ = 22
SEQ = 2048
NSEQ = 4
TS = 64
PAST = 4096
EPS = 1e-6
QSCALE = 96.0 ** -0.5

CH_F1U, CH_F1D = 0, 11
CH_Q, CH_KV, CH_U0, CH_V0, CH_MG0, CH_O0 = 19, 20, 21, 23, 25, 33
CH_F2U, CH_F2D = 35, 46
NCH = 54
CW = 4096

WS_Q, WS_QS, WS_K, WS_V, WS_ST, WS_EMB, WS_ID, WS_N = 0, 2304, 3072, 4608, 5632, 6656, 6752, 6880
VC_G1, VC_G2, VC_G3, VC_BMOD, VC_GQL, VC_BG, VC_GQ, VC_GK, VC_GV, VC_EPS, VC_ID5, VC_N = 0, 8, 16, 24, 96, 99, 115, 116, 117, 125, 126, 131

PAGE = 256
DEBUG = {}
CFG = {"nseq": NSEQ, "sample": True}


class Ins:
    __slots__ = ("eng", "fn", "deps", "marked", "ticket", "isdma", "dsem", "dval", "label")

    def __init__(self, eng, fn, isdma=False):
        self.eng = eng
        self.fn = fn
        self.deps = []
        self.marked = False
        self.ticket = 0
        self.isdma = isdma
        self.dsem = None
        self.dval = 0


class Prog:
    ENGS = ("pe", "act", "dve", "pool", "sp")

    def __init__(self, nc):
        self.nc = nc
        self.streams = {e: [] for e in self.ENGS}
        self.w = {}
        self.r = {}
        self.dmacount = {e: 0 for e in self.ENGS}
        self.dmahist = {e: [] for e in self.ENGS}
        self.KRING = 8
        self.out_dmas = []
        self.kcache = {}
        self.label = ''

    def keys(self, ap):
        name = ap.tensor.name
        if name.startswith("ps"):
            return [("ps", int(name[2:]))]
        if name != "arena":
            return []
        es = ESZ[ap.dtype]
        pairs = [tuple(x) for x in ap.ap]
        ck = (ap.offset, tuple(pairs), es)
        got = self.kcache.get(ck)
        if got is not None:
            return got
        pstride = pairs[0][0]
        lo = ap.offset % pstride if pstride > 0 else ap.offset

        def expand(off, dims):
            dims = [d for d in dims if d[1] > 1]
            if not dims:
                return [(off, off)]
            st, cnt = dims[0]
            rest = dims[1:]
            ext = sum((c - 1) * s_ for s_, c in rest)
            if st <= ext + 1 or cnt > 32:
                return [(off, off + (cnt - 1) * st + ext)]
            out = []
            for i in range(cnt):
                out += expand(off + i * st, rest)
            return out

        pages = set()
        for (a, b) in expand(lo, pairs[1:]):
            for p in range(a * es // PAGE, ((b + 1) * es - 1) // PAGE + 1):
                pages.add(p)
        got = [("sb", p) for p in sorted(pages)]
        self.kcache[ck] = got
        return got

    def _dep(self, ins, prod, kind):
        if prod is None or prod is ins:
            return
        if (not prod.isdma) and (not ins.isdma) and prod.eng == ins.eng:
            if ins.eng == "pe":
                return
        if prod not in ins.deps:
            ins.deps.append(prod)
            prod.marked = True

    def add(self, eng, fn, reads=(), writes=(), isdma=False, rkeys=(), wkeys=()):
        ins = Ins(eng, fn, isdma)
        ins.label = self.label
        rk = list(rkeys)
        for ap in reads:
            if ap is not None and not isinstance(ap, (int, float)):
                rk += self.keys(ap)
        wk = list(wkeys)
        for ap in writes:
            if ap is not None:
                wk += self.keys(ap)
        for k in rk:
            self._dep(ins, self.w.get(k), "raw")
            if k[0] == "ps":
                for rd in self.r.get(k, ()):
                    if rd.eng != ins.eng:
                        self._dep(ins, rd, "rar")
        for k in wk:
            self._dep(ins, self.w.get(k), "waw")
            for rd in self.r.get(k, ()):
                self._dep(ins, rd, "war")
        for k in rk:
            self.r.setdefault(k, []).append(ins)
        for k in wk:
            self.w[k] = ins
            self.r[k] = []
        if isdma:
            i = self.dmacount[eng]
            self.dmacount[eng] += 1
            hist = self.dmahist[eng]
            if i >= self.KRING:
                prev = hist[i - self.KRING]
                if prev not in ins.deps:
                    ins.deps.append(prev)
            hist.append(ins)
            ins.dval = 16 * (i // self.KRING + 1)
            ins.dsem = (eng, i % self.KRING)
        self.streams[eng].append(ins)
        return ins

    def emit(self):
        nc = self.nc
        import contextlib
        with contextlib.ExitStack() as es:
            esem = {e: es.enter_context(nc.semaphore("done_" + e)) for e in ("pe", "act", "dve", "pool")}
            dsem = {}
            for e in self.ENGS:
                if self.dmacount[e] > 0:
                    for i in range(self.KRING):
                        dsem[(e, i)] = es.enter_context(nc.semaphore("dma_%s_%d" % (e, i)))
            for e in ("pe", "act", "dve", "pool"):
                t = 0
                for ins in self.streams[e]:
                    if ins.isdma:
                        continue
                    if ins.marked:
                        t += 1
                        ins.ticket = t
            block = es.enter_context(nc.Block())

            def run(ename, eng):
                seen = {}
                for ins in self.streams[ename]:
                    for p in ins.deps:
                        if p.isdma:
                            key, val, sem = p.dsem, p.dval, dsem[p.dsem]
                        else:
                            key, val, sem = p.eng, p.ticket, esem[p.eng]
                        if seen.get(key, 0) >= val:
                            continue
                        seen[key] = val
                        eng.wait_ge(sem, val)
                    r = ins.fn(eng)
                    if ins.isdma:
                        r.then_inc(dsem[ins.dsem], 16)
                    elif ins.marked:
                        r.then_inc(esem[ins.eng], 1)

            @block.tensor
            def _(e):
                run("pe", e)

            @block.scalar
            def _(e):
                run("act", e)

            @block.vector
            def _(e):
                run("dve", e)

            @block.gpsimd
            def _(e):
                run("pool", e)

            @block.sync
            def _(e):
                run("sp", e)

    def mm(self, out, lhsT, rhs, start=True, stop=True):
        return self.add("pe", lambda e: e.matmul(out, lhsT=lhsT, rhs=rhs, start=start, stop=stop),
                        reads=[lhsT, rhs], writes=[out])

    def tr(self, out, in_, ident):
        return self.add("pe", lambda e: e.transpose(out, in_, ident), reads=[in_, ident], writes=[out])

    def act(self, out, in_, func, bias=None, scale=1.0, accum_out=None):
        kw = {}
        if bias is not None:
            kw["bias"] = bias
        if accum_out is not None:
            kw["accum_out"] = accum_out
        return self.add("act", lambda e: e.activation(out=out, in_=in_, func=func, scale=scale, **kw),
                        reads=[in_, bias, scale], writes=[out, accum_out])

    def tt(self, out, in0, in1, op, eng="dve"):
        return self.add(eng, lambda e: e.tensor_tensor(out=out, in0=in0, in1=in1, op=op),
                        reads=[in0, in1], writes=[out])

    def ts(self, out, in0, s1, op0, s2=None, op1=None, eng="dve"):
        if op1 is None:
            fn = lambda e: e.tensor_scalar(out=out, in0=in0, scalar1=s1, scalar2=None, op0=op0)
        else:
            fn = lambda e: e.tensor_scalar(out=out, in0=in0, scalar1=s1, scalar2=s2, op0=op0, op1=op1)
        return self.add(eng, fn, reads=[in0, s1, s2], writes=[out])

    def stt(self, out, in0, scalar, in1, op0, op1):
        return self.add("dve", lambda e: e.scalar_tensor_tensor(out=out, in0=in0, scalar=scalar, in1=in1, op0=op0, op1=op1),
                        reads=[in0, scalar, in1], writes=[out])

    def copy(self, out, in_, eng="dve"):
        return self.add(eng, lambda e: e.tensor_copy(out=out, in_=in_), reads=[in_], writes=[out])

    def memset(self, out, val, eng="pool"):
        return self.add(eng, lambda e: e.memset(out, val), writes=[out])

    def dma(self, out, in_, q="pool", rkeys=(), wkeys=(), is_out=False):
        ins = self.add(q, lambda e: e.dma_start(out=out, in_=in_), reads=[in_], writes=[out], isdma=True,
                       rkeys=rkeys, wkeys=wkeys)
        if is_out:
            self.out_dmas.append(ins)
        return ins

    def finish(self):
        ins = Ins("sp", lambda e: e.nop())
        for d in self.out_dmas:
            ins.deps.append(d)
        for e in ("pe", "act", "dve", "pool"):
            if self.streams[e]:
                last = [x for x in self.streams[e] if not x.isdma]
                if last:
                    last[-1].marked = True
                    ins.deps.append(last[-1])
        self.streams["sp"].append(ins)


def build_program():
    nc = bass.Bass("TRN2", target_bir_lowering=False)
    nseq = CFG["nseq"]

    def din(name, shape, dt=F32):
        return nc.dram_tensor(name, list(shape), dt, kind="ExternalInput").ap()

    def dout(name, shape, dt=F32):
        return nc.dram_tensor(name, list(shape), dt, kind="ExternalOutput").ap()

    d_xp = din("xp", [NSEQ, 2, 128, 8192])
    d_xs = din("xs", [128, 8 * TS])
    d_ct = din("ct", [128, 40])
    d_wmod = din("wmod", [1024, 9216])
    d_wch = din("wch", [NCH, 128, CW])
    d_wsm = din("wsmall", [128, WS_N])
    d_mask = din("maskT", [128, 128])
    d_vecs = din("vecs", [128, VC_N])
    d_gkvb = din("gkvb", [128, 256])
    d_bb = din("bb", [128, 1024])
    d_gvb = din("gvb", [64, 1024])
    d_rqc = din("ropeqc", [32, SEQ + TS])
    d_rqs = din("ropeqs", [32, SEQ + TS])
    d_rkc = din("ropekc", [SEQ + TS, 32])
    d_rks = din("ropeks", [SEQ + TS, 32])
    d_cckv = din("cckv", [128, 2, PAST])
    d_ckr = din("ckr", [32, PAST])

    d_yp = dout("yp", [NSEQ, 2, 128, 8192])
    d_ys = dout("ys", [128, 8 * TS])
    d_ckvp = dout("ckvp", [NSEQ, SEQ, 256])
    d_krp = dout("krp", [NSEQ, SEQ, 32])
    d_ckvs = dout("ckvs", [TS, 256])
    d_krs = dout("krs", [TS, 32])
    d_vgs = dout("vgs", [TS, 1024])
    d_scr = nc.dram_tensor("wscr", [NCH, 128, CW], BF16, kind="Internal").ap()
    dbg = {}
    for name, shape in DEBUG.items():
        dbg[name] = dout("dbg_" + name, shape)

    import contextlib
    es = contextlib.ExitStack()
    ASZ = 212736
    arena = es.enter_context(nc.sbuf_tensor("arena", [128, ASZ], U8))
    psb = [es.enter_context(nc.psum_tensor("ps%d" % i, [128, 512], F32)) for i in range(8)]
    P = Prog(nc)

    def view(off, dt, shape):
        n = 1
        for s in shape:
            n *= s
        nbytes = n * ESZ[dt]
        assert off % 4 == 0 and off + nbytes <= ASZ, (off, nbytes)
        a = arena[:, off:off + nbytes].bitcast(dt)
        if len(shape) == 1:
            return a
        if len(shape) == 2:
            return a.rearrange("p (a b) -> p a b", a=shape[0])
        if len(shape) == 3:
            return a.rearrange("p (a b c) -> p a b c", a=shape[0], b=shape[1])
        raise ValueError

    class Alloc:
        def __init__(self, base, limit):
            self.base, self.p, self.limit = base, base, limit

        def take(self, nbytes):
            nbytes = (nbytes + PAGE - 1) // PAGE * PAGE
            o = self.p
            self.p += nbytes
            assert self.p <= self.limit, ("arena overflow", self.p, self.limit)
            return o

        def v(self, dt, shape):
            n = 1
            for s in shape:
                n *= s
            return view(self.take(n * ESZ[dt]), dt, shape)

    RA = Alloc(0, ASZ)
    wsm = RA.v(BF16, [WS_N])
    wq = wsm[:, WS_Q:WS_QS].rearrange("p (a b c) -> p a b c", a=3, b=8)
    wqs = wsm[:, WS_QS:WS_K].rearrange("p (a b c) -> p a b c", a=3, b=8)
    wk = wsm[:, WS_K:WS_V].rearrange("p (a b c) -> p a b c", a=2, b=8)
    wv = wsm[:, WS_V:WS_ST].rearrange("p (a b) -> p a b", a=2)
    wst = wsm[:, WS_ST:WS_EMB].rearrange("p (a b) -> p a b", a=8)
    emb = wsm[:, WS_EMB:WS_ID]
    ident = wsm[:, WS_ID:WS_N]
    ones_bf = RA.v(BF16, [128])
    ones_f = RA.v(F32, [64])
    vecs = RA.v(F32, [VC_N])
    eps_t = vecs[:, VC_EPS:VC_EPS + 1]
    modt = RA.v(F32, [72, 5])
    gs = RA.v(F32, [3, 8, 5])
    gt = RA.v(F32, [3, 8, 5])
    gkvb = RA.v(F32, [256])
    bb = RA.v(F32, [8, 128])
    RING0 = RA.take(4 * 8192)
    RES_END = RA.p

    ring_state = {"n": 0}

    def stream(cid):
        i = ring_state["n"]
        ring_state["n"] += 1
        off = RING0 + (i % 4) * 8192
        slot = view(off, BF16, [CW])
        P.dma(slot, d_scr[cid], q="sp", rkeys=[("scr", cid)])
        return slot

    bank_state = {"n": 0}

    def bank():
        i = bank_state["n"]
        bank_state["n"] += 1
        return psb[i % 6]

    acc_state = {"n": 0}

    def acc_bank():
        i = acc_state["n"]
        acc_state["n"] += 1
        return psb[6 + i % 2]

    def sh_ap(i, k, s):
        return modt[:, (3 * i) * 8 + k, s:s + 1]

    def sh_ap2(i, k, s):
        return modt[:, (3 * i) * 8 + k, s:s + 1]

    class _Stop(Exception):
        pass

    def checkpoint(name):
        if CFG.get('stop') == name:
            raise _Stop()

    def dump(name, ap):
        if name in dbg:
            P.dma(dbg[name], ap, is_out=True)

    def phase_layout(A, N, sample):
        L = {}
        base = A.p
        nsub = 2 if not sample else 1
        F = Alloc(base, ASZ)
        L["fh"] = F.v(BF16, [nsub, 8, N])
        fg0 = F.p
        L["fg"] = F.v(BF16, [nsub, NJ, N])
        fend = F.p
        L["xsqs"] = [view(fg0 + i * 8 * N * 2, BF16, [8, N]) for i in range(nsub)]
        M = Alloc(base, ASZ)
        L["mh"] = M.v(BF16, [8, N])
        if sample:
            L["va"] = M.v(F32, [1, 1024])
            L["vb"] = M.v(BF16, [1, 1024])
            L["vout"] = M.v(F32, [1024])
            L["gvb"] = M.v(F32, [1024])
            L["zq"] = M.v(F32, [3, N])
            L["xsq"] = M.v(BF16, [8, N])
        else:
            a0 = M.take(8192)
            L["va"] = view(a0, BF16, [4, 1024])
            L["zq"] = view(a0, F32, [3, N])
            L["xsq"] = view(a0, BF16, [8, N])
        q0 = M.take(6 * N * 2)
        L["qn"] = view(q0, BF16, [3, N])
        L["xsq3"] = view(q0 + 3 * N * 2, BF16, [3, N])
        L["Qh"] = M.v(BF16, [8, N])
        L["u"] = M.v(BF16, [8, N])
        L["ob"] = M.v(BF16, [8, N])
        NBm = max(1, N // 128)
        tsz = max(256, 4 * N) + max(256, 2 * N)
        kvsz = NBm * (1024 + 512 + 128) + max(256, NBm * 64) + 2 * max(256, NBm * 128)
        atsz = 6 * 1024 + 2 * max(256, N * 4)
        ksz = max(8 * N * 2, tsz + max(kvsz, atsz))
        k0 = M.take(ksz)
        L["oa"] = view(k0, BF16, [8, N])
        o = k0
        L["ckvT"] = view(o, BF16, [2, N]); o += max(256, 4 * N)
        L["krT"] = view(o, BF16, [N]); o += max(256, 2 * N)
        o1 = o
        L["ckv_o"] = view(o, F32, [NBm, 256]); o += NBm * 1024
        L["ckv_b"] = view(o, BF16, [NBm, 256]); o += NBm * 512
        L["kr_o"] = view(o, F32, [NBm, 32]); o += NBm * 128
        L["kr_b"] = view(o, BF16, [NBm, 32]); o += max(256, NBm * 64)
        L["rCt"] = view(o, F32, [NBm, 32]); o += max(256, NBm * 128)
        L["rSt"] = view(o, F32, [NBm, 32]); o += max(256, NBm * 128)
        assert o <= k0 + ksz
        o = o1
        L["pt"] = [view(o + i * 1024, BF16, [512]) for i in range(3)]
        L["ptd"] = [view(o + (3 + i) * 1024, BF16, [512]) for i in range(3)]
        o += 6144
        L["rC"] = view(o, F32, [N]); o += max(256, N * 4)
        L["rS"] = view(o, F32, [N]); o += max(256, N * 4)
        assert o <= k0 + ksz
        if 3 * N * 2 >= 2048:
            L["xs96"] = [view(q0 + 3 * N * 2 + i * 1024, BF16, [512]) for i in range(2)]
        else:
            L["xs96"] = [M.v(BF16, [512]) for _ in range(2)]
        mend = M.p
        S = Alloc(max(fend, mend), ASZ)
        for t in range(6):
            L["T%d" % t] = S.v(F32, [512])
        L["sm"] = S.v(F32, [64])
        L["base"] = base
        A.p = S.p
        return L

    PA = Alloc(RES_END, ASZ)
    KTp = PA.v(BF16, [8, SEQ])
    Vp = PA.v(BF16, [16, 8, 65])
    xt = PA.v(F32, [2, 8, 512])
    Lp = phase_layout(PA, 512, False)
    xt_flat = xt.rearrange("p a b c -> p (a b c)")

    def xdram(d, sq, st):
        return d[sq, st].rearrange("p (a b c) -> p a b c", a=2, b=8)

    PH0 = Alloc(Lp["base"], ASZ)
    P.memset(ones_bf, 1.0)
    P.memset(ones_f, 1.0)
    P.dma(vecs, d_vecs)
    P.dma(gkvb, d_gkvb)
    P.dma(bb.rearrange("p a b -> p (a b)"), d_bb)
    ctile = PH0.v(F32, [40])
    csil = PH0.v(F32, [8, 5])
    P.dma(ctile, d_ct)
    if nseq > 0:
        P.dma(xt_flat, d_xp[0, 0])
    P.act(csil.rearrange("p a b -> p (a b)"), ctile, AF.Silu)
    wst32 = PH0.v(F32, [WS_N])
    mk = PH0.v(F32, [128])
    P.dma(wst32, d_wsm)
    P.dma(mk, d_mask)
    for g in range(8):
        sl = wst32[:, WS_ST + g * 128: WS_ST + (g + 1) * 128]
        P.tt(sl, sl, mk, ALU.mult)
    P.copy(wsm[:, 0:3440], wst32[:, 0:3440], eng="dve")
    P.copy(wsm[:, 3440:WS_N], wst32[:, 3440:WS_N], eng="dve")
    wmv = d_wmod.rearrange("(k p) c -> p k c", p=128)
    mtbs = [PH0.v(F32, [512]) for _ in range(2)]
    for blk in range(18):
        stg = view(RING0 + (blk % 2) * 16384, F32, [8, 512])
        P.dma(stg, wmv[:, :, blk * 512:(blk + 1) * 512], q="sp", wkeys=[("wm", blk)])
        ps = bank()
        for k in range(8):
            P.mm(ps[0:5, 0:512], csil[:, k, :], stg[:, k, :], start=(k == 0), stop=(k == 7))
        mtb = mtbs[blk % 2]
        P.copy(mtb[0:5, :], ps[0:5, 0:512])
        pt_ = bank()
        for m4 in range(4):
            P.tr(pt_[:, m4 * 8:m4 * 8 + 5], mtb[0:5, m4 * 128:(m4 + 1) * 128], vecs[0:5, VC_ID5:VC_ID5 + 5])
        for m4 in range(4):
            col = blk * 4 + m4
            P.ts(modt[:, col, :], pt_[:, m4 * 8:m4 * 8 + 5], vecs[:, VC_BMOD + col:VC_BMOD + col + 1], ALU.add)
    for c in range(NCH):
        P.dma(d_scr[c].rearrange("p (a b) -> p a b", b=2048), d_wch[c].rearrange("p (a b) -> p a b", b=2048),
              q="pool", rkeys=[("wm", min(17, 12 + c))], wkeys=[("scr", c)])
    for i in range(3):
        gbase = (VC_G1, VC_G2, VC_G3)[i]
        for k in range(8):
            P.ts(gs[:, i, k, :], modt[:, (3 * i + 1) * 8 + k, :], 1.0, ALU.add,
                 vecs[:, gbase + k:gbase + k + 1], ALU.mult)
        P.ts(gt[:, i].rearrange("p a b -> p (a b)"),
             modt[:, (3 * i + 2) * 8:(3 * i + 3) * 8, :].rearrange("p a b -> p (a b)"),
             0.5 if i != 1 else 1.0, ALU.mult)

    def norm_multi(items, i, s, N, use_pool=True):
        for (xsub, h_out, xsq, tln, trs, t1s) in items:
            if use_pool:
                P.act(xsq[:, 0:6, 0:N], xsub[:, 0:6, :], AF.Square)
                for k in (6, 7):
                    P.tt(xsq[:, k, 0:N], xsub[:, k, :], xsub[:, k, :], ALU.mult, eng="pool")
            else:
                P.act(xsq[:, :, 0:N], xsub, AF.Square)
        pss = []
        for (xsub, h_out, xsq, tln, trs, t1s) in items:
            ps = bank()
            for k in range(8):
                P.mm(ps[:, 0:N], ones_bf, xsq[:, k, 0:N], start=(k == 0), stop=(k == 7))
            pss.append(ps)
        for idx, (xsub, h_out, xsq, tln, trs, t1s) in enumerate(items):
            P.act(tln[:, 0:N], pss[idx][:, 0:N], AF.Ln, bias=eps_t, scale=1.0 / D)
            P.act(trs[:, 0:N], tln[:, 0:N], AF.Exp, scale=-0.5)
        for (xsub, h_out, xsq, tln, trs, t1s) in items:
            for k in range(8):
                t1 = t1s[k % len(t1s)]
                P.tt(t1[:, 0:N], xsub[:, k, :], trs[:, 0:N], ALU.mult)
                P.act(h_out[:, k, :], t1[:, 0:N], AF.Identity, bias=sh_ap2(i, k, s), scale=gs[:, i, k, s:s + 1])

    def ffn(xv, N, nsub, cbase_u, cbase_d, i, s, L, after_m=None, use_pool=True):
        h, g = L["fh"], L["fg"]
        items = []
        for sub in range(nsub):
            tb_ = [(L["T0"], L["T1"], [L["T4"]]), (L["T2"], L["T3"], [L["T5"]])][sub]
            items.append((xv[:, sub], h[:, sub], L["xsqs"][sub], tb_[0], tb_[1], tb_[2]))
        P.label = "ffn%d.norm" % i
        norm_multi(items, i, s, N, use_pool=use_pool)
        P.label = "ffn%d.up" % i
        for gi in range(11):
            w = stream(cbase_u + gi).rearrange("p (k j c) -> p k j c", k=8, j=2)
            for jj in range(2):
                j = 2 * gi + jj
                for sub in range(nsub):
                    pg, pu = bank(), bank()
                    for k in range(8):
                        P.mm(pg[:, 0:N], w[:, k, jj, 0:128], h[:, sub, k, :], start=(k == 0), stop=(k == 7))
                    for k in range(8):
                        P.mm(pu[:, 0:N], w[:, k, jj, 128:256], h[:, sub, k, :], start=(k == 0), stop=(k == 7))
                    sil = L["T2"] if (j + sub) % 2 == 0 else L["T3"]
                    P.act(sil[:, 0:N], pg[:, 0:N], AF.Silu)
                    P.tt(g[:, sub, j, :], pu[:, 0:N], sil[:, 0:N], ALU.mult)
        P.label = "ffn%d.down" % i
        for m in range(8):
            w = stream(cbase_d + m)[:, 0:NJ * 128].rearrange("p (j c) -> p j c", j=NJ)
            for sub in range(nsub):
                po = bank()
                for j in range(NJ):
                    P.mm(po[:, 0:N], w[:, j, :], g[:, sub, j, :], start=(j == 0), stop=(j == NJ - 1))
                P.stt(xv[:, sub, m, :], po[:, 0:N], gt[:, i, m, s:s + 1], xv[:, sub, m, :], ALU.mult, ALU.add)
            if after_m is not None:
                after_m(m)

    def kv_heads(ckvT, krT, n, kcol0, L, KT):
        def proj(hd):
            pk = bank()
            P.mm(pk[0:96, 0:n], wk[:, 0, hd, :], ckvT[:, 0, 0:n], start=True, stop=False)
            P.mm(pk[0:96, 0:n], wk[:, 1, hd, :], ckvT[:, 1, 0:n], start=False, stop=False)
            P.mm(pk[0:96, 0:n], emb[0:32, :], krT[0:32, 0:n], start=False, stop=True)
            xs = L["xs96"][hd % 2]
            P.act(xs[0:96, 0:n], pk[0:96, 0:n], AF.Square)
            return pk, xs

        def fin(hd, pk, xs):
            pn = bank()
            ta, tb_ = (L["T0"], L["T1"]) if hd % 2 == 0 else (L["T2"], L["T3"])
            P.mm(pn[0:96, 0:n], ones_bf[0:96, 0:96], xs[0:96, 0:n])
            P.act(ta[0:96, 0:n], pn[0:96, 0:n], AF.Ln, bias=eps_t[0:96], scale=1.0 / 96)
            P.act(tb_[0:96, 0:n], ta[0:96, 0:n], AF.Exp, scale=-0.5)
            P.stt(KT[0:96, hd, kcol0:kcol0 + n], pk[0:96, 0:n], vecs[0:96, VC_GK:VC_GK + 1], tb_[0:96, 0:n],
                  ALU.mult, ALU.mult)
        prev = None
        for hd in range(8):
            cur = proj(hd)
            if prev is not None:
                fin(hd - 1, *prev)
            prev = cur
        fin(7, *prev)

    def v_rows(ckvT, n, blk0, V):
        TBk = min(128, n)
        for tb in range(n // TBk):
            pv = bank()
            for c in range(2):
                P.mm(pv[0:TBk, 0:512], ckvT[:, c, tb * TBk:(tb + 1) * TBk], wv[:, c, :], start=(c == 0), stop=(c == 1))
            P.copy(V[0:TBk, blk0 + tb, :, 0:64], pv[0:TBk, 0:512].rearrange("p (a b) -> p a b", a=8))

    def build_kv(ckvT, krT, n, kcol0, blk0, L, KT, V):
        kv_heads(ckvT, krT, n, kcol0, L, KT)
        v_rows(ckvT, n, blk0, V)

    mix_count = {"n": 0}

    def mixer(xsub, N, s, sample, t0, L, KT, V, d_ckv_rows, d_kr_rows, pos0):
        first = (mix_count["n"] == 0)
        mix_count["n"] += 1
        TB = min(128, N)
        NB = N // TB
        h = L["mh"]
        sm = L["sm"]
        zq, xsq3, qn, Qh = L["zq"], L["xsq3"], L["qn"], L["Qh"]
        ckv_o, ckv_b, kr_o, kr_b = L["ckv_o"], L["ckv_b"], L["kr_o"], L["kr_b"]
        ckvT, krT, rCt, rSt = L["ckvT"], L["krT"], L["rCt"], L["rSt"]
        u, va, oa, ob = L["u"], L["va"], L["oa"], L["ob"]
        P.label = "mix.norm"
        norm_multi([(xsub, h, L["xsq"], L["T0"], L["T1"], [L["T4"], L["T5"]])], 1, s, N)
        P.label = "mix.zq_kv"
        P.dma(rCt[0:TB, 0:NB, :], d_rkc[pos0:pos0 + N, :].rearrange("(a p) f -> p a f", p=TB))
        P.dma(rSt[0:TB, 0:NB, :], d_rks[pos0:pos0 + N, :].rearrange("(a p) f -> p a f", p=TB))
        wqc = stream(CH_Q)[:, 0:8 * 384].rearrange("p (k c) -> p k c", k=8)
        for mq in range(3):
            ps = bank()
            for k in range(8):
                P.mm(ps[:, 0:N], wqc[:, k, mq * 128:(mq + 1) * 128], h[:, k, :], start=(k == 0), stop=(k == 7))
            P.act(zq[:, mq, 0:N], ps[:, 0:N], AF.Identity)
            P.tt(xsq3[:, mq, 0:N], zq[:, mq, 0:N], zq[:, mq, 0:N], ALU.mult, eng="pool")
        wkvc = stream(CH_KV)[:, 0:8 * 320].rearrange("p (k c) -> p k c", k=8)
        pks = []
        for tb in range(NB):
            pk = bank()
            for k in range(8):
                P.mm(pk[0:TB, 0:320], h[:, k, tb * TB:(tb + 1) * TB], wkvc[:, k, :], start=(k == 0), stop=(k == 7))
            pks.append(pk)
        ps = bank()
        for mq in range(3):
            P.mm(ps[:, 0:N], ones_bf, xsq3[:, mq, 0:N], start=(mq == 0), stop=(mq == 2))
        P.act(L["T0"][:, 0:N], ps[:, 0:N], AF.Ln, bias=eps_t, scale=1.0 / 384)
        P.act(L["T1"][:, 0:N], L["T0"][:, 0:N], AF.Exp, scale=-0.5)
        for mq in range(3):
            P.stt(qn[:, mq, 0:N], zq[:, mq, 0:N], vecs[:, VC_GQL + mq:VC_GQL + mq + 1], L["T1"][:, 0:N],
                  ALU.mult, ALU.mult)
        for tb in range(NB):
            pk = pks[tb]
            P.act(L["T4"][0:TB, 0:256], pk[0:TB, 0:256], AF.Square, accum_out=sm[0:TB, tb:tb + 1])
            P.act(sm[0:TB, 8 + tb:9 + tb], sm[0:TB, tb:tb + 1], AF.Ln, bias=eps_t[0:TB], scale=1.0 / 256)
            P.act(sm[0:TB, 16 + tb:17 + tb], sm[0:TB, 8 + tb:9 + tb], AF.Exp, scale=-0.5)
            P.stt(ckv_o[0:TB, tb, :], pk[0:TB, 0:256], sm[0:TB, 16 + tb:17 + tb], gkvb[0:TB, :], ALU.mult, ALU.mult)
            P.copy(ckv_b[0:TB, tb, :], ckv_o[0:TB, tb, :], eng="pool")
            P.tt(L["T5"][0:TB, 0:32], pk[0:TB, 256:288], rCt[0:TB, tb, :], ALU.mult)
            P.tt(L["T5"][0:TB, 32:64], pk[0:TB, 288:320], rSt[0:TB, tb, :], ALU.mult)
            P.tt(kr_o[0:TB, tb, :], L["T5"][0:TB, 0:32], L["T5"][0:TB, 32:64], ALU.add)
            P.copy(kr_b[0:TB, tb, :], kr_o[0:TB, tb, :], eng="pool")
        P.dma(d_ckv_rows.rearrange("(a p) f -> p a f", p=TB), ckv_o[0:TB, 0:NB, :], is_out=True)
        P.dma(d_kr_rows.rearrange("(a p) f -> p a f", p=TB), kr_o[0:TB, 0:NB, :], is_out=True)
        P.label = "mix.vu"
        for half in range(2):
            w = stream(CH_V0 + half).rearrange("p (k c) -> p k c", k=8)
            for tb in range(NB):
                ps = bank()
                for k in range(8):
                    P.mm(ps[0:TB, 0:512], h[:, k, tb * TB:(tb + 1) * TB], w[:, k, :], start=(k == 0), stop=(k == 7))
                vsl = va[0:TB, tb, half * 512:(half + 1) * 512]
                P.act(vsl, ps[0:TB, 0:512], AF.Gelu_apprx_tanh)
                P.act(L["T4"][0:TB, 0:512], vsl, AF.Square, accum_out=sm[0:TB, 24 + 2 * tb + half:25 + 2 * tb + half])
        for half in range(2):
            w = stream(CH_U0 + half).rearrange("p (k c) -> p k c", k=8)
            for m4 in range(4):
                m = half * 4 + m4
                ps = bank()
                for k in range(8):
                    P.mm(ps[:, 0:N], w[:, k, m4 * 128:(m4 + 1) * 128], h[:, k, :], start=(k == 0), stop=(k == 7))
                P.act(u[:, m, 0:N], ps[:, 0:N], AF.Gelu_apprx_tanh)
        ssv = sm[0:TB, 24:24 + 2 * NB].rearrange("p (a b) -> p a b", b=2)
        P.tt(sm[0:TB, 32:32 + NB], ssv[:, :, 0], ssv[:, :, 1], ALU.add)
        P.act(sm[0:TB, 40:40 + NB], sm[0:TB, 32:32 + NB], AF.Ln, bias=eps_t[0:TB], scale=1.0 / D)
        P.act(sm[0:TB, 48:48 + NB], sm[0:TB, 40:40 + NB], AF.Exp, scale=-0.5)
        if sample:
            vb = L["vb"]
            P.ts(vb[0:TB, 0, :], va[0:TB, 0, :], sm[0:TB, 48:49], ALU.mult)
            P.stt(L["vout"][0:TB, :], va[0:TB, 0, :], sm[0:TB, 48:49], L["gvb"][0:TB, :], ALU.mult, ALU.mult)
            P.dma(d_vgs, L["vout"][0:TB, :], is_out=True)
        else:
            vb = va
            for tb in range(NB):
                P.ts(vb[0:TB, tb, :], va[0:TB, tb, :], sm[0:TB, 48 + tb:49 + tb], ALU.mult)
        P.label = "mix.tr"
        for tb in range(NB):
            ptp = bank().bitcast(BF16)
            for c in range(2):
                P.tr(ptp[:, c * TB:(c + 1) * TB], ckv_b[0:TB, tb, c * 128:(c + 1) * 128], ident[0:TB, 0:TB])
            P.tr(ptp[0:32, 2 * TB:3 * TB], kr_b[0:TB, tb, :], ident[0:TB, 0:TB])
            P.copy(ckvT[:, :, tb * TB:(tb + 1) * TB], ptp[:, 0:2 * TB].rearrange("p (a b) -> p a b", a=2))
            P.copy(krT[0:32, tb * TB:(tb + 1) * TB], ptp[0:32, 2 * TB:3 * TB])
        if first:
            checkpoint("m_q")
        P.label = "mix.qheads"
        P.dma(L["rC"][0:32, 0:N], d_rqc[:, pos0:pos0 + N])
        P.dma(L["rS"][0:32, 0:N], d_rqs[:, pos0:pos0 + N])
        kcol0 = PAST if sample else t0
        blk0 = (PAST // 128) if sample else (t0 // 128)
        st_ = {}

        ab = {"n": 0}

        def abank():
            ab["n"] += 1
            return psb[(0, 1, 2, 5)[ab["n"] % 4]]

        def qproj(hd):
            pq, psw = psb[3], psb[4]
            for mq in range(3):
                P.mm(pq[0:96, 0:N], wq[:, mq, hd, :], qn[:, mq, 0:N], start=(mq == 0), stop=(mq == 2))
            for mq in range(3):
                P.mm(psw[0:32, 0:N], wqs[:, mq, hd, :], qn[:, mq, 0:N], start=(mq == 0), stop=(mq == 2))
            xs = L["xs96"][0]
            P.copy(L["T2"][0:96, 0:N], pq[0:96, 0:N])
            P.tt(xs[0:96, 0:N], pq[0:96, 0:N], L["T2"][0:96, 0:N], ALU.mult)
            st_[("q", hd)] = (pq, psw, xs)

        def qfin(hd):
            pq, psw, xs = st_.pop(("q", hd))
            pn = abank()
            P.mm(pn[0:96, 0:N], ones_bf[0:96, 0:96], xs[0:96, 0:N])
            P.act(L["T0"][0:96, 0:N], pn[0:96, 0:N], AF.Ln, bias=eps_t[0:96], scale=1.0 / 96)
            P.act(L["T1"][0:96, 0:N], L["T0"][0:96, 0:N], AF.Exp, scale=-0.5)
            P.tt(L["T2"][0:32, 0:N], pq[0:32, 0:N], L["rC"][0:32, 0:N], ALU.mult)
            P.tt(L["T3"][0:32, 0:N], psw[0:32, 0:N], L["rS"][0:32, 0:N], ALU.mult)
            P.tt(L["T2"][0:32, 0:N], L["T2"][0:32, 0:N], L["T3"][0:32, 0:N], ALU.add)
            P.stt(Qh[0:96, hd, 0:N], pq[0:96, 0:N], vecs[0:96, VC_GQ:VC_GQ + 1], L["T1"][0:96, 0:N], ALU.mult, ALU.mult)
            P.stt(Qh[0:32, hd, 0:N], L["T2"][0:32, 0:N], vecs[0:32, VC_GQ:VC_GQ + 1], L["T1"][0:32, 0:N],
                  ALU.mult, ALU.mult)

        def kproj(hd):
            pk = psb[3]
            P.mm(pk[0:96, 0:N], wk[:, 0, hd, :], ckvT[:, 0, 0:N], start=True, stop=False)
            P.mm(pk[0:96, 0:N], wk[:, 1, hd, :], ckvT[:, 1, 0:N], start=False, stop=False)
            P.mm(pk[0:96, 0:N], emb[0:32, :], krT[0:32, 0:N], start=False, stop=True)
            xs = L["xs96"][1]
            P.copy(L["T3"][0:96, 0:N], pk[0:96, 0:N])
            P.tt(xs[0:96, 0:N], pk[0:96, 0:N], L["T3"][0:96, 0:N], ALU.mult)
            st_[("k", hd)] = (pk, xs)

        def kfin(hd):
            pk, xs = st_.pop(("k", hd))
            pn = abank()
            P.mm(pn[0:96, 0:N], ones_bf[0:96, 0:96], xs[0:96, 0:N])
            P.act(L["T0"][0:96, 0:N], pn[0:96, 0:N], AF.Ln, bias=eps_t[0:96], scale=1.0 / 96)
            P.act(L["T1"][0:96, 0:N], L["T0"][0:96, 0:N], AF.Exp, scale=-0.5)
            P.stt(KT[0:96, hd, kcol0:kcol0 + N], pk[0:96, 0:N], vecs[0:96, VC_GK:VC_GK + 1], L["T1"][0:96, 0:N],
                  ALU.mult, ALU.mult)

        def head_steps(hd):
            return [lambda: qproj(hd), lambda: qfin(hd), lambda: kproj(hd), lambda: kfin(hd)]
        v_rows(ckvT, N, blk0, V)
        for stp in head_steps(0) + head_steps(1):
            stp()
        if first:
            checkpoint("m_kv")
            checkpoint("m_bkv")
        P.label = "mix.attn"
        if sample:
            jl = [(j, 128, 0) for j in range(PAST // 128)] + [(PAST // 128, TS, 0)]
        else:
            jl = []
            for j in range((t0 + N) // 128):
                a = j - t0 // 128
                jl.append((j, 128, 128 * a if a > 0 else 0))
        nj = len(jl)
        LA = 2
        cnt = {"n": 0, "d": 0}

        def att_head(hd, side, fin_prev):
            po = acc_bank()
            pend = []
            ngroups = nj if not sample else (PAST // 128 + 7) // 8 + 1
            every = max(1, ngroups // max(1, len(side))) if side else 1

            def pv_step(item):
                j, kn, qlo, pt, idx = item
                P.mm(po[0:65, qlo:N], V[0:kn, j, hd, 0:65], pt[0:kn, 0:N - qlo], start=(idx == 0), stop=(idx == nj - 1))

            gi = 0
            idx = 0
            while idx < nj:
                j, kn, qlo = jl[idx]
                if sample and kn == 128:
                    grp = [jl[idx + t] for t in range(min(8, nj - idx)) if jl[idx + t][1] == 128]
                else:
                    grp = [jl[idx]]
                ng = len(grp)
                nq = N - qlo
                pss = abank()
                for t, (jj, kk, ql) in enumerate(grp):
                    P.mm(pss[0:kk, t * nq:(t + 1) * nq], KT[0:96, hd, jj * 128:jj * 128 + kk], Qh[0:96, hd, ql:N])
                if (not sample) and j * 128 >= t0:
                    pt = L["ptd"][cnt["d"] % 3]
                    cnt["d"] += 1
                    P.act(pt[0:128, 0:nq], pss[0:128, 0:nq], AF.Exp, scale=QSCALE)
                    P.memset(pt[64:128, 0:64], 0.0, eng="dve")
                else:
                    pt = L["pt"][cnt["n"] % 3]
                    cnt["n"] += 1
                    P.act(pt[0:kn, 0:ng * nq], pss[0:kn, 0:ng * nq], AF.Exp, scale=QSCALE)
                for t, (jj, kk, ql) in enumerate(grp):
                    pend.append((jj, kk, ql, pt[:, t * nq:(t + 1) * nq], idx + t))
                idx += ng
                gi += 1
                while len(pend) > LA * ng:
                    pv_step(pend.pop(0))
                if fin_prev is not None and gi == min(2, ngroups):
                    fin_prev()
                    fin_prev = None
                if side and (gi % every == 0):
                    P.label = "mix.heads_side"
                    side.pop(0)()
                    P.label = "mix.attn"
            while pend:
                pv_step(pend.pop(0))
            if fin_prev is not None:
                fin_prev()
            while side:
                side.pop(0)()
            ri = L["T5"]
            P.act(L["T4"][64:65, 0:N], po[64:65, 0:N], AF.Ln)
            P.act(ri[0:1, 0:N], L["T4"][64:65, 0:N], AF.Exp, scale=-1.0)
            return po, ri

        def att_fin(hd, po, ri):
            pb = abank()
            P.mm(pb[0:64, 0:N], ones_f[0:1, 0:64], ri[0:1, 0:N])
            P.copy(L["T4"][0:64, 0:N], pb[0:64, 0:N])
            P.tt(ob[0:64, hd, 0:N], po[0:64, 0:N], L["T4"][0:64, 0:N], ALU.mult)
        prev = None
        for hd in range(8):
            fp = (lambda h=hd - 1, pr=prev: att_fin(h, *pr)) if prev is not None else None
            prev = att_head(hd, head_steps(hd + 2) if hd + 2 < 8 else [], fp)
        att_fin(7, *prev)
        if first:
            checkpoint("m_att")
        P.label = "mix.spatial"
        for g in range(8):
            ps = bank()
            for cb in range(NB):
                P.mm(ps[:, cb * TB:(cb + 1) * TB], vb[0:TB, cb, g * 128:(g + 1) * 128], wst[0:TB, g, 0:TB])
            tm = L["T5"] if g % 2 == 0 else L["T4"]
            for cb in range(NB):
                P.stt(tm[:, cb * TB:(cb + 1) * TB], ps[:, cb * TB:(cb + 1) * TB], vecs[:, VC_GV + g:VC_GV + g + 1],
                      bb[:, g, 0:TB], ALU.mult, ALU.add)
            P.tt(oa[:, g, 0:N], tm[:, 0:N], u[:, g, 0:N], ALU.mult)
        if first:
            checkpoint("m_gmlp")
        P.label = "mix.merge"
        mg = L["u"]
        for m in range(8):
            w = stream(CH_MG0 + m).rearrange("p (b c) -> p b c", b=32)
            pga, pgb, pa, pbb = bank(), bank(), bank(), bank()
            for k in range(8):
                P.mm(pga[:, 0:N], w[:, k, :], h[:, k, :], start=(k == 0), stop=(k == 7))
            for k in range(8):
                P.mm(pgb[:, 0:N], w[:, 8 + k, :], h[:, k, :], start=(k == 0), stop=(k == 7))
            for k in range(8):
                P.mm(pa[:, 0:N], w[:, 16 + k, :], oa[:, k, 0:N], start=(k == 0), stop=(k == 7))
            for hd in range(8):
                P.mm(pbb[:, 0:N], w[0:64, 24 + hd, :], ob[0:64, hd, 0:N], start=(hd == 0), stop=(hd == 7))
            P.act(L["T2"][:, 0:N], pga[:, 0:N], AF.Sigmoid, bias=vecs[:, VC_BG + m:VC_BG + m + 1])
            P.act(L["T3"][:, 0:N], pgb[:, 0:N], AF.Sigmoid, bias=vecs[:, VC_BG + 8 + m:VC_BG + 9 + m])
            P.tt(L["T2"][:, 0:N], pa[:, 0:N], L["T2"][:, 0:N], ALU.mult)
            P.tt(L["T3"][:, 0:N], pbb[:, 0:N], L["T3"][:, 0:N], ALU.mult)
            P.tt(mg[:, m, 0:N], L["T2"][:, 0:N], L["T3"][:, 0:N], ALU.add)
        P.label = "mix.out"
        for half in range(2):
            w = stream(CH_O0 + half).rearrange("p (k c) -> p k c", k=8)
            for m4 in range(4):
                m = half * 4 + m4
                ps = bank()
                for k in range(8):
                    P.mm(ps[:, 0:N], w[:, k, m4 * 128:(m4 + 1) * 128], mg[:, k, 0:N], start=(k == 0), stop=(k == 7))
                P.stt(xsub[:, m, :], ps[:, 0:N], gt[:, 1, m, s:s + 1], xsub[:, m, :], ALU.mult, ALU.add)

    def main_body():
        dump('modt', modt.rearrange('p a b -> p (a b)'))
        dump('wsm', wsm)
        checkpoint('init')
        if nseq > 0:
            P.memset(Vp[:, :, :, 64:65], 1.0, eng="pool")
        tiles = [(sq, st) for sq in range(nseq) for st in range(2)]
        for ti, (sq, st) in enumerate(tiles):
            ffn(xt, 512, 2, CH_F1U, CH_F1D, 0, sq, Lp, use_pool=(ti > 0))
            if "x1" in dbg and sq == 0:
                P.dma(dbg["x1"][st], xt_flat, is_out=True)
            checkpoint("ffn1")
            for sub in range(2):
                t0 = st * 1024 + sub * 512
                mixer(xt[:, sub], 512, sq, False, t0, Lp, KTp, Vp,
                      d_ckvp[sq, t0:t0 + 512, :], d_krp[sq, t0:t0 + 512, :], t0)
            if "x2" in dbg and sq == 0:
                P.dma(dbg["x2"][st], xt_flat, is_out=True)
            checkpoint("mix")
            nxt = tiles[ti + 1] if ti + 1 < len(tiles) else None

            def after_m(m, sq=sq, st=st, nxt=nxt):
                P.dma(xdram(d_yp, sq, st)[:, :, m, :], xt[:, :, m, :], is_out=True)
                if nxt is not None:
                    P.dma(xt[:, :, m, :], xdram(d_xp, nxt[0], nxt[1])[:, :, m, :])
            ffn(xt, 512, 2, CH_F2U, CH_F2D, 2, sq, Lp, after_m=after_m)

        if CFG["sample"]:
            SA = Alloc(RES_END, ASZ)
            NK = PAST + TS
            KTs = SA.v(BF16, [8, NK])
            Vs = SA.v(BF16, [NK // 128 + 1, 8, 65])
            xs = SA.v(F32, [1, 8, TS])
            Ls = phase_layout(SA, TS, True)
            c32 = SA.v(F32, [2, 512])
            c16 = SA.v(BF16, [2, 512])
            k32 = SA.v(F32, [512])
            k16 = SA.v(BF16, [512])
            P.memset(Vs[:, :, :, 64:65], 1.0, eng="pool")
            P.dma(Ls["gvb"][0:64, :], d_gvb)
            for pc in range(PAST // 512):
                P.dma(c32, d_cckv[:, :, pc * 512:(pc + 1) * 512])
                P.dma(k32[0:32, :], d_ckr[:, pc * 512:(pc + 1) * 512])
                P.copy(c16.rearrange("p a b -> p (a b)"), c32.rearrange("p a b -> p (a b)"))
                P.copy(k16[0:32, :], k32[0:32, :])
                build_kv(c16, k16, 512, pc * 512, pc * 4, Ls, KTs, Vs)
            P.dma(xs.rearrange("p a b c -> p (a b c)"), d_xs)
            ffn(xs, TS, 1, CH_F1U, CH_F1D, 0, 4, Ls)
            mixer(xs[:, 0], TS, 4, True, 0, Ls, KTs, Vs, d_ckvs, d_krs, SEQ)
            ffn(xs, TS, 1, CH_F2U, CH_F2D, 2, 4, Ls)
            P.dma(d_ys, xs.rearrange("p a b c -> p (a b c)"), is_out=True)

    try:
        main_body()
    except _Stop:
        pass
    P.finish()
    P.emit()
    es.close()
    build_program.last_prog = P
    return nc


def _rope_tables():
    half = 16
    freqs = (np.float32(10000.0) ** (-np.arange(half, dtype=np.float32) / np.float32(half))).astype(np.float32)
    pos = np.concatenate([np.arange(SEQ, dtype=np.float32), np.arange(TS, dtype=np.float32) + np.float32(PAST)])
    ang = (pos[:, None] * freqs[None, :]).astype(np.float32)
    cos, sin = np.cos(ang).astype(np.float32), np.sin(ang).astype(np.float32)
    ck = np.concatenate([cos, cos], axis=1)
    sk = np.concatenate([-sin, sin], axis=1)
    return np.ascontiguousarray(ck.T), np.ascontiguousarray(sk.T), np.ascontiguousarray(ck), np.ascontiguousarray(sk)


def _kchunks(w, ncols):
    kk = w.shape[0] // 128
    return np.ascontiguousarray(w.reshape(kk, 128, ncols).transpose(1, 0, 2).reshape(128, kk * ncols))


def _pad(a):
    out = np.zeros((128, CW), np.float32)
    out[:a.shape[0], :a.shape[1]] = a
    return out


def _weight_chunks(w_ffn1_up, w_ffn1_down, w_in, w_branch_a, w_branch_b, w_out, w_ffn2_up, w_ffn2_down):
    ch = np.zeros((NCH, 128, CW), np.float32)

    def ffn_chunks(up, down, bu, bd):
        for gi in range(11):
            blk = np.zeros((8, 128, 2, 256), np.float32)
            upk = up.reshape(8, 128, 2 * DFF)
            for jj in range(2):
                j = 2 * gi + jj
                blk[:, :, jj, 0:128] = upk[:, :, j * 128:(j + 1) * 128]
                blk[:, :, jj, 128:256] = upk[:, :, DFF + j * 128:DFF + (j + 1) * 128]
            ch[bu + gi] = blk.transpose(1, 0, 2, 3).reshape(128, CW)
        dk = down.reshape(NJ, 128, D)
        for m in range(8):
            ch[bd + m] = _pad(dk[:, :, m * 128:(m + 1) * 128].transpose(1, 0, 2).reshape(128, NJ * 128))

    ffn_chunks(w_ffn1_up, w_ffn1_down, CH_F1U, CH_F1D)
    ffn_chunks(w_ffn2_up, w_ffn2_down, CH_F2U, CH_F2D)
    ch[CH_Q] = _pad(_kchunks(w_in[:, 2048:2432], 384))
    kr = w_in[:, 2688:2720]
    kr_sw = np.concatenate([kr[:, 16:32], kr[:, 0:16]], axis=1)
    ch[CH_KV] = _pad(_kchunks(np.concatenate([w_in[:, 2432:2720], kr_sw], axis=1), 320))
    for half in range(2):
        ch[CH_U0 + half] = _kchunks(w_in[:, half * 512:(half + 1) * 512], 512)
        ch[CH_V0 + half] = _kchunks(w_in[:, 1024 + half * 512:1024 + (half + 1) * 512], 512)
        ch[CH_O0 + half] = _kchunks(w_out[:, half * 512:(half + 1) * 512], 512)
    for m in range(8):
        blk = np.zeros((128, 32, 128), np.float32)
        blk[:, 0:8, :] = w_in[:, 2720 + m * 128:2720 + (m + 1) * 128].reshape(8, 128, 128).transpose(1, 0, 2)
        blk[:, 8:16, :] = w_in[:, 3744 + m * 128:3744 + (m + 1) * 128].reshape(8, 128, 128).transpose(1, 0, 2)
        blk[:, 16:24, :] = w_branch_a[:, m * 128:(m + 1) * 128].reshape(8, 128, 128).transpose(1, 0, 2)
        blk[0:64, 24:32, :] = w_branch_b[:, m * 128:(m + 1) * 128].reshape(8, 64, 128).transpose(1, 0, 2)
        ch[CH_MG0 + m] = blk.reshape(128, CW)
    return ch


def _head_perm():
    return np.concatenate([np.arange(64, 96), np.arange(0, 64)])


def _small_weights(w_uq, w_uk, w_uv, gmlp_ws):
    ws = np.zeros((128, WS_N), np.float32)
    perm = _head_perm()
    uq = w_uq.reshape(3, 128, 8, 96)
    ws[:, WS_Q:WS_QS] = uq[:, :, :, perm].transpose(1, 0, 2, 3).reshape(128, -1)
    swap = np.concatenate([np.arange(80, 96), np.arange(64, 80)])
    ws[:, WS_QS:WS_K] = uq[:, :, :, swap].transpose(1, 0, 2, 3).reshape(128, -1)
    uk = np.zeros((2, 128, 8, 96), np.float32)
    uk[:, :, :, 32:96] = w_uk.reshape(2, 128, 8, 64)
    ws[:, WS_K:WS_V] = uk.transpose(1, 0, 2, 3).reshape(128, -1)
    ws[:, WS_V:WS_ST] = w_uv.reshape(2, 128, 512).transpose(1, 0, 2).reshape(128, -1)
    ws[:, WS_ST:WS_EMB] = gmlp_ws.transpose(2, 0, 1).reshape(128, -1)
    ws[0:32, WS_EMB:WS_EMB + 32] = np.eye(32, dtype=np.float32)
    ws[:, WS_ID:WS_N] = np.eye(128, dtype=np.float32)
    return ws


_NC_CACHE = {}


def kernel(x_prompt, x_sample, c_prompt, c_sample, cache_ckv, cache_krope,
           w_mod, b_mod, g_ffn1, w_ffn1_up, w_ffn1_down, g_mix, w_in, g_gmlp_v, gmlp_ws, gmlp_b,
           g_q_lat, w_uq, g_kv_lat, w_uk, w_uv, g_qnorm, g_knorm, b_gate, w_branch_a, w_branch_b,
           w_out, g_ffn2, w_ffn2_up, w_ffn2_down):
    ncore = 8
    key = (CFG["nseq"], CFG["sample"], CFG.get("stop"), tuple(sorted(DEBUG.keys())))
    if key not in _NC_CACHE:
        _NC_CACHE[key] = build_program()
    nc = _NC_CACHE[key]
    in_maps = _prep(x_prompt, x_sample, c_prompt, c_sample, cache_ckv, cache_krope,
                    w_mod, b_mod, g_ffn1, w_ffn1_up, w_ffn1_down, g_mix, w_in, g_gmlp_v, gmlp_ws, gmlp_b,
                    g_q_lat, w_uq, g_kv_lat, w_uk, w_uv, g_qnorm, g_knorm, b_gate, w_branch_a, w_branch_b,
                    w_out, g_ffn2, w_ffn2_up, w_ffn2_down)
    res = run_bass_kernel_spmd(nc, in_maps, core_ids=list(range(ncore)))
    R = res.results
    kernel.last_results = R
    return _gather(R)


def _prep(x_prompt, x_sample, c_prompt, c_sample, cache_ckv, cache_krope,
          w_mod, b_mod, g_ffn1, w_ffn1_up, w_ffn1_down, g_mix, w_in, g_gmlp_v, gmlp_ws, gmlp_b,
          g_q_lat, w_uq, g_kv_lat, w_uk, w_uv, g_qnorm, g_knorm, b_gate, w_branch_a, w_branch_b,
          w_out, g_ffn2, w_ffn2_up, w_ffn2_down):
    f = lambda a: np.asarray(a, dtype=np.float32)
    x_prompt, x_sample, c_prompt, c_sample = f(x_prompt), f(x_sample), f(c_prompt), f(c_sample)
    cache_ckv, cache_krope = f(cache_ckv)[0], f(cache_krope)[0]
    ncore = 8

    wch = _weight_chunks(f(w_ffn1_up)[0], f(w_ffn1_down)[0], f(w_in)[0], f(w_branch_a)[0], f(w_branch_b)[0],
                         f(w_out)[0], f(w_ffn2_up)[0], f(w_ffn2_down)[0])
    wsmall = _small_weights(f(w_uq)[0], f(w_uk)[0], f(w_uv)[0], f(gmlp_ws)[0])
    maskT = np.triu(np.ones((128, 128), np.float32))
    perm = _head_perm()
    vecs = np.zeros((128, VC_N), np.float32)
    fm = lambda v: np.ascontiguousarray(v.reshape(-1, 128).T)
    vecs[:, VC_G1:VC_G1 + 8] = fm(f(g_ffn1)[0])
    vecs[:, VC_G2:VC_G2 + 8] = fm(f(g_mix)[0])
    vecs[:, VC_G3:VC_G3 + 8] = fm(f(g_ffn2)[0])
    vecs[:, VC_BMOD:VC_BMOD + 72] = fm(f(b_mod)[0])
    vecs[:, VC_GQL:VC_GQL + 3] = fm(f(g_q_lat)[0])
    vecs[:, VC_BG:VC_BG + 16] = fm(f(b_gate)[0])
    vecs[0:96, VC_GQ] = f(g_qnorm)[0][perm]
    vecs[0:96, VC_GK] = f(g_knorm)[0][perm]
    vecs[:, VC_GV:VC_GV + 8] = fm(f(g_gmlp_v)[0])
    vecs[:, VC_EPS] = EPS
    vecs[0:5, VC_ID5:VC_ID5 + 5] = np.eye(5, dtype=np.float32)
    gkvb = np.ascontiguousarray(np.broadcast_to(f(g_kv_lat)[0][None, :], (128, 256)))
    bbr = np.ascontiguousarray(np.broadcast_to(f(gmlp_b)[0].reshape(1, 1024), (128, 1024)))
    gvb = np.ascontiguousarray(np.broadcast_to(f(g_gmlp_v)[0][None, :], (64, 1024)))
    rqc, rqs, rkc, rks = _rope_tables()
    wmod = np.ascontiguousarray(f(w_mod)[0])

    in_maps = []
    for c in range(ncore):
        xp = x_prompt[4 * c:4 * c + 4]
        xpl = xp.reshape(4, 2, 2, 512, 8, 128).transpose(0, 1, 5, 2, 4, 3).reshape(4, 2, 128, 8192)
        xsl = x_sample[c].reshape(TS, 8, 128).transpose(2, 1, 0).reshape(128, 8 * TS)
        call = np.concatenate([c_prompt[4 * c:4 * c + 4], c_sample[c:c + 1]], axis=0)
        ct = call.reshape(5, 8, 128).transpose(2, 1, 0).reshape(128, 40)
        cckv = cache_ckv[c].reshape(PAST, 2, 128).transpose(2, 1, 0)
        ckr = cache_krope[c].T
        in_maps.append({
            "xp": np.ascontiguousarray(xpl), "xs": np.ascontiguousarray(xsl), "ct": np.ascontiguousarray(ct),
            "wmod": wmod, "wch": wch, "wsmall": wsmall, "maskT": maskT, "vecs": vecs, "gkvb": gkvb, "bb": bbr,
            "gvb": gvb, "ropeqc": rqc, "ropeqs": rqs, "ropekc": rkc, "ropeks": rks,
            "cckv": np.ascontiguousarray(cckv), "ckr": np.ascontiguousarray(ckr),
        })
    return in_maps


def _gather(R):
    ncore = 8
    y_p = np.zeros((32, SEQ, D), np.float32)
    y_s = np.zeros((8, TS, D), np.float32)
    ckv_p = np.zeros((1, 32, SEQ, 256), np.float32)
    kr_p = np.zeros((1, 32, SEQ, 32), np.float32)
    ckv_s = np.zeros((1, 8, TS, 256), np.float32)
    kr_s = np.zeros((1, 8, TS, 32), np.float32)
    vg_s = np.zeros((1, 8, TS, D), np.float32)
    for c in range(ncore):
        yp = np.asarray(R[c]["yp"]).reshape(4, 2, 128, 2, 8, 512).transpose(0, 1, 3, 5, 4, 2).reshape(4, SEQ, D)
        y_p[4 * c:4 * c + 4] = yp
        y_s[c] = np.asarray(R[c]["ys"]).reshape(128, 8, TS).transpose(2, 1, 0).reshape(TS, D)
        ckv_p[0, 4 * c:4 * c + 4] = np.asarray(R[c]["ckvp"])
        kr_p[0, 4 * c:4 * c + 4] = np.asarray(R[c]["krp"])
        ckv_s[0, c] = np.asarray(R[c]["ckvs"])
        kr_s[0, c] = np.asarray(R[c]["krs"])
        vg_s[0, c] = np.asarray(R[c]["vgs"])
    return (y_p, y_s, ckv_p, kr_p, ckv_s, kr_s, vg_s)
```

```python
import numpy as np
import concourse.bass as bass
import concourse.mybir as mybir
from concourse.bass_utils import run_bass_kernel_spmd

F32 = mybir.dt.float32
BF16 = mybir.dt.bfloat16
U8 = mybir.dt.uint8
AF = mybir.ActivationFunctionType
ALU = mybir.AluOpType
ESZ = {F32: 4, BF16: 2, U8: 1}

D = 1024
DFF = 2816
NJ = 22
SEQ = 2048
NSEQ = 4
TS = 64
PAST = 4096
EPS = 1e-6
QSCALE = 96.0 ** -0.5

CH_F1U, CH_F1D = 0, 11
CH_Q, CH_KV, CH_U0, CH_V0, CH_MG0, CH_O0 = 19, 20, 21, 23, 25, 33
CH_F2U, CH_F2D = 35, 46
NCH = 54
CW = 4096

WS_Q, WS_QS, WS_K, WS_V, WS_ST, WS_EMB, WS_ID, WS_N = 0, 2304, 3072, 4608, 5632, 6656, 6752, 6880
VC_G1, VC_G2, VC_G3, VC_BMOD, VC_GQL, VC_BG, VC_GQ, VC_GK, VC_GV, VC_EPS, VC_ID5, VC_N = 0, 8, 16, 24, 96, 99, 115, 116, 117, 125, 126, 131

PAGE = 256
DEBUG = {}
CFG = {"nseq": NSEQ, "sample": True}


class Ins:
    __slots__ = ("eng", "fn", "deps", "marked", "ticket", "isdma", "dsem", "dval", "label")

    def __init__(self, eng, fn, isdma=False):
        self.eng = eng
        self.fn = fn
        self.deps = []
        self.marked = False
        self.ticket = 0
        self.isdma = isdma
        self.dsem = None
        self.dval = 0


class Prog:
    ENGS = ("pe", "act", "dve", "pool", "sp")

    def __init__(self, nc):
        self.nc = nc
        self.streams = {e: [] for e in self.ENGS}
        self.w = {}
        self.r = {}
        self.dmacount = {e: 0 for e in self.ENGS}
        self.dmahist = {e: [] for e in self.ENGS}
        self.KRING = 8
        self.out_dmas = []
        self.kcache = {}
        self.label = ''

    def keys(self, ap):
        name = ap.tensor.name
        if name.startswith("ps"):
            return [("ps", int(name[2:]))]
        if name != "arena":
            return []
        es = ESZ[ap.dtype]
        pairs = [tuple(x) for x in ap.ap]
        ck = (ap.offset, tuple(pairs), es)
        got = self.kcache.get(ck)
        if got is not None:
            return got
        pstride = pairs[0][0]
        lo = ap.offset % pstride if pstride > 0 else ap.offset

        def expand(off, dims):
            dims = [d for d in dims if d[1] > 1]
            if not dims:
                return [(off, off)]
            st, cnt = dims[0]
            rest = dims[1:]
            ext = sum((c - 1) * s_ for s_, c in rest)
            if st <= ext + 1 or cnt > 32:
                return [(off, off + (cnt - 1) * st + ext)]
            out = []
            for i in range(cnt):
                out += expand(off + i * st, rest)
            return out

        pages = set()
        for (a, b) in expand(lo, pairs[1:]):
            for p in range(a * es // PAGE, ((b + 1) * es - 1) // PAGE + 1):
                pages.add(p)
        got = [("sb", p) for p in sorted(pages)]
        self.kcache[ck] = got
        return got

    def _dep(self, ins, prod, kind):
        if prod is None or prod is ins:
            return
        if (not prod.isdma) and (not ins.isdma) and prod.eng == ins.eng:
            if ins.eng == "pe":
                return
        if prod not in ins.deps:
            ins.deps.append(prod)
            prod.marked = True

    def add(self, eng, fn, reads=(), writes=(), isdma=False, rkeys=(), wkeys=()):
        ins = Ins(eng, fn, isdma)
        ins.label = self.label
        rk = list(rkeys)
        for ap in reads:
            if ap is not None and not isinstance(ap, (int, float)):
                rk += self.keys(ap)
        wk = list(wkeys)
        for ap in writes:
            if ap is not None:
                wk += self.keys(ap)
        for k in rk:
            self._dep(ins, self.w.get(k), "raw")
            if k[0] == "ps":
                for rd in self.r.get(k, ()):
                    if rd.eng != ins.eng:
                        self._dep(ins, rd, "rar")
        for k in wk:
            self._dep(ins, self.w.get(k), "waw")
            for rd in self.r.get(k, ()):
                self._dep(ins, rd, "war")
        for k in rk:
            self.r.setdefault(k, []).append(ins)
        for k in wk:
            self.w[k] = ins
            self.r[k] = []
        if isdma:
            i = self.dmacount[eng]
            self.dmacount[eng] += 1
            hist = self.dmahist[eng]
            if i >= self.KRING:
                prev = hist[i - self.KRING]
                if prev not in ins.deps:
                    ins.deps.append(prev)
            hist.append(ins)
            ins.dval = 16 * (i // self.KRING + 1)
            ins.dsem = (eng, i % self.KRING)
        self.streams[eng].append(ins)
        return ins

    def emit(self):
        nc = self.nc
        import contextlib
        with contextlib.ExitStack() as es:
            esem = {e: es.enter_context(nc.semaphore("done_" + e)) for e in ("pe", "act", "dve", "pool")}
            dsem = {}
            for e in self.ENGS:
                if self.dmacount[e] > 0:
                    for i in range(self.KRING):
                        dsem[(e, i)] = es.enter_context(nc.semaphore("dma_%s_%d" % (e, i)))
            for e in ("pe", "act", "dve", "pool"):
                t = 0
                for ins in self.streams[e]:
                    if ins.isdma:
                        continue
                    if ins.marked:
                        t += 1
                        ins.ticket = t
            block = es.enter_context(nc.Block())

            def run(ename, eng):
                seen = {}
                for ins in self.streams[ename]:
                    for p in ins.deps:
                        if p.isdma:
                            key, val, sem = p.dsem, p.dval, dsem[p.dsem]
                        else:
                            key, val, sem = p.eng, p.ticket, esem[p.eng]
                        if seen.get(key, 0) >= val:
                            continue
                        seen[key] = val
                        eng.wait_ge(sem, val)
                    r = ins.fn(eng)
                    if ins.isdma:
                        r.then_inc(dsem[ins.dsem], 16)
                    elif ins.marked:
                        r.then_inc(esem[ins.eng], 1)

            @block.tensor
            def _(e):
                run("pe", e)

            @block.scalar
            def _(e):
                run("act", e)

            @block.vector
            def _(e):
                run("dve", e)

            @block.gpsimd
            def _(e):
                run("pool", e)

            @block.sync
            def _(e):
                run("sp", e)

    def mm(self, out, lhsT, rhs, start=True, stop=True):
        return self.add("pe", lambda e: e.matmul(out, lhsT=lhsT, rhs=rhs, start=start, stop=stop),
                        reads=[lhsT, rhs], writes=[out])

    def tr(self, out, in_, ident):
        return self.add("pe", lambda e: e.transpose(out, in_, ident), reads=[in_, ident], writes=[out])

    def act(self, out, in_, func, bias=None, scale=1.0, accum_out=None):
        kw = {}
        if bias is not None:
            kw["bias"] = bias
        if accum_out is not None:
            kw["accum_out"] = accum_out
        return self.add("act", lambda e: e.activation(out=out, in_=in_, func=func, scale=scale, **kw),
                        reads=[in_, bias, scale], writes=[out, accum_out])

    def tt(self, out, in0, in1, op, eng="dve"):
        return self.add(eng, lambda e: e.tensor_tensor(out=out, in0=in0, in1=in1, op=op),
                        reads=[in0, in1], writes=[out])

    def ts(self, out, in0, s1, op0, s2=None, op1=None, eng="dve"):
        if op1 is None:
            fn = lambda e: e.tensor_scalar(out=out, in0=in0, scalar1=s1, scalar2=None, op0=op0)
        else:
            fn = lambda e: e.tensor_scalar(out=out, in0=in0, scalar1=s1, scalar2=s2, op0=op0, op1=op1)
        return self.add(eng, fn, reads=[in0, s1, s2], writes=[out])

    def stt(self, out, in0, scalar, in1, op0, op1):
        return self.add("dve", lambda e: e.scalar_tensor_tensor(out=out, in0=in0, scalar=scalar, in1=in1, op0=op0, op1=op1),
                        reads=[in0, scalar, in1], writes=[out])

    def copy(self, out, in_, eng="dve"):
        return self.add(eng, lambda e: e.tensor_copy(out=out, in_=in_), reads=[in_], writes=[out])

    def memset(self, out, val, eng="pool"):
        return self.add(eng, lambda e: e.memset(out, val), writes=[out])

    def dma(self, out, in_, q="pool", rkeys=(), wkeys=(), is_out=False):
        ins = self.add(q, lambda e: e.dma_start(out=out, in_=in_), reads=[in_], writes=[out], isdma=True,
                       rkeys=rkeys, wkeys=wkeys)
        if is_out:
            self.out_dmas.append(ins)
        return ins

    def finish(self):
        ins = Ins("sp", lambda e: e.nop())
        for d in self.out_dmas:
            ins.deps.append(d)
        for e in ("pe", "act", "dve", "pool"):
            if self.streams[e]:
                last = [x for x in self.streams[e] if not x.isdma]
                if last:
                    last[-1].marked = True
                    ins.deps.append(last[-1])
        self.streams["sp"].append(ins)


def build_program():
    nc = bass.Bass("TRN2", target_bir_lowering=False)
    nseq = CFG["nseq"]

    def din(name, shape, dt=F32):
        return nc.dram_tensor(name, list(shape), dt, kind="ExternalInput").ap()

    def dout(name, shape, dt=F32):
        return nc.dram_tensor(name, list(shape), dt, kind="ExternalOutput").ap()

    d_xp = din("xp", [NSEQ, 2, 128, 8192])
    d_xs = din("xs", [128, 8 * TS])
    d_ct = din("ct", [128, 40])
    d_wmod = din("wmod", [1024, 9216])
    d_wch = din("wch", [NCH, 128, CW])
    d_wsm = din("wsmall", [128, WS_N])
    d_mask = din("maskT", [128, 128])
    d_vecs = din("vecs", [128, VC_N])
    d_gkvb = din("gkvb", [128, 256])
    d_bb = din("bb", [128, 1024])
    d_gvb = din("gvb", [64, 1024])
    d_rqc = din("ropeqc", [32, SEQ + TS])
    d_rqs = din("ropeqs", [32, SEQ + TS])
    d_rkc = din("ropekc", [SEQ + TS, 32])
    d_rks = din("ropeks", [SEQ + TS, 32])
    d_cckv = din("cckv", [128, 2, PAST])
    d_ckr = din("ckr", [32, PAST])

    d_yp = dout("yp", [NSEQ, 2, 128, 8192])
    d_ys = dout("ys", [128, 8 * TS])
    d_ckvp = dout("ckvp", [NSEQ, SEQ, 256])
    d_krp = dout("krp", [NSEQ, SEQ, 32])
    d_ckvs = dout("ckvs", [TS, 256])
    d_krs = dout("krs", [TS, 32])
    d_vgs = dout("vgs", [TS, 1024])
    d_scr = nc.dram_tensor("wscr", [NCH, 128, CW], BF16, kind="Internal").ap()
    dbg = {}
    for name, shape in DEBUG.items():
        dbg[name] = dout("dbg_" + name, shape)

    import contextlib
    es = contextlib.ExitStack()
    ASZ = 212736
    arena = es.enter_context(nc.sbuf_tensor("arena", [128, ASZ], U8))
    psb = [es.enter_context(nc.psum_tensor("ps%d" % i, [128, 512], F32)) for i in range(8)]
    P = Prog(nc)

    def view(off, dt, shape):
        n = 1
        for s in shape:
            n *= s
        nbytes = n * ESZ[dt]
        assert off % 4 == 0 and off + nbytes <= ASZ, (off, nbytes)
        a = arena[:, off:off + nbytes].bitcast(dt)
        if len(shape) == 1:
            return a
        if len(shape) == 2:
            return a.rearrange("p (a b) -> p a b", a=shape[0])
        if len(shape) == 3:
            return a.rearrange("p (a b c) -> p a b c", a=shape[0], b=shape[1])
        raise ValueError

    class Alloc:
        def __init__(self, base, limit):
            self.base, self.p, self.limit = base, base, limit

        def take(self, nbytes):
            nbytes = (nbytes + PAGE - 1) // PAGE * PAGE
            o = self.p
            self.p += nbytes
            assert self.p <= self.limit, ("arena overflow", self.p, self.limit)
            return o

        def v(self, dt, shape):
            n = 1
            for s in shape:
                n *= s
            return view(self.take(n * ESZ[dt]), dt, shape)

    RA = Alloc(0, ASZ)
    wsm = RA.v(BF16, [WS_N])
    wq = wsm[:, WS_Q:WS_QS].rearrange("p (a b c) -> p a b c", a=3, b=8)
    wqs = wsm[:, WS_QS:WS_K].rearrange("p (a b c) -> p a b c", a=3, b=8)
    wk = wsm[:, WS_K:WS_V].rearrange("p (a b c) -> p a b c", a=2, b=8)
    wv = wsm[:, WS_V:WS_ST].rearrange("p (a b) -> p a b", a=2)
    wst = wsm[:, WS_ST:WS_EMB].rearrange("p (a b) -> p a b", a=8)
    emb = wsm[:, WS_EMB:WS_ID]
    ident = wsm[:, WS_ID:WS_N]
    ones_bf = RA.v(BF16, [128])
    ones_f = RA.v(F32, [64])
    vecs = RA.v(F32, [VC_N])
    eps_t = vecs[:, VC_EPS:VC_EPS + 1]
    modt = RA.v(F32, [72, 5])
    gs = RA.v(F32, [3, 8, 5])
    gt = RA.v(F32, [3, 8, 5])
    gkvb = RA.v(F32, [256])
    bb = RA.v(F32, [8, 128])
    RING0 = RA.take(4 * 8192)
    RES_END = RA.p

    ring_state = {"n": 0}

    def stream(cid):
        i = ring_state["n"]
        ring_state["n"] += 1
        off = RING0 + (i % 4) * 8192
        slot = view(off, BF16, [CW])
        P.dma(slot, d_scr[cid], q="sp", rkeys=[("scr", cid)])
        return slot

    bank_state = {"n": 0}

    def bank():
        i = bank_state["n"]
        bank_state["n"] += 1
        return psb[i % 6]

    acc_state = {"n": 0}

    def acc_bank():
        i = acc_state["n"]
        acc_state["n"] += 1
        return psb[6 + i % 2]

    def sh_ap(i, k, s):
        return modt[:, (3 * i) * 8 + k, s:s + 1]

    def sh_ap2(i, k, s):
        return modt[:, (3 * i) * 8 + k, s:s + 1]

    class _Stop(Exception):
        pass

    def checkpoint(name):
        if CFG.get('stop') == name:
            raise _Stop()

    def dump(name, ap):
        if name in dbg:
            P.dma(dbg[name], ap, is_out=True)

    def phase_layout(A, N, sample):
        L = {}
        base = A.p
        nsub = 2 if not sample else 1
        F = Alloc(base, ASZ)
        L["fh"] = F.v(BF16, [nsub, 8, N])
        fg0 = F.p
        L["fg"] = F.v(BF16, [nsub, NJ, N])
        fend = F.p
        L["xsqs"] = [view(fg0 + i * 8 * N * 2, BF16, [8, N]) for i in range(nsub)]
        M = Alloc(base, ASZ)
        L["mh"] = M.v(BF16, [8, N])
        if sample:
            L["va"] = M.v(F32, [1, 1024])
            L["vb"] = M.v(BF16, [1, 1024])
            L["vout"] = M.v(F32, [1024])
            L["gvb"] = M.v(F32, [1024])
            L["zq"] = M.v(F32, [3, N])
            L["xsq"] = M.v(BF16, [8, N])
        else:
            a0 = M.take(8192)
            L["va"] = view(a0, BF16, [4, 1024])
            L["zq"] = view(a0, F32, [3, N])
            L["xsq"] = view(a0, BF16, [8, N])
        q0 = M.take(6 * N * 2)
        L["qn"] = view(q0, BF16, [3, N])
        L["xsq3"] = view(q0 + 3 * N * 2, BF16, [3, N])
        L["Qh"] = M.v(BF16, [8, N])
        L["u"] = M.v(BF16, [8, N])
        L["ob"] = M.v(BF16, [8, N])
        NBm = max(1, N // 128)
        tsz = max(256, 4 * N) + max(256, 2 * N)
        kvsz = NBm * (1024 + 512 + 128) + max(256, NBm * 64) + 2 * max(256, NBm * 128)
        atsz = 7 * 1024 + 2 * max(256, N * 4)
        ksz = max(8 * N * 2, tsz + max(kvsz, atsz))
        k0 = M.take(ksz)
        L["oa"] = view(k0, BF16, [8, N])
        o = k0
        L["ckvT"] = view(o, BF16, [2, N]); o += max(256, 4 * N)
        L["krT"] = view(o, BF16, [N]); o += max(256, 2 * N)
        o1 = o
        L["ckv_o"] = view(o, F32, [NBm, 256]); o += NBm * 1024
        L["ckv_b"] = view(o, BF16, [NBm, 256]); o += NBm * 512
        L["kr_o"] = view(o, F32, [NBm, 32]); o += NBm * 128
        L["kr_b"] = view(o, BF16, [NBm, 32]); o += max(256, NBm * 64)
        L["rCt"] = view(o, F32, [NBm, 32]); o += max(256, NBm * 128)
        L["rSt"] = view(o, F32, [NBm, 32]); o += max(256, NBm * 128)
        assert o <= k0 + ksz
        o = o1
        L["pt"] = [view(o + i * 1024, BF16, [512]) for i in range(4)]
        L["ptd"] = [view(o + (4 + i) * 1024, BF16, [512]) for i in range(3)]
        o += 7168
        L["rC"] = view(o, F32, [N]); o += max(256, N * 4)
        L["rS"] = view(o, F32, [N]); o += max(256, N * 4)
        assert o <= k0 + ksz
        if 3 * N * 2 >= 2048:
            L["xs96"] = [view(q0 + 3 * N * 2 + i * 1024, BF16, [512]) for i in range(2)]
        else:
            L["xs96"] = [M.v(BF16, [512]) for _ in range(2)]
        mend = M.p
        S = Alloc(max(fend, mend), ASZ)
        for t in range(6):
            L["T%d" % t] = S.v(F32, [512])
        L["sm"] = S.v(F32, [64])
        L["base"] = base
        A.p = S.p
        return L

    PA = Alloc(RES_END, ASZ)
    KTp = PA.v(BF16, [8, SEQ])
    Vp = PA.v(BF16, [16, 8, 65])
    xt = PA.v(F32, [2, 8, 512])
    Lp = phase_layout(PA, 512, False)
    xt_flat = xt.rearrange("p a b c -> p (a b c)")

    def xdram(d, sq, st):
        return d[sq, st].rearrange("p (a b c) -> p a b c", a=2, b=8)

    PH0 = Alloc(Lp["base"], ASZ)
    P.memset(ones_bf, 1.0)
    P.memset(ones_f, 1.0)
    P.dma(vecs, d_vecs)
    P.dma(gkvb, d_gkvb)
    P.dma(bb.rearrange("p a b -> p (a b)"), d_bb)
    ctile = PH0.v(F32, [40])
    csil = PH0.v(F32, [8, 5])
    P.dma(ctile, d_ct)
    if nseq > 0:
        P.dma(xt_flat, d_xp[0, 0])
    P.act(csil.rearrange("p a b -> p (a b)"), ctile, AF.Silu)
    wst32 = PH0.v(F32, [WS_N])
    mk = PH0.v(F32, [128])
    P.dma(wst32, d_wsm)
    P.dma(mk, d_mask)
    for g in range(8):
        sl = wst32[:, WS_ST + g * 128: WS_ST + (g + 1) * 128]
        P.tt(sl, sl, mk, ALU.mult)
    P.copy(wsm[:, 0:3440], wst32[:, 0:3440], eng="dve")
    P.copy(wsm[:, 3440:WS_N], wst32[:, 3440:WS_N], eng="dve")
    wmv = d_wmod.rearrange("(k p) c -> p k c", p=128)
    mtbs = [PH0.v(F32, [512]) for _ in range(2)]
    for blk in range(18):
        stg = view(RING0 + (blk % 2) * 16384, F32, [8, 512])
        P.dma(stg, wmv[:, :, blk * 512:(blk + 1) * 512], q="sp", wkeys=[("wm", blk)])
        ps = bank()
        for k in range(8):
            P.mm(ps[0:5, 0:512], csil[:, k, :], stg[:, k, :], start=(k == 0), stop=(k == 7))
        mtb = mtbs[blk % 2]
        P.copy(mtb[0:5, :], ps[0:5, 0:512])
        pt_ = bank()
        for m4 in range(4):
            P.tr(pt_[:, m4 * 8:m4 * 8 + 5], mtb[0:5, m4 * 128:(m4 + 1) * 128], vecs[0:5, VC_ID5:VC_ID5 + 5])
        for m4 in range(4):
            col = blk * 4 + m4
            P.ts(modt[:, col, :], pt_[:, m4 * 8:m4 * 8 + 5], vecs[:, VC_BMOD + col:VC_BMOD + col + 1], ALU.add)
    for c in range(NCH):
        P.dma(d_scr[c].rearrange("p (a b) -> p a b", b=2048), d_wch[c].rearrange("p (a b) -> p a b", b=2048),
              q="pool", rkeys=[("wm", min(17, 12 + c))], wkeys=[("scr", c)])
    for i in range(3):
        gbase = (VC_G1, VC_G2, VC_G3)[i]
        for k in range(8):
            P.ts(gs[:, i, k, :], modt[:, (3 * i + 1) * 8 + k, :], 1.0, ALU.add,
                 vecs[:, gbase + k:gbase + k + 1], ALU.mult)
        P.ts(gt[:, i].rearrange("p a b -> p (a b)"),
             modt[:, (3 * i + 2) * 8:(3 * i + 3) * 8, :].rearrange("p a b -> p (a b)"),
             0.5 if i != 1 else 1.0, ALU.mult)

    def norm_multi(items, i, s, N, use_pool=True):
        for (xsub, h_out, xsq, tln, trs, t1s) in items:
            if use_pool:
                P.act(xsq[:, 0:6, 0:N], xsub[:, 0:6, :], AF.Square)
                for k in (6, 7):
                    P.tt(xsq[:, k, 0:N], xsub[:, k, :], xsub[:, k, :], ALU.mult, eng="pool")
            else:
                P.act(xsq[:, :, 0:N], xsub, AF.Square)
        pss = []
        for (xsub, h_out, xsq, tln, trs, t1s) in items:
            ps = bank()
            for k in range(8):
                P.mm(ps[:, 0:N], ones_bf, xsq[:, k, 0:N], start=(k == 0), stop=(k == 7))
            pss.append(ps)
        for idx, (xsub, h_out, xsq, tln, trs, t1s) in enumerate(items):
            P.act(tln[:, 0:N], pss[idx][:, 0:N], AF.Ln, bias=eps_t, scale=1.0 / D)
            P.act(trs[:, 0:N], tln[:, 0:N], AF.Exp, scale=-0.5)
        for (xsub, h_out, xsq, tln, trs, t1s) in items:
            for k in range(8):
                t1 = t1s[k % len(t1s)]
                P.tt(t1[:, 0:N], xsub[:, k, :], trs[:, 0:N], ALU.mult)
                P.act(h_out[:, k, :], t1[:, 0:N], AF.Identity, bias=sh_ap2(i, k, s), scale=gs[:, i, k, s:s + 1])

    def ffn(xv, N, nsub, cbase_u, cbase_d, i, s, L, after_m=None, use_pool=True):
        h, g = L["fh"], L["fg"]
        items = []
        for sub in range(nsub):
            tb_ = [(L["T0"], L["T1"], [L["T4"]]), (L["T2"], L["T3"], [L["T5"]])][sub]
            items.append((xv[:, sub], h[:, sub], L["xsqs"][sub], tb_[0], tb_[1], tb_[2]))
        P.label = "ffn%d.norm" % i
        norm_multi(items, i, s, N, use_pool=use_pool)
        P.label = "ffn%d.up" % i
        for gi in range(11):
            w = stream(cbase_u + gi).rearrange("p (k j c) -> p k j c", k=8, j=2)
            for jj in range(2):
                j = 2 * gi + jj
                for sub in range(nsub):
                    pg, pu = bank(), bank()
                    for k in range(8):
                        P.mm(pg[:, 0:N], w[:, k, jj, 0:128], h[:, sub, k, :], start=(k == 0), stop=(k == 7))
                    for k in range(8):
                        P.mm(pu[:, 0:N], w[:, k, jj, 128:256], h[:, sub, k, :], start=(k == 0), stop=(k == 7))
                    sil = L["T2"] if (j + sub) % 2 == 0 else L["T3"]
                    P.act(sil[:, 0:N], pg[:, 0:N], AF.Silu)
                    P.tt(g[:, sub, j, :], pu[:, 0:N], sil[:, 0:N], ALU.mult)
        P.label = "ffn%d.down" % i
        for m in range(8):
            w = stream(cbase_d + m)[:, 0:NJ * 128].rearrange("p (j c) -> p j c", j=NJ)
            for sub in range(nsub):
                po = bank()
                for j in range(NJ):
                    P.mm(po[:, 0:N], w[:, j, :], g[:, sub, j, :], start=(j == 0), stop=(j == NJ - 1))
                P.stt(xv[:, sub, m, :], po[:, 0:N], gt[:, i, m, s:s + 1], xv[:, sub, m, :], ALU.mult, ALU.add)
            if after_m is not None:
                after_m(m)

    def kv_heads(ckvT, krT, n, kcol0, L, KT):
        def proj(hd):
            pk = bank()
            P.mm(pk[0:96, 0:n], wk[:, 0, hd, :], ckvT[:, 0, 0:n], start=True, stop=False)
            P.mm(pk[0:96, 0:n], wk[:, 1, hd, :], ckvT[:, 1, 0:n], start=False, stop=False)
            P.mm(pk[0:96, 0:n], emb[0:32, :], krT[0:32, 0:n], start=False, stop=True)
            xs = L["xs96"][hd % 2]
            P.act(xs[0:96, 0:n], pk[0:96, 0:n], AF.Square)
            return pk, xs

        def fin(hd, pk, xs):
            pn = bank()
            ta, tb_ = (L["T0"], L["T1"]) if hd % 2 == 0 else (L["T2"], L["T3"])
            P.mm(pn[0:96, 0:n], ones_bf[0:96, 0:96], xs[0:96, 0:n])
            P.act(ta[0:96, 0:n], pn[0:96, 0:n], AF.Ln, bias=eps_t[0:96], scale=1.0 / 96)
            P.act(tb_[0:96, 0:n], ta[0:96, 0:n], AF.Exp, scale=-0.5)
            P.stt(KT[0:96, hd, kcol0:kcol0 + n], pk[0:96, 0:n], vecs[0:96, VC_GK:VC_GK + 1], tb_[0:96, 0:n],
                  ALU.mult, ALU.mult)
        prev = None
        for hd in range(8):
            cur = proj(hd)
            if prev is not None:
                fin(hd - 1, *prev)
            prev = cur
        fin(7, *prev)

    def v_rows(ckvT, n, blk0, V):
        TBk = min(128, n)
        for tb in range(n // TBk):
            pv = bank()
            for c in range(2):
                P.mm(pv[0:TBk, 0:512], ckvT[:, c, tb * TBk:(tb + 1) * TBk], wv[:, c, :], start=(c == 0), stop=(c == 1))
            P.copy(V[0:TBk, blk0 + tb, :, 0:64], pv[0:TBk, 0:512].rearrange("p (a b) -> p a b", a=8))

    def build_kv(ckvT, krT, n, kcol0, blk0, L, KT, V):
        kv_heads(ckvT, krT, n, kcol0, L, KT)
        v_rows(ckvT, n, blk0, V)

    mix_count = {"n": 0}

    def mixer(xsub, N, s, sample, t0, L, KT, V, d_ckv_rows, d_kr_rows, pos0):
        first = (mix_count["n"] == 0)
        mix_count["n"] += 1
        TB = min(128, N)
        NB = N // TB
        h = L["mh"]
        sm = L["sm"]
        zq, xsq3, qn, Qh = L["zq"], L["xsq3"], L["qn"], L["Qh"]
        ckv_o, ckv_b, kr_o, kr_b = L["ckv_o"], L["ckv_b"], L["kr_o"], L["kr_b"]
        ckvT, krT, rCt, rSt = L["ckvT"], L["krT"], L["rCt"], L["rSt"]
        u, va, oa, ob = L["u"], L["va"], L["oa"], L["ob"]
        P.label = "mix.norm"
        norm_multi([(xsub, h, L["xsq"], L["T0"], L["T1"], [L["T4"], L["T5"]])], 1, s, N)
        P.label = "mix.zq_kv"
        P.dma(rCt[0:TB, 0:NB, :], d_rkc[pos0:pos0 + N, :].rearrange("(a p) f -> p a f", p=TB))
        P.dma(rSt[0:TB, 0:NB, :], d_rks[pos0:pos0 + N, :].rearrange("(a p) f -> p a f", p=TB))
        wqc = stream(CH_Q)[:, 0:8 * 384].rearrange("p (k c) -> p k c", k=8)
        for mq in range(3):
            ps = bank()
            for k in range(8):
                P.mm(ps[:, 0:N], wqc[:, k, mq * 128:(mq + 1) * 128], h[:, k, :], start=(k == 0), stop=(k == 7))
            P.act(zq[:, mq, 0:N], ps[:, 0:N], AF.Identity)
            P.tt(xsq3[:, mq, 0:N], zq[:, mq, 0:N], zq[:, mq, 0:N], ALU.mult, eng="pool")
        wkvc = stream(CH_KV)[:, 0:8 * 320].rearrange("p (k c) -> p k c", k=8)
        pks = []
        for tb in range(NB):
            pk = bank()
            for k in range(8):
                P.mm(pk[0:TB, 0:320], h[:, k, tb * TB:(tb + 1) * TB], wkvc[:, k, :], start=(k == 0), stop=(k == 7))
            pks.append(pk)
        ps = bank()
        for mq in range(3):
            P.mm(ps[:, 0:N], ones_bf, xsq3[:, mq, 0:N], start=(mq == 0), stop=(mq == 2))
        P.act(L["T0"][:, 0:N], ps[:, 0:N], AF.Ln, bias=eps_t, scale=1.0 / 384)
        P.act(L["T1"][:, 0:N], L["T0"][:, 0:N], AF.Exp, scale=-0.5)
        for mq in range(3):
            P.stt(qn[:, mq, 0:N], zq[:, mq, 0:N], vecs[:, VC_GQL + mq:VC_GQL + mq + 1], L["T1"][:, 0:N],
                  ALU.mult, ALU.mult)
        for tb in range(NB):
            pk = pks[tb]
            P.act(L["T4"][0:TB, 0:256], pk[0:TB, 0:256], AF.Square, accum_out=sm[0:TB, tb:tb + 1])
            P.act(sm[0:TB, 8 + tb:9 + tb], sm[0:TB, tb:tb + 1], AF.Ln, bias=eps_t[0:TB], scale=1.0 / 256)
            P.act(sm[0:TB, 16 + tb:17 + tb], sm[0:TB, 8 + tb:9 + tb], AF.Exp, scale=-0.5)
            P.stt(ckv_o[0:TB, tb, :], pk[0:TB, 0:256], sm[0:TB, 16 + tb:17 + tb], gkvb[0:TB, :], ALU.mult, ALU.mult)
            P.copy(ckv_b[0:TB, tb, :], ckv_o[0:TB, tb, :], eng="pool")
            P.tt(L["T5"][0:TB, 0:32], pk[0:TB, 256:288], rCt[0:TB, tb, :], ALU.mult)
            P.tt(L["T5"][0:TB, 32:64], pk[0:TB, 288:320], rSt[0:TB, tb, :], ALU.mult)
            P.tt(kr_o[0:TB, tb, :], L["T5"][0:TB, 0:32], L["T5"][0:TB, 32:64], ALU.add)
            P.copy(kr_b[0:TB, tb, :], kr_o[0:TB, tb, :], eng="pool")
        P.dma(d_ckv_rows.rearrange("(a p) f -> p a f", p=TB), ckv_o[0:TB, 0:NB, :], is_out=True)
        P.dma(d_kr_rows.rearrange("(a p) f -> p a f", p=TB), kr_o[0:TB, 0:NB, :], is_out=True)
        P.label = "mix.vu"
        for half in range(2):
            w = stream(CH_V0 + half).rearrange("p (k c) -> p k c", k=8)
            for tb in range(NB):
                ps = bank()
                for k in range(8):
                    P.mm(ps[0:TB, 0:512], h[:, k, tb * TB:(tb + 1) * TB], w[:, k, :], start=(k == 0), stop=(k == 7))
                vsl = va[0:TB, tb, half * 512:(half + 1) * 512]
                P.act(vsl, ps[0:TB, 0:512], AF.Gelu_apprx_tanh)
                P.act(L["T4"][0:TB, 0:512], vsl, AF.Square, accum_out=sm[0:TB, 24 + 2 * tb + half:25 + 2 * tb + half])
        for half in range(2):
            w = stream(CH_U0 + half).rearrange("p (k c) -> p k c", k=8)
            for m4 in range(4):
                m = half * 4 + m4
                ps = bank()
                for k in range(8):
                    P.mm(ps[:, 0:N], w[:, k, m4 * 128:(m4 + 1) * 128], h[:, k, :], start=(k == 0), stop=(k == 7))
                P.act(u[:, m, 0:N], ps[:, 0:N], AF.Gelu_apprx_tanh)
        ssv = sm[0:TB, 24:24 + 2 * NB].rearrange("p (a b) -> p a b", b=2)
        P.tt(sm[0:TB, 32:32 + NB], ssv[:, :, 0], ssv[:, :, 1], ALU.add)
        P.act(sm[0:TB, 40:40 + NB], sm[0:TB, 32:32 + NB], AF.Ln, bias=eps_t[0:TB], scale=1.0 / D)
        P.act(sm[0:TB, 48:48 + NB], sm[0:TB, 40:40 + NB], AF.Exp, scale=-0.5)
        if sample:
            vb = L["vb"]
            P.ts(vb[0:TB, 0, :], va[0:TB, 0, :], sm[0:TB, 48:49], ALU.mult)
            P.stt(L["vout"][0:TB, :], va[0:TB, 0, :], sm[0:TB, 48:49], L["gvb"][0:TB, :], ALU.mult, ALU.mult)
            P.dma(d_vgs, L["vout"][0:TB, :], is_out=True)
        else:
            vb = va
            for tb in range(NB):
                P.ts(vb[0:TB, tb, :], va[0:TB, tb, :], sm[0:TB, 48 + tb:49 + tb], ALU.mult)
        P.label = "mix.tr"
        for tb in range(NB):
            ptp = bank().bitcast(BF16)
            for c in range(2):
                P.tr(ptp[:, c * TB:(c + 1) * TB], ckv_b[0:TB, tb, c * 128:(c + 1) * 128], ident[0:TB, 0:TB])
            P.tr(ptp[0:32, 2 * TB:3 * TB], kr_b[0:TB, tb, :], ident[0:TB, 0:TB])
            P.copy(ckvT[:, :, tb * TB:(tb + 1) * TB], ptp[:, 0:2 * TB].rearrange("p (a b) -> p a b", a=2))
            P.copy(krT[0:32, tb * TB:(tb + 1) * TB], ptp[0:32, 2 * TB:3 * TB])
        if first:
            checkpoint("m_q")
        P.label = "mix.qheads"
        P.dma(L["rC"][0:32, 0:N], d_rqc[:, pos0:pos0 + N])
        P.dma(L["rS"][0:32, 0:N], d_rqs[:, pos0:pos0 + N])
        kcol0 = PAST if sample else t0
        blk0 = (PAST // 128) if sample else (t0 // 128)
        st_ = {}

        ab = {"n": 0}

        def abank():
            ab["n"] += 1
            return psb[(0, 1, 2, 5)[ab["n"] % 4]]

        def qproj(hd):
            pq, psw = psb[3], psb[4]
            for mq in range(3):
                P.mm(pq[0:96, 0:N], wq[:, mq, hd, :], qn[:, mq, 0:N], start=(mq == 0), stop=(mq == 2))
            for mq in range(3):
                P.mm(psw[0:32, 0:N], wqs[:, mq, hd, :], qn[:, mq, 0:N], start=(mq == 0), stop=(mq == 2))
            xs = L["xs96"][0]
            P.act(xs[0:96, 0:N], pq[0:96, 0:N], AF.Square)
            st_[("q", hd)] = (pq, psw, xs)

        def qfin(hd):
            pq, psw, xs = st_.pop(("q", hd))
            pn = abank()
            P.mm(pn[0:96, 0:N], ones_bf[0:96, 0:96], xs[0:96, 0:N])
            P.act(L["T0"][0:96, 0:N], pn[0:96, 0:N], AF.Ln, bias=eps_t[0:96], scale=1.0 / 96)
            P.act(L["T1"][0:96, 0:N], L["T0"][0:96, 0:N], AF.Exp, scale=-0.5)
            P.tt(L["T2"][0:32, 0:N], pq[0:32, 0:N], L["rC"][0:32, 0:N], ALU.mult)
            P.tt(L["T3"][0:32, 0:N], psw[0:32, 0:N], L["rS"][0:32, 0:N], ALU.mult)
            P.tt(L["T2"][0:32, 0:N], L["T2"][0:32, 0:N], L["T3"][0:32, 0:N], ALU.add)
            P.stt(Qh[0:96, hd, 0:N], pq[0:96, 0:N], vecs[0:96, VC_GQ:VC_GQ + 1], L["T1"][0:96, 0:N], ALU.mult, ALU.mult)
            P.stt(Qh[0:32, hd, 0:N], L["T2"][0:32, 0:N], vecs[0:32, VC_GQ:VC_GQ + 1], L["T1"][0:32, 0:N],
                  ALU.mult, ALU.mult)

        def kproj(hd):
            pk = psb[3]
            P.mm(pk[0:96, 0:N], wk[:, 0, hd, :], ckvT[:, 0, 0:N], start=True, stop=False)
            P.mm(pk[0:96, 0:N], wk[:, 1, hd, :], ckvT[:, 1, 0:N], start=False, stop=False)
            P.mm(pk[0:96, 0:N], emb[0:32, :], krT[0:32, 0:N], start=False, stop=True)
            xs = L["xs96"][1]
            P.act(xs[0:96, 0:N], pk[0:96, 0:N], AF.Square)
            st_[("k", hd)] = (pk, xs)

        def kfin(hd):
            pk, xs = st_.pop(("k", hd))
            pn = abank()
            P.mm(pn[0:96, 0:N], ones_bf[0:96, 0:96], xs[0:96, 0:N])
            P.act(L["T0"][0:96, 0:N], pn[0:96, 0:N], AF.Ln, bias=eps_t[0:96], scale=1.0 / 96)
            P.act(L["T1"][0:96, 0:N], L["T0"][0:96, 0:N], AF.Exp, scale=-0.5)
            P.stt(KT[0:96, hd, kcol0:kcol0 + N], pk[0:96, 0:N], vecs[0:96, VC_GK:VC_GK + 1], L["T1"][0:96, 0:N],
                  ALU.mult, ALU.mult)

        def head_steps(hd):
            return [lambda: qproj(hd), lambda: qfin(hd), lambda: kproj(hd), lambda: kfin(hd)]
        v_rows(ckvT, N, blk0, V)
        for stp in head_steps(0) + head_steps(1):
            stp()
        if first:
            checkpoint("m_kv")
            checkpoint("m_bkv")
        P.label = "mix.attn"
        if sample:
            jl = [(j, 128, 0) for j in range(PAST // 128)] + [(PAST // 128, TS, 0)]
        else:
            jl = []
            for j in range((t0 + N) // 128):
                a = j - t0 // 128
                jl.append((j, 128, 128 * a if a > 0 else 0))
        nj = len(jl)
        LA = 2 if sample else 3
        cnt = {"n": 0, "d": 0}

        def att_head(hd, side, fin_prev):
            po = acc_bank()
            pend = []
            ngroups = nj if not sample else (PAST // 128 + 7) // 8 + 1
            every = max(1, ngroups // max(1, len(side))) if side else 1

            def pv_step(item):
                j, kn, qlo, pt, idx = item
                P.mm(po[0:65, qlo:N], V[0:kn, j, hd, 0:65], pt[0:kn, 0:N - qlo], start=(idx == 0), stop=(idx == nj - 1))

            gi = 0
            idx = 0
            while idx < nj:
                j, kn, qlo = jl[idx]
                if sample and kn == 128:
                    grp = [jl[idx + t] for t in range(min(8, nj - idx)) if jl[idx + t][1] == 128]
                else:
                    grp = [jl[idx]]
                ng = len(grp)
                nq = N - qlo
                pss = abank()
                for t, (jj, kk, ql) in enumerate(grp):
                    P.mm(pss[0:kk, t * nq:(t + 1) * nq], KT[0:96, hd, jj * 128:jj * 128 + kk], Qh[0:96, hd, ql:N])
                if (not sample) and j * 128 >= t0:
                    while len(pend) > 2:
                        pv_step(pend.pop(0))
                    pt = L["ptd"][cnt["d"] % 3]
                    cnt["d"] += 1
                    P.act(pt[0:128, 0:nq], pss[0:128, 0:nq], AF.Exp, scale=QSCALE)
                    P.memset(pt[64:128, 0:64], 0.0, eng="dve")
                else:
                    pt = L["pt"][cnt["n"] % 4]
                    cnt["n"] += 1
                    P.act(pt[0:kn, 0:ng * nq], pss[0:kn, 0:ng * nq], AF.Exp, scale=QSCALE)
                for t, (jj, kk, ql) in enumerate(grp):
                    pend.append((jj, kk, ql, pt[:, t * nq:(t + 1) * nq], idx + t))
                idx += ng
                gi += 1
                while len(pend) > LA * ng:
                    pv_step(pend.pop(0))
                if fin_prev is not None and gi == min(2, ngroups):
                    fin_prev()
                    fin_prev = None
                if side and (gi % every == 0):
                    P.label = "mix.heads_side"
                    side.pop(0)()
                    P.label = "mix.attn"
            while pend:
                pv_step(pend.pop(0))
            if fin_prev is not None:
                fin_prev()
            while side:
                side.pop(0)()
            ri = L["T5"]
            P.act(L["T4"][64:65, 0:N], po[64:65, 0:N], AF.Ln)
            P.act(ri[0:1, 0:N], L["T4"][64:65, 0:N], AF.Exp, scale=-1.0)
            return po, ri

        def att_fin(hd, po, ri):
            pb = abank()
            P.mm(pb[0:64, 0:N], ones_f[0:1, 0:64], ri[0:1, 0:N])
            P.copy(L["T4"][0:64, 0:N], pb[0:64, 0:N])
            P.tt(ob[0:64, hd, 0:N], po[0:64, 0:N], L["T4"][0:64, 0:N], ALU.mult)
        prev = None
        for hd in range(8):
            fp = (lambda h=hd - 1, pr=prev: att_fin(h, *pr)) if prev is not None else None
            prev = att_head(hd, head_steps(hd + 2) if hd + 2 < 8 else [], fp)
        att_fin(7, *prev)
        if first:
            checkpoint("m_att")
        P.label = "mix.spatial"
        for g in range(8):
            ps = bank()
            for cb in range(NB):
                P.mm(ps[:, cb * TB:(cb + 1) * TB], vb[0:TB, cb, g * 128:(g + 1) * 128], wst[0:TB, g, 0:TB])
            tm = L["T5"] if g % 2 == 0 else L["T4"]
            for cb in range(NB):
                P.stt(tm[:, cb * TB:(cb + 1) * TB], ps[:, cb * TB:(cb + 1) * TB], vecs[:, VC_GV + g:VC_GV + g + 1],
                      bb[:, g, 0:TB], ALU.mult, ALU.add)
            P.tt(oa[:, g, 0:N], tm[:, 0:N], u[:, g, 0:N], ALU.mult)
        if first:
            checkpoint("m_gmlp")
        P.label = "mix.merge"
        mg = L["u"]
        for m in range(8):
            w = stream(CH_MG0 + m).rearrange("p (b c) -> p b c", b=32)
            pga, pgb, pa, pbb = bank(), bank(), bank(), bank()
            for k in range(8):
                P.mm(pga[:, 0:N], w[:, k, :], h[:, k, :], start=(k == 0), stop=(k == 7))
            for k in range(8):
                P.mm(pgb[:, 0:N], w[:, 8 + k, :], h[:, k, :], start=(k == 0), stop=(k == 7))
            for k in range(8):
                P.mm(pa[:, 0:N], w[:, 16 + k, :], oa[:, k, 0:N], start=(k == 0), stop=(k == 7))
            for hd in range(8):
                P.mm(pbb[:, 0:N], w[0:64, 24 + hd, :], ob[0:64, hd, 0:N], start=(hd == 0), stop=(hd == 7))
            P.act(L["T2"][:, 0:N], pga[:, 0:N], AF.Sigmoid, bias=vecs[:, VC_BG + m:VC_BG + m + 1])
            P.act(L["T3"][:, 0:N], pgb[:, 0:N], AF.Sigmoid, bias=vecs[:, VC_BG + 8 + m:VC_BG + 9 + m])
            P.tt(L["T2"][:, 0:N], pa[:, 0:N], L["T2"][:, 0:N], ALU.mult)
            P.tt(L["T3"][:, 0:N], pbb[:, 0:N], L["T3"][:, 0:N], ALU.mult)
            P.tt(mg[:, m, 0:N], L["T2"][:, 0:N], L["T3"][:, 0:N], ALU.add)
        P.label = "mix.out"
        for half in range(2):
            w = stream(CH_O0 + half).rearrange("p (k c) -> p k c", k=8)
            for m4 in range(4):
                m = half * 4 + m4
                ps = bank()
                for k in range(8):
                    P.mm(ps[:, 0:N], w[:, k, m4 * 128:(m4 + 1) * 128], mg[:, k, 0:N], start=(k == 0), stop=(k == 7))
                P.stt(xsub[:, m, :], ps[:, 0:N], gt[:, 1, m, s:s + 1], xsub[:, m, :], ALU.mult, ALU.add)

    def main_body():
        dump('modt', modt.rearrange('p a b -> p (a b)'))
        dump('wsm', wsm)
        checkpoint('init')
        if nseq > 0:
            P.memset(Vp[:, :, :, 64:65], 1.0, eng="pool")
        tiles = [(sq, st) for sq in range(nseq) for st in range(2)]
        for ti, (sq, st) in enumerate(tiles):
            ffn(xt, 512, 2, CH_F1U, CH_F1D, 0, sq, Lp, use_pool=(ti > 0))
            if "x1" in dbg and sq == 0:
                P.dma(dbg["x1"][st], xt_flat, is_out=True)
            checkpoint("ffn1")
            for sub in range(2):
                t0 = st * 1024 + sub * 512
                mixer(xt[:, sub], 512, sq, False, t0, Lp, KTp, Vp,
                      d_ckvp[sq, t0:t0 + 512, :], d_krp[sq, t0:t0 + 512, :], t0)
            if "x2" in dbg and sq == 0:
                P.dma(dbg["x2"][st], xt_flat, is_out=True)
            checkpoint("mix")
            nxt = tiles[ti + 1] if ti + 1 < len(tiles) else None

            def after_m(m, sq=sq, st=st, nxt=nxt):
                P.dma(xdram(d_yp, sq, st)[:, :, m, :], xt[:, :, m, :], is_out=True)
                if nxt is not None:
                    P.dma(xt[:, :, m, :], xdram(d_xp, nxt[0], nxt[1])[:, :, m, :])
            ffn(xt, 512, 2, CH_F2U, CH_F2D, 2, sq, Lp, after_m=after_m)

        if CFG["sample"]:
            SA = Alloc(RES_END, ASZ)
            NK = PAST + TS
            KTs = SA.v(BF16, [8, NK])
            Vs = SA.v(BF16, [NK // 128 + 1, 8, 65])
            xs = SA.v(F32, [1, 8, TS])
            Ls = phase_layout(SA, TS, True)
            c32 = SA.v(F32, [2, 512])
            c16 = SA.v(BF16, [2, 512])
            k32 = SA.v(F32, [512])
            k16 = SA.v(BF16, [512])
            P.memset(Vs[:, :, :, 64:65], 1.0, eng="pool")
            P.dma(Ls["gvb"][0:64, :], d_gvb)
            for pc in range(PAST // 512):
                P.dma(c32, d_cckv[:, :, pc * 512:(pc + 1) * 512])
                P.dma(k32[0:32, :], d_ckr[:, pc * 512:(pc + 1) * 512])
                P.copy(c16.rearrange("p a b -> p (a b)"), c32.rearrange("p a b -> p (a b)"))
                P.copy(k16[0:32, :], k32[0:32, :])
                build_kv(c16, k16, 512, pc * 512, pc * 4, Ls, KTs, Vs)
            P.dma(xs.rearrange("p a b c -> p (a b c)"), d_xs)
            ffn(xs, TS, 1, CH_F1U, CH_F1D, 0, 4, Ls)
            mixer(xs[:, 0], TS, 4, True, 0, Ls, KTs, Vs, d_ckvs, d_krs, SEQ)
            ffn(xs, TS, 1, CH_F2U, CH_F2D, 2, 4, Ls)
            P.dma(d_ys, xs.rearrange("p a b c -> p (a b c)"), is_out=True)

    try:
        main_body()
    except _Stop:
        pass
    P.finish()
    P.emit()
    es.close()
    build_program.last_prog = P
    return nc


def _rope_tables():
    half = 16
    freqs = (np.float32(10000.0) ** (-np.arange(half, dtype=np.float32) / np.float32(half))).astype(np.float32)
    pos = np.concatenate([np.arange(SEQ, dtype=np.float32), np.arange(TS, dtype=np.float32) + np.float32(PAST)])
    ang = (pos[:, None] * freqs[None, :]).astype(np.float32)
    cos, sin = np.cos(ang).astype(np.float32), np.sin(ang).astype(np.float32)
    ck = np.concatenate([cos, cos], axis=1)
    sk = np.concatenate([-sin, sin], axis=1)
    return np.ascontiguousarray(ck.T), np.ascontiguousarray(sk.T), np.ascontiguousarray(ck), np.ascontiguousarray(sk)


def _kchunks(w, ncols):
    kk = w.shape[0] // 128
    return np.ascontiguousarray(w.reshape(kk, 128, ncols).transpose(1, 0, 2).reshape(128, kk * ncols))


def _pad(a):
    out = np.zeros((128, CW), np.float32)
    out[:a.shape[0], :a.shape[1]] = a
    return out


def _weight_chunks(w_ffn1_up, w_ffn1_down, w_in, w_branch_a, w_branch_b, w_out, w_ffn2_up, w_ffn2_down):
    ch = np.zeros((NCH, 128, CW), np.float32)

    def ffn_chunks(up, down, bu, bd):
        for gi in range(11):
            blk = np.zeros((8, 128, 2, 256), np.float32)
            upk = up.reshape(8, 128, 2 * DFF)
            for jj in range(2):
                j = 2 * gi + jj
                blk[:, :, jj, 0:128] = upk[:, :, j * 128:(j + 1) * 128]
                blk[:, :, jj, 128:256] = upk[:, :, DFF + j * 128:DFF + (j + 1) * 128]
            ch[bu + gi] = blk.transpose(1, 0, 2, 3).reshape(128, CW)
        dk = down.reshape(NJ, 128, D)
        for m in range(8):
            ch[bd + m] = _pad(dk[:, :, m * 128:(m + 1) * 128].transpose(1, 0, 2).reshape(128, NJ * 128))

    ffn_chunks(w_ffn1_up, w_ffn1_down, CH_F1U, CH_F1D)
    ffn_chunks(w_ffn2_up, w_ffn2_down, CH_F2U, CH_F2D)
    ch[CH_Q] = _pad(_kchunks(w_in[:, 2048:2432], 384))
    kr = w_in[:, 2688:2720]
    kr_sw = np.concatenate([kr[:, 16:32], kr[:, 0:16]], axis=1)
    ch[CH_KV] = _pad(_kchunks(np.concatenate([w_in[:, 2432:2720], kr_sw], axis=1), 320))
    for half in range(2):
        ch[CH_U0 + half] = _kchunks(w_in[:, half * 512:(half + 1) * 512], 512)
        ch[CH_V0 + half] = _kchunks(w_in[:, 1024 + half * 512:1024 + (half + 1) * 512], 512)
        ch[CH_O0 + half] = _kchunks(w_out[:, half * 512:(half + 1) * 512], 512)
    for m in range(8):
        blk = np.zeros((128, 32, 128), np.float32)
        blk[:, 0:8, :] = w_in[:, 2720 + m * 128:2720 + (m + 1) * 128].reshape(8, 128, 128).transpose(1, 0, 2)
        blk[:, 8:16, :] = w_in[:, 3744 + m * 128:3744 + (m + 1) * 128].reshape(8, 128, 128).transpose(1, 0, 2)
        blk[:, 16:24, :] = w_branch_a[:, m * 128:(m + 1) * 128].reshape(8, 128, 128).transpose(1, 0, 2)
        blk[0:64, 24:32, :] = w_branch_b[:, m * 128:(m + 1) * 128].reshape(8, 64, 128).transpose(1, 0, 2)
        ch[CH_MG0 + m] = blk.reshape(128, CW)
    return ch


def _head_perm():
    return np.concatenate([np.arange(64, 96), np.arange(0, 64)])


def _small_weights(w_uq, w_uk, w_uv, gmlp_ws):
    ws = np.zeros((128, WS_N), np.float32)
    perm = _head_perm()
    uq = w_uq.reshape(3, 128, 8, 96)
    ws[:, WS_Q:WS_QS] = uq[:, :, :, perm].transpose(1, 0, 2, 3).reshape(128, -1)
    swap = np.concatenate([np.arange(80, 96), np.arange(64, 80)])
    ws[:, WS_QS:WS_K] = uq[:, :, :, swap].transpose(1, 0, 2, 3).reshape(128, -1)
    uk = np.zeros((2, 128, 8, 96), np.float32)
    uk[:, :, :, 32:96] = w_uk.reshape(2, 128, 8, 64)
    ws[:, WS_K:WS_V] = uk.transpose(1, 0, 2, 3).reshape(128, -1)
    ws[:, WS_V:WS_ST] = w_uv.reshape(2, 128, 512).transpose(1, 0, 2).reshape(128, -1)
    ws[:, WS_ST:WS_EMB] = gmlp_ws.transpose(2, 0, 1).reshape(128, -1)
    ws[0:32, WS_EMB:WS_EMB + 32] = np.eye(32, dtype=np.float32)
    ws[:, WS_ID:WS_N] = np.eye(128, dtype=np.float32)
    return ws


_NC_CACHE = {}


def kernel(x_prompt, x_sample, c_prompt, c_sample, cache_ckv, cache_krope,
           w_mod, b_mod, g_ffn1, w_ffn1_up, w_ffn1_down, g_mix, w_in, g_gmlp_v, gmlp_ws, gmlp_b,
           g_q_lat, w_uq, g_kv_lat, w_uk, w_uv, g_qnorm, g_knorm, b_gate, w_branch_a, w_branch_b,
           w_out, g_ffn2, w_ffn2_up, w_ffn2_down):
    ncore = 8
    key = (CFG["nseq"], CFG["sample"], CFG.get("stop"), tuple(sorted(DEBUG.keys())))
    if key not in _NC_CACHE:
        _NC_CACHE[key] = build_program()
    nc = _NC_CACHE[key]
    in_maps = _prep(x_prompt, x_sample, c_prompt, c_sample, cache_ckv, cache_krope,
                    w_mod, b_mod, g_ffn1, w_ffn1_up, w_ffn1_down, g_mix, w_in, g_gmlp_v, gmlp_ws, gmlp_b,
                    g_q_lat, w_uq, g_kv_lat, w_uk, w_uv, g_qnorm, g_knorm, b_gate, w_branch_a, w_branch_b,
                    w_out, g_ffn2, w_ffn2_up, w_ffn2_down)
    res = run_bass_kernel_spmd(nc, in_maps, core_ids=list(range(ncore)))
    R = res.results
    kernel.last_results = R
    return _gather(R)


def _prep(x_prompt, x_sample, c_prompt, c_sample, cache_ckv, cache_krope,
          w_mod, b_mod, g_ffn1, w_ffn1_up, w_ffn1_down, g_mix, w_in, g_gmlp_v, gmlp_ws, gmlp_b,
          g_q_lat, w_uq, g_kv_lat, w_uk, w_uv, g_qnorm, g_knorm, b_gate, w_branch_a, w_branch_b,
          w_out, g_ffn2, w_ffn2_up, w_ffn2_down):
    f = lambda a: np.asarray(a, dtype=np.float32)
    x_prompt, x_sample, c_prompt, c_sample = f(x_prompt), f(x_sample), f(c_prompt), f(c_sample)
    cache_ckv, cache_krope = f(cache_ckv)[0], f(cache_krope)[0]
    ncore = 8

    wch = _weight_chunks(f(w_ffn1_up)[0], f(w_ffn1_down)[0], f(w_in)[0], f(w_branch_a)[0], f(w_branch_b)[0],
                         f(w_out)[0], f(w_ffn2_up)[0], f(w_ffn2_down)[0])
    wsmall = _small_weights(f(w_uq)[0], f(w_uk)[0], f(w_uv)[0], f(gmlp_ws)[0])
    maskT = np.triu(np.ones((128, 128), np.float32))
    perm = _head_perm()
    vecs = np.zeros((128, VC_N), np.float32)
    fm = lambda v: np.ascontiguousarray(v.reshape(-1, 128).T)
    vecs[:, VC_G1:VC_G1 + 8] = fm(f(g_ffn1)[0])
    vecs[:, VC_G2:VC_G2 + 8] = fm(f(g_mix)[0])
    vecs[:, VC_G3:VC_G3 + 8] = fm(f(g_ffn2)[0])
    vecs[:, VC_BMOD:VC_BMOD + 72] = fm(f(b_mod)[0])
    vecs[:, VC_GQL:VC_GQL + 3] = fm(f(g_q_lat)[0])
    vecs[:, VC_BG:VC_BG + 16] = fm(f(b_gate)[0])
    vecs[0:96, VC_GQ] = f(g_qnorm)[0][perm]
    vecs[0:96, VC_GK] = f(g_knorm)[0][perm]
    vecs[:, VC_GV:VC_GV + 8] = fm(f(g_gmlp_v)[0])
    vecs[:, VC_EPS] = EPS
    vecs[0:5, VC_ID5:VC_ID5 + 5] = np.eye(5, dtype=np.float32)
    gkvb = np.ascontiguousarray(np.broadcast_to(f(g_kv_lat)[0][None, :], (128, 256)))
    bbr = np.ascontiguousarray(np.broadcast_to(f(gmlp_b)[0].reshape(1, 1024), (128, 1024)))
    gvb = np.ascontiguousarray(np.broadcast_to(f(g_gmlp_v)[0][None, :], (64, 1024)))
    rqc, rqs, rkc, rks = _rope_tables()
    wmod = np.ascontiguousarray(f(w_mod)[0])

    in_maps = []
    for c in range(ncore):
        xp = x_prompt[4 * c:4 * c + 4]
        xpl = xp.reshape(4, 2, 2, 512, 8, 128).transpose(0, 1, 5, 2, 4, 3).reshape(4, 2, 128, 8192)
        xsl = x_sample[c].reshape(TS, 8, 128).transpose(2, 1, 0).reshape(128, 8 * TS)
        call = np.concatenate([c_prompt[4 * c:4 * c + 4], c_sample[c:c + 1]], axis=0)
        ct = call.reshape(5, 8, 128).transpose(2, 1, 0).reshape(128, 40)
        cckv = cache_ckv[c].reshape(PAST, 2, 128).transpose(2, 1, 0)
        ckr = cache_krope[c].T
        in_maps.append({
            "xp": np.ascontiguousarray(xpl), "xs": np.ascontiguousarray(xsl), "ct": np.ascontiguousarray(ct),
            "wmod": wmod, "wch": wch, "wsmall": wsmall, "maskT": maskT, "vecs": vecs, "gkvb": gkvb, "bb": bbr,
            "gvb": gvb, "ropeqc": rqc, "ropeqs": rqs, "ropekc": rkc, "ropeks": rks,
            "cckv": np.ascontiguousarray(cckv), "ckr": np.ascontiguousarray(ckr),
        })
    return in_maps


def _gather(R):
    ncore = 8
    y_p = np.zeros((32, SEQ, D), np.float32)
    y_s = np.zeros((8, TS, D), np.float32)
    ckv_p = np.zeros((1, 32, SEQ, 256), np.float32)
    kr_p = np.zeros((1, 32, SEQ, 32), np.float32)
    ckv_s = np.zeros((1, 8, TS, 256), np.float32)
    kr_s = np.zeros((1, 8, TS, 32), np.float32)
    vg_s = np.zeros((1, 8, TS, D), np.float32)
    for c in range(ncore):
        yp = np.asarray(R[c]["yp"]).reshape(4, 2, 128, 2, 8, 512).transpose(0, 1, 3, 5, 4, 2).reshape(4, SEQ, D)
        y_p[4 * c:4 * c + 4] = yp
        y_s[c] = np.asarray(R[c]["ys"]).reshape(128, 8, TS).transpose(2, 1, 0).reshape(TS, D)
        ckv_p[0, 4 * c:4 * c + 4] = np.asarray(R[c]["ckvp"])
        kr_p[0, 4 * c:4 * c + 4] = np.asarray(R[c]["krp"])
        ckv_s[0, c] = np.asarray(R[c]["ckvs"])
        kr_s[0, c] = np.asarray(R[c]["krs"])
        vg_s[0, c] = np.asarray(R[c]["vgs"])
    return (y_p, y_s, ckv_p, kr_p, ckv_s, kr_s, vg_s)
```

```python
import numpy as np
import concourse.bass as bass
import concourse.mybir as mybir
from concourse.bass_utils import run_bass_kernel_spmd

F32 = mybir.dt.float32
BF16 = mybir.dt.bfloat16
U8 = mybir.dt.uint8
AF = mybir.ActivationFunctionType
ALU = mybir.AluOpType
ESZ = {F32: 4, BF16: 2, U8: 1}

D = 1024
DFF = 2816
NJ = 22
SEQ = 2048
NSEQ = 4
TS = 64
PAST = 4096
EPS = 1e-6
QSCALE = 96.0 ** -0.5

CH_F1U, CH_F1D = 0, 11
CH_Q, CH_KV, CH_U0, CH_V0, CH_MG0, CH_O0 = 19, 20, 21, 23, 25, 33
CH_F2U, CH_F2D = 35, 46
NCH = 54
CW = 4096

WS_Q, WS_QS, WS_K, WS_V, WS_ST, WS_EMB, WS_ID, WS_N = 0, 2304, 3072, 4608, 5632, 6656, 6752, 6880
VC_G1, VC_G2, VC_G3, VC_BMOD, VC_GQL, VC_BG, VC_GQ, VC_GK, VC_GV, VC_EPS, VC_ID5, VC_N = 0, 8, 16, 24, 96, 99, 115, 116, 117, 125, 126, 131

PAGE = 256
DEBUG = {}
CFG = {"nseq": NSEQ, "sample": True}


class Ins:
    __slots__ = ("eng", "fn", "deps", "marked", "ticket", "isdma", "dsem", "dval", "label")

    def __init__(self, eng, fn, isdma=False):
        self.eng = eng
        self.fn = fn
        self.deps = []
        self.marked = False
        self.ticket = 0
        self.isdma = isdma
        self.dsem = None
        self.dval = 0


class Prog:
    ENGS = ("pe", "act", "dve", "pool", "sp")

    def __init__(self, nc):
        self.nc = nc
        self.streams = {e: [] for e in self.ENGS}
        self.w = {}
        self.r = {}
        self.dmacount = {e: 0 for e in self.ENGS}
        self.dmahist = {e: [] for e in self.ENGS}
        self.KRING = 8
        self.out_dmas = []
        self.kcache = {}
        self.label = ''

    def keys(self, ap):
        name = ap.tensor.name
        if name.startswith("ps"):
            return [("ps", int(name[2:]))]
        if name != "arena":
            return []
        es = ESZ[ap.dtype]
        pairs = [tuple(x) for x in ap.ap]
        ck = (ap.offset, tuple(pairs), es)
        got = self.kcache.get(ck)
        if got is not None:
            return got
        pstride = pairs[0][0]
        lo = ap.offset % pstride if pstride > 0 else ap.offset

        def expand(off, dims):
            dims = [d for d in dims if d[1] > 1]
            if not dims:
                return [(off, off)]
            st, cnt = dims[0]
            rest = dims[1:]
            ext = sum((c - 1) * s_ for s_, c in rest)
            if st <= ext + 1 or cnt > 32:
                return [(off, off + (cnt - 1) * st + ext)]
            out = []
            for i in range(cnt):
                out += expand(off + i * st, rest)
            return out

        pages = set()
        for (a, b) in expand(lo, pairs[1:]):
            for p in range(a * es // PAGE, ((b + 1) * es - 1) // PAGE + 1):
                pages.add(p)
        got = [("sb", p) for p in sorted(pages)]
        self.kcache[ck] = got
        return got

    def _dep(self, ins, prod, kind):
        if prod is None or prod is ins:
            return
        if (not prod.isdma) and (not ins.isdma) and prod.eng == ins.eng:
            if ins.eng == "pe":
                return
        if prod not in ins.deps:
            ins.deps.append(prod)
            prod.marked = True

    def add(self, eng, fn, reads=(), writes=(), isdma=False, rkeys=(), wkeys=()):
        ins = Ins(eng, fn, isdma)
        ins.label = self.label
        rk = list(rkeys)
        for ap in reads:
            if ap is not None and not isinstance(ap, (int, float)):
                rk += self.keys(ap)
        wk = list(wkeys)
        for ap in writes:
            if ap is not None:
                wk += self.keys(ap)
        for k in rk:
            self._dep(ins, self.w.get(k), "raw")
            if k[0] == "ps":
                for rd in self.r.get(k, ()):
                    if rd.eng != ins.eng:
                        self._dep(ins, rd, "rar")
        for k in wk:
            self._dep(ins, self.w.get(k), "waw")
            for rd in self.r.get(k, ()):
                self._dep(ins, rd, "war")
        for k in rk:
            self.r.setdefault(k, []).append(ins)
        for k in wk:
            self.w[k] = ins
            self.r[k] = []
        if isdma:
            i = self.dmacount[eng]
            self.dmacount[eng] += 1
            hist = self.dmahist[eng]
            if i >= self.KRING:
                prev = hist[i - self.KRING]
                if prev not in ins.deps:
                    ins.deps.append(prev)
            hist.append(ins)
            ins.dval = 16 * (i // self.KRING + 1)
            ins.dsem = (eng, i % self.KRING)
        self.streams[eng].append(ins)
        return ins

    def emit(self):
        nc = self.nc
        import contextlib
        with contextlib.ExitStack() as es:
            esem = {e: es.enter_context(nc.semaphore("done_" + e)) for e in ("pe", "act", "dve", "pool")}
            dsem = {}
            for e in self.ENGS:
                if self.dmacount[e] > 0:
                    for i in range(self.KRING):
                        dsem[(e, i)] = es.enter_context(nc.semaphore("dma_%s_%d" % (e, i)))
            for e in ("pe", "act", "dve", "pool"):
                t = 0
                for ins in self.streams[e]:
                    if ins.isdma:
                        continue
                    if ins.marked:
                        t += 1
                        ins.ticket = t
            block = es.enter_context(nc.Block())

            def run(ename, eng):
                seen = {}
                for ins in self.streams[ename]:
                    for p in ins.deps:
                        if p.isdma:
                            key, val, sem = p.dsem, p.dval, dsem[p.dsem]
                        else:
                            key, val, sem = p.eng, p.ticket, esem[p.eng]
                        if seen.get(key, 0) >= val:
                            continue
                        seen[key] = val
                        eng.wait_ge(sem, val)
                    r = ins.fn(eng)
                    if ins.isdma:
                        r.then_inc(dsem[ins.dsem], 16)
                    elif ins.marked:
                        r.then_inc(esem[ins.eng], 1)

            @block.tensor
            def _(e):
                run("pe", e)

            @block.scalar
            def _(e):
                run("act", e)

            @block.vector
            def _(e):
                run("dve", e)

            @block.gpsimd
            def _(e):
                run("pool", e)

            @block.sync
            def _(e):
                run("sp", e)

    def mm(self, out, lhsT, rhs, start=True, stop=True):
        return self.add("pe", lambda e: e.matmul(out, lhsT=lhsT, rhs=rhs, start=start, stop=stop),
                        reads=[lhsT, rhs], writes=[out])

    def tr(self, out, in_, ident):
        return self.add("pe", lambda e: e.transpose(out, in_, ident), reads=[in_, ident], writes=[out])

    def act(self, out, in_, func, bias=None, scale=1.0, accum_out=None):
        kw = {}
        if bias is not None:
            kw["bias"] = bias
        if accum_out is not None:
            kw["accum_out"] = accum_out
        return self.add("act", lambda e: e.activation(out=out, in_=in_, func=func, scale=scale, **kw),
                        reads=[in_, bias, scale], writes=[out, accum_out])

    def tt(self, out, in0, in1, op, eng="dve"):
        return self.add(eng, lambda e: e.tensor_tensor(out=out, in0=in0, in1=in1, op=op),
                        reads=[in0, in1], writes=[out])

    def ts(self, out, in0, s1, op0, s2=None, op1=None, eng="dve"):
        if op1 is None:
            fn = lambda e: e.tensor_scalar(out=out, in0=in0, scalar1=s1, scalar2=None, op0=op0)
        else:
            fn = lambda e: e.tensor_scalar(out=out, in0=in0, scalar1=s1, scalar2=s2, op0=op0, op1=op1)
        return self.add(eng, fn, reads=[in0, s1, s2], writes=[out])

    def stt(self, out, in0, scalar, in1, op0, op1):
        return self.add("dve", lambda e: e.scalar_tensor_tensor(out=out, in0=in0, scalar=scalar, in1=in1, op0=op0, op1=op1),
                        reads=[in0, scalar, in1], writes=[out])

    def copy(self, out, in_, eng="dve"):
        return self.add(eng, lambda e: e.tensor_copy(out=out, in_=in_), reads=[in_], writes=[out])

    def memset(self, out, val, eng="pool"):
        return self.add(eng, lambda e: e.memset(out, val), writes=[out])

    def dma(self, out, in_, q="pool", rkeys=(), wkeys=(), is_out=False):
        ins = self.add(q, lambda e: e.dma_start(out=out, in_=in_), reads=[in_], writes=[out], isdma=True,
                       rkeys=rkeys, wkeys=wkeys)
        if is_out:
            self.out_dmas.append(ins)
        return ins

    def finish(self):
        ins = Ins("sp", lambda e: e.nop())
        for d in self.out_dmas:
            ins.deps.append(d)
        for e in ("pe", "act", "dve", "pool"):
            if self.streams[e]:
                last = [x for x in self.streams[e] if not x.isdma]
                if last:
                    last[-1].marked = True
                    ins.deps.append(last[-1])
        self.streams["sp"].append(ins)


def build_program():
    nc = bass.Bass("TRN2", target_bir_lowering=False)
    nseq = CFG["nseq"]

    def din(name, shape, dt=F32):
        return nc.dram_tensor(name, list(shape), dt, kind="ExternalInput").ap()

    def dout(name, shape, dt=F32):
        return nc.dram_tensor(name, list(shape), dt, kind="ExternalOutput").ap()

    d_xp = din("xp", [NSEQ, 2, 128, 8192])
    d_xs = din("xs", [128, 8 * TS])
    d_ct = din("ct", [128, 40])
    d_wmod = din("wmod", [1024, 9216])
    d_wch = din("wch", [NCH, 128, CW])
    d_wsm = din("wsmall", [128, WS_N])
    d_mask = din("maskT", [128, 128])
    d_vecs = din("vecs", [128, VC_N])
    d_gkvb = din("gkvb", [128, 256])
    d_bb = din("bb", [128, 1024])
    d_gvb = din("gvb", [64, 1024])
    d_rqc = din("ropeqc", [32, SEQ + TS])
    d_rqs = din("ropeqs", [32, SEQ + TS])
    d_rkc = din("ropekc", [SEQ + TS, 32])
    d_rks = din("ropeks", [SEQ + TS, 32])
    d_cckv = din("cckv", [128, 2, PAST])
    d_ckr = din("ckr", [32, PAST])

    d_yp = dout("yp", [NSEQ, 2, 128, 8192])
    d_ys = dout("ys", [128, 8 * TS])
    d_ckvp = dout("ckvp", [NSEQ, SEQ, 256])
    d_krp = dout("krp", [NSEQ, SEQ, 32])
    d_ckvs = dout("ckvs", [TS, 256])
    d_krs = dout("krs", [TS, 32])
    d_vgs = dout("vgs", [TS, 1024])
    d_scr = nc.dram_tensor("wscr", [NCH, 128, CW], BF16, kind="Internal").ap()
    dbg = {}
    for name, shape in DEBUG.items():
        dbg[name] = dout("dbg_" + name, shape)

    import contextlib
    es = contextlib.ExitStack()
    ASZ = 212736
    arena = es.enter_context(nc.sbuf_tensor("arena", [128, ASZ], U8))
    psb = [es.enter_context(nc.psum_tensor("ps%d" % i, [128, 512], F32)) for i in range(8)]
    P = Prog(nc)

    def view(off, dt, shape):
        n = 1
        for s in shape:
            n *= s
        nbytes = n * ESZ[dt]
        assert off % 4 == 0 and off + nbytes <= ASZ, (off, nbytes)
        a = arena[:, off:off + nbytes].bitcast(dt)
        if len(shape) == 1:
            return a
        if len(shape) == 2:
            return a.rearrange("p (a b) -> p a b", a=shape[0])
        if len(shape) == 3:
            return a.rearrange("p (a b c) -> p a b c", a=shape[0], b=shape[1])
        raise ValueError

    class Alloc:
        def __init__(self, base, limit):
            self.base, self.p, self.limit = base, base, limit

        def take(self, nbytes):
            nbytes = (nbytes + PAGE - 1) // PAGE * PAGE
            o = self.p
            self.p += nbytes
            assert self.p <= self.limit, ("arena overflow", self.p, self.limit)
            return o

        def v(self, dt, shape):
            n = 1
            for s in shape:
                n *= s
            return view(self.take(n * ESZ[dt]), dt, shape)

    RA = Alloc(0, ASZ)
    wsm = RA.v(BF16, [WS_N])
    wq = wsm[:, WS_Q:WS_QS].rearrange("p (a b c) -> p a b c", a=3, b=8)
    wqs = wsm[:, WS_QS:WS_K].rearrange("p (a b c) -> p a b c", a=3, b=8)
    wk = wsm[:, WS_K:WS_V].rearrange("p (a b c) -> p a b c", a=2, b=8)
    wv = wsm[:, WS_V:WS_ST].rearrange("p (a b) -> p a b", a=2)
    wst = wsm[:, WS_ST:WS_EMB].rearrange("p (a b) -> p a b", a=8)
    emb = wsm[:, WS_EMB:WS_ID]
    ident = wsm[:, WS_ID:WS_N]
    ones_bf = RA.v(BF16, [128])
    ones_f = RA.v(F32, [64])
    vecs = RA.v(F32, [VC_N])
    eps_t = vecs[:, VC_EPS:VC_EPS + 1]
    modt = RA.v(F32, [72, 5])
    gs = RA.v(F32, [3, 8, 5])
    gt = RA.v(F32, [3, 8, 5])
    gkvb = RA.v(F32, [256])
    bb = RA.v(F32, [8, 128])
    RING0 = RA.take(4 * 8192)
    RES_END = RA.p

    ring_state = {"n": 0}

    def stream(cid):
        i = ring_state["n"]
        ring_state["n"] += 1
        off = RING0 + (i % 4) * 8192
        slot = view(off, BF16, [CW])
        P.dma(slot, d_scr[cid], q="sp", rkeys=[("scr", cid)])
        return slot

    bank_state = {"n": 0}

    def bank():
        i = bank_state["n"]
        bank_state["n"] += 1
        return psb[i % 6]

    acc_state = {"n": 0}

    def acc_bank():
        i = acc_state["n"]
        acc_state["n"] += 1
        return psb[6 + i % 2]

    def sh_ap(i, k, s):
        return modt[:, (3 * i) * 8 + k, s:s + 1]

    def sh_ap2(i, k, s):
        return modt[:, (3 * i) * 8 + k, s:s + 1]

    class _Stop(Exception):
        pass

    def checkpoint(name):
        if CFG.get('stop') == name:
            raise _Stop()

    def dump(name, ap):
        if name in dbg:
            P.dma(dbg[name], ap, is_out=True)

    def phase_layout(A, N, sample):
        L = {}
        base = A.p
        nsub = 2 if not sample else 1
        F = Alloc(base, ASZ)
        L["fh"] = F.v(BF16, [nsub, 8, N])
        fg0 = F.p
        L["fg"] = F.v(BF16, [nsub, NJ, N])
        fend = F.p
        L["xsqs"] = [view(fg0 + i * 8 * N * 2, BF16, [8, N]) for i in range(nsub)]
        M = Alloc(base, ASZ)
        L["mh"] = M.v(BF16, [8, N])
        if sample:
            L["va"] = M.v(F32, [1, 1024])
            L["vb"] = M.v(BF16, [1, 1024])
            L["vout"] = M.v(F32, [1024])
            L["gvb"] = M.v(F32, [1024])
            L["zq"] = M.v(F32, [3, N])
            L["xsq"] = M.v(BF16, [8, N])
            L["mg"] = M.v(BF16, [8, N])
        else:
            a0 = M.take(8192)
            L["va"] = view(a0, BF16, [4, 1024])
            L["zq"] = view(a0, F32, [3, N])
            L["xsq"] = view(a0, BF16, [8, N])
            L["mg"] = view(a0, BF16, [8, N])
        q0 = M.take(6 * N * 2)
        L["qn"] = view(q0, BF16, [3, N])
        L["xsq3"] = view(q0 + 3 * N * 2, BF16, [3, N])
        L["Qh"] = M.v(BF16, [8, N])
        L["u"] = M.v(BF16, [8, N])
        L["oa"] = L["u"]
        L["ob"] = M.v(BF16, [8, N])
        NBm = max(1, N // 128)
        tsz = max(256, 4 * N) + max(256, 2 * N)
        kvsz = NBm * (1024 + 512 + 128) + max(256, NBm * 64) + 2 * max(256, NBm * 128)
        atsz = 6 * 1024 + 2 * max(256, N * 4)
        ksz = tsz + max(kvsz, atsz)
        k0 = M.take(ksz)
        o = k0
        L["ckvT"] = view(o, BF16, [2, N]); o += max(256, 4 * N)
        L["krT"] = view(o, BF16, [N]); o += max(256, 2 * N)
        o1 = o
        L["ckv_o"] = view(o, F32, [NBm, 256]); o += NBm * 1024
        L["ckv_b"] = view(o, BF16, [NBm, 256]); o += NBm * 512
        L["kr_o"] = view(o, F32, [NBm, 32]); o += NBm * 128
        L["kr_b"] = view(o, BF16, [NBm, 32]); o += max(256, NBm * 64)
        L["rCt"] = view(o, F32, [NBm, 32]); o += max(256, NBm * 128)
        L["rSt"] = view(o, F32, [NBm, 32]); o += max(256, NBm * 128)
        assert o <= k0 + ksz
        o = o1
        L["pt"] = [view(o + i * 1024, BF16, [512]) for i in range(3)]
        L["ptd"] = [view(o + (3 + i) * 1024, BF16, [512]) for i in range(3)]
        o += 6144
        L["rC"] = view(o, F32, [N]); o += max(256, N * 4)
        L["rS"] = view(o, F32, [N]); o += max(256, N * 4)
        assert o <= k0 + ksz
        if 3 * N * 2 >= 2048:
            L["xs96"] = [view(q0 + 3 * N * 2 + i * 1024, BF16, [512]) for i in range(2)]
        else:
            L["xs96"] = [M.v(BF16, [512]) for _ in range(2)]
        mend = M.p
        S = Alloc(max(fend, mend), ASZ)
        for t in range(6):
            L["T%d" % t] = S.v(F32, [512])
        L["sm"] = S.v(F32, [64])
        L["base"] = base
        A.p = S.p
        return L

    PA = Alloc(RES_END, ASZ)
    KTp = PA.v(BF16, [8, SEQ])
    Vp = PA.v(BF16, [16, 8, 65])
    xt = PA.v(F32, [2, 8, 512])
    Lp = phase_layout(PA, 512, False)
    xt_flat = xt.rearrange("p a b c -> p (a b c)")

    def xdram(d, sq, st):
        return d[sq, st].rearrange("p (a b c) -> p a b c", a=2, b=8)

    PH0 = Alloc(Lp["base"], ASZ)
    P.memset(ones_bf, 1.0)
    P.memset(ones_f, 1.0)
    P.dma(vecs, d_vecs)
    P.dma(gkvb, d_gkvb)
    P.dma(bb.rearrange("p a b -> p (a b)"), d_bb)
    ctile = PH0.v(F32, [40])
    csil = PH0.v(F32, [8, 5])
    P.dma(ctile, d_ct)
    if nseq > 0:
        P.dma(xt_flat, d_xp[0, 0])
    P.act(csil.rearrange("p a b -> p (a b)"), ctile, AF.Silu)
    wst32 = PH0.v(F32, [WS_N])
    mk = PH0.v(F32, [128])
    P.dma(wst32, d_wsm)
    P.dma(mk, d_mask)
    for g in range(8):
        sl = wst32[:, WS_ST + g * 128: WS_ST + (g + 1) * 128]
        P.tt(sl, sl, mk, ALU.mult)
    P.copy(wsm[:, 0:3440], wst32[:, 0:3440], eng="dve")
    P.copy(wsm[:, 3440:WS_N], wst32[:, 3440:WS_N], eng="dve")
    wmv = d_wmod.rearrange("(k p) c -> p k c", p=128)
    mtbs = [PH0.v(F32, [512]) for _ in range(2)]
    for blk in range(18):
        stg = view(RING0 + (blk % 2) * 16384, F32, [8, 512])
        P.dma(stg, wmv[:, :, blk * 512:(blk + 1) * 512], q="sp", wkeys=[("wm", blk)])
        ps = bank()
        for k in range(8):
            P.mm(ps[0:5, 0:512], csil[:, k, :], stg[:, k, :], start=(k == 0), stop=(k == 7))
        mtb = mtbs[blk % 2]
        P.copy(mtb[0:5, :], ps[0:5, 0:512])
        pt_ = bank()
        for m4 in range(4):
            P.tr(pt_[:, m4 * 8:m4 * 8 + 5], mtb[0:5, m4 * 128:(m4 + 1) * 128], vecs[0:5, VC_ID5:VC_ID5 + 5])
        for m4 in range(4):
            col = blk * 4 + m4
            P.ts(modt[:, col, :], pt_[:, m4 * 8:m4 * 8 + 5], vecs[:, VC_BMOD + col:VC_BMOD + col + 1], ALU.add)
    for c in range(NCH):
        P.dma(d_scr[c].rearrange("p (a b) -> p a b", b=2048), d_wch[c].rearrange("p (a b) -> p a b", b=2048),
              q="pool", rkeys=[("wm", min(17, 12 + c))], wkeys=[("scr", c)])
    for i in range(3):
        gbase = (VC_G1, VC_G2, VC_G3)[i]
        for k in range(8):
            P.ts(gs[:, i, k, :], modt[:, (3 * i + 1) * 8 + k, :], 1.0, ALU.add,
                 vecs[:, gbase + k:gbase + k + 1], ALU.mult)
        P.ts(gt[:, i].rearrange("p a b -> p (a b)"),
             modt[:, (3 * i + 2) * 8:(3 * i + 3) * 8, :].rearrange("p a b -> p (a b)"),
             0.5 if i != 1 else 1.0, ALU.mult)

    def norm_multi(items, i, s, N, use_pool=True):
        for (xsub, h_out, xsq, tln, trs, t1s) in items:
            if use_pool:
                P.act(xsq[:, 0:6, 0:N], xsub[:, 0:6, :], AF.Square)
                for k in (6, 7):
                    P.tt(xsq[:, k, 0:N], xsub[:, k, :], xsub[:, k, :], ALU.mult, eng="pool")
            else:
                P.act(xsq[:, :, 0:N], xsub, AF.Square)
        pss = []
        for (xsub, h_out, xsq, tln, trs, t1s) in items:
            ps = bank()
            for k in range(8):
                P.mm(ps[:, 0:N], ones_bf, xsq[:, k, 0:N], start=(k == 0), stop=(k == 7))
            pss.append(ps)
        for idx, (xsub, h_out, xsq, tln, trs, t1s) in enumerate(items):
            P.act(tln[:, 0:N], pss[idx][:, 0:N], AF.Ln, bias=eps_t, scale=1.0 / D)
            P.act(trs[:, 0:N], tln[:, 0:N], AF.Exp, scale=-0.5)
        for (xsub, h_out, xsq, tln, trs, t1s) in items:
            for k in range(8):
                t1 = t1s[k % len(t1s)]
                P.tt(t1[:, 0:N], xsub[:, k, :], trs[:, 0:N], ALU.mult)
                P.act(h_out[:, k, :], t1[:, 0:N], AF.Identity, bias=sh_ap2(i, k, s), scale=gs[:, i, k, s:s + 1])

    def ffn(xv, N, nsub, cbase_u, cbase_d, i, s, L, after_m=None, use_pool=True):
        h, g = L["fh"], L["fg"]
        items = []
        for sub in range(nsub):
            tb_ = [(L["T0"], L["T1"], [L["T4"]]), (L["T2"], L["T3"], [L["T5"]])][sub]
            items.append((xv[:, sub], h[:, sub], L["xsqs"][sub], tb_[0], tb_[1], tb_[2]))
        P.label = "ffn%d.norm" % i
        norm_multi(items, i, s, N, use_pool=use_pool)
        P.label = "ffn%d.up" % i
        for gi in range(11):
            w = stream(cbase_u + gi).rearrange("p (k j c) -> p k j c", k=8, j=2)
            for jj in range(2):
                j = 2 * gi + jj
                for sub in range(nsub):
                    pg, pu = bank(), bank()
                    for k in range(8):
                        P.mm(pg[:, 0:N], w[:, k, jj, 0:128], h[:, sub, k, :], start=(k == 0), stop=(k == 7))
                    for k in range(8):
                        P.mm(pu[:, 0:N], w[:, k, jj, 128:256], h[:, sub, k, :], start=(k == 0), stop=(k == 7))
                    sil = L["T2"] if (j + sub) % 2 == 0 else L["T3"]
                    P.act(sil[:, 0:N], pg[:, 0:N], AF.Silu)
                    P.tt(g[:, sub, j, :], pu[:, 0:N], sil[:, 0:N], ALU.mult)
        P.label = "ffn%d.down" % i
        for m in range(8):
            w = stream(cbase_d + m)[:, 0:NJ * 128].rearrange("p (j c) -> p j c", j=NJ)
            for sub in range(nsub):
                po = bank()
                for j in range(NJ):
                    P.mm(po[:, 0:N], w[:, j, :], g[:, sub, j, :], start=(j == 0), stop=(j == NJ - 1))
                P.stt(xv[:, sub, m, :], po[:, 0:N], gt[:, i, m, s:s + 1], xv[:, sub, m, :], ALU.mult, ALU.add)
            if after_m is not None:
                after_m(m)

    def kv_heads(ckvT, krT, n, kcol0, L, KT):
        def proj(hd):
            pk = bank()
            P.mm(pk[0:96, 0:n], wk[:, 0, hd, :], ckvT[:, 0, 0:n], start=True, stop=False)
            P.mm(pk[0:96, 0:n], wk[:, 1, hd, :], ckvT[:, 1, 0:n], start=False, stop=False)
            P.mm(pk[0:96, 0:n], emb[0:32, :], krT[0:32, 0:n], start=False, stop=True)
            xs = L["xs96"][hd % 2]
            P.act(xs[0:96, 0:n], pk[0:96, 0:n], AF.Square)
            return pk, xs

        def fin(hd, pk, xs):
            pn = bank()
            ta, tb_ = (L["T0"], L["T1"]) if hd % 2 == 0 else (L["T2"], L["T3"])
            P.mm(pn[0:96, 0:n], ones_bf[0:96, 0:96], xs[0:96, 0:n])
            P.act(ta[0:96, 0:n], pn[0:96, 0:n], AF.Ln, bias=eps_t[0:96], scale=1.0 / 96)
            P.act(tb_[0:96, 0:n], ta[0:96, 0:n], AF.Exp, scale=-0.5)
            P.stt(KT[0:96, hd, kcol0:kcol0 + n], pk[0:96, 0:n], vecs[0:96, VC_GK:VC_GK + 1], tb_[0:96, 0:n],
                  ALU.mult, ALU.mult)
        prev = None
        for hd in range(8):
            cur = proj(hd)
            if prev is not None:
                fin(hd - 1, *prev)
            prev = cur
        fin(7, *prev)

    def v_rows(ckvT, n, blk0, V):
        TBk = min(128, n)
        for tb in range(n // TBk):
            pv = bank()
            for c in range(2):
                P.mm(pv[0:TBk, 0:512], ckvT[:, c, tb * TBk:(tb + 1) * TBk], wv[:, c, :], start=(c == 0), stop=(c == 1))
            P.copy(V[0:TBk, blk0 + tb, :, 0:64], pv[0:TBk, 0:512].rearrange("p (a b) -> p a b", a=8))

    def build_kv(ckvT, krT, n, kcol0, blk0, L, KT, V):
        kv_heads(ckvT, krT, n, kcol0, L, KT)
        v_rows(ckvT, n, blk0, V)

    mix_count = {"n": 0}

    def mixer(xsub, N, s, sample, t0, L, KT, V, d_ckv_rows, d_kr_rows, pos0):
        first = (mix_count["n"] == 0)
        mix_count["n"] += 1
        TB = min(128, N)
        NB = N // TB
        h = L["mh"]
        sm = L["sm"]
        zq, xsq3, qn, Qh = L["zq"], L["xsq3"], L["qn"], L["Qh"]
        ckv_o, ckv_b, kr_o, kr_b = L["ckv_o"], L["ckv_b"], L["kr_o"], L["kr_b"]
        ckvT, krT, rCt, rSt = L["ckvT"], L["krT"], L["rCt"], L["rSt"]
        u, va, oa, ob = L["u"], L["va"], L["oa"], L["ob"]
        P.label = "mix.norm"
        norm_multi([(xsub, h, L["xsq"], L["T0"], L["T1"], [L["T4"], L["T5"]])], 1, s, N)
        P.label = "mix.zq_kv"
        P.dma(rCt[0:TB, 0:NB, :], d_rkc[pos0:pos0 + N, :].rearrange("(a p) f -> p a f", p=TB))
        P.dma(rSt[0:TB, 0:NB, :], d_rks[pos0:pos0 + N, :].rearrange("(a p) f -> p a f", p=TB))
        wqc = stream(CH_Q)[:, 0:8 * 384].rearrange("p (k c) -> p k c", k=8)
        for mq in range(3):
            ps = bank()
            for k in range(8):
                P.mm(ps[:, 0:N], wqc[:, k, mq * 128:(mq + 1) * 128], h[:, k, :], start=(k == 0), stop=(k == 7))
            P.act(zq[:, mq, 0:N], ps[:, 0:N], AF.Identity)
            P.tt(xsq3[:, mq, 0:N], zq[:, mq, 0:N], zq[:, mq, 0:N], ALU.mult, eng="pool")
        wkvc = stream(CH_KV)[:, 0:8 * 320].rearrange("p (k c) -> p k c", k=8)
        pks = []
        for tb in range(NB):
            pk = bank()
            for k in range(8):
                P.mm(pk[0:TB, 0:320], h[:, k, tb * TB:(tb + 1) * TB], wkvc[:, k, :], start=(k == 0), stop=(k == 7))
            pks.append(pk)
        ps = bank()
        for mq in range(3):
            P.mm(ps[:, 0:N], ones_bf, xsq3[:, mq, 0:N], start=(mq == 0), stop=(mq == 2))
        P.act(L["T0"][:, 0:N], ps[:, 0:N], AF.Ln, bias=eps_t, scale=1.0 / 384)
        P.act(L["T1"][:, 0:N], L["T0"][:, 0:N], AF.Exp, scale=-0.5)
        for mq in range(3):
            P.stt(qn[:, mq, 0:N], zq[:, mq, 0:N], vecs[:, VC_GQL + mq:VC_GQL + mq + 1], L["T1"][:, 0:N],
                  ALU.mult, ALU.mult)
        for tb in range(NB):
            pk = pks[tb]
            P.act(L["T4"][0:TB, 0:256], pk[0:TB, 0:256], AF.Square, accum_out=sm[0:TB, tb:tb + 1])
            P.act(sm[0:TB, 8 + tb:9 + tb], sm[0:TB, tb:tb + 1], AF.Ln, bias=eps_t[0:TB], scale=1.0 / 256)
            P.act(sm[0:TB, 16 + tb:17 + tb], sm[0:TB, 8 + tb:9 + tb], AF.Exp, scale=-0.5)
            P.stt(ckv_o[0:TB, tb, :], pk[0:TB, 0:256], sm[0:TB, 16 + tb:17 + tb], gkvb[0:TB, :], ALU.mult, ALU.mult)
            P.copy(ckv_b[0:TB, tb, :], ckv_o[0:TB, tb, :], eng="pool")
            P.tt(L["T5"][0:TB, 0:32], pk[0:TB, 256:288], rCt[0:TB, tb, :], ALU.mult)
            P.tt(L["T5"][0:TB, 32:64], pk[0:TB, 288:320], rSt[0:TB, tb, :], ALU.mult)
            P.tt(kr_o[0:TB, tb, :], L["T5"][0:TB, 0:32], L["T5"][0:TB, 32:64], ALU.add)
            P.copy(kr_b[0:TB, tb, :], kr_o[0:TB, tb, :], eng="pool")
        P.dma(d_ckv_rows.rearrange("(a p) f -> p a f", p=TB), ckv_o[0:TB, 0:NB, :], is_out=True)
        P.dma(d_kr_rows.rearrange("(a p) f -> p a f", p=TB), kr_o[0:TB, 0:NB, :], is_out=True)
        P.label = "mix.vu"
        for half in range(2):
            w = stream(CH_V0 + half).rearrange("p (k c) -> p k c", k=8)
            for tb in range(NB):
                ps = bank()
                for k in range(8):
                    P.mm(ps[0:TB, 0:512], h[:, k, tb * TB:(tb + 1) * TB], w[:, k, :], start=(k == 0), stop=(k == 7))
                vsl = va[0:TB, tb, half * 512:(half + 1) * 512]
                P.act(vsl, ps[0:TB, 0:512], AF.Gelu_apprx_tanh)
                P.act(L["T4"][0:TB, 0:512], vsl, AF.Square, accum_out=sm[0:TB, 24 + 2 * tb + half:25 + 2 * tb + half])
        for half in range(2):
            w = stream(CH_U0 + half).rearrange("p (k c) -> p k c", k=8)
            for m4 in range(4):
                m = half * 4 + m4
                ps = bank()
                for k in range(8):
                    P.mm(ps[:, 0:N], w[:, k, m4 * 128:(m4 + 1) * 128], h[:, k, :], start=(k == 0), stop=(k == 7))
                P.act(u[:, m, 0:N], ps[:, 0:N], AF.Gelu_apprx_tanh)
        ssv = sm[0:TB, 24:24 + 2 * NB].rearrange("p (a b) -> p a b", b=2)
        P.tt(sm[0:TB, 32:32 + NB], ssv[:, :, 0], ssv[:, :, 1], ALU.add)
        P.act(sm[0:TB, 40:40 + NB], sm[0:TB, 32:32 + NB], AF.Ln, bias=eps_t[0:TB], scale=1.0 / D)
        P.act(sm[0:TB, 48:48 + NB], sm[0:TB, 40:40 + NB], AF.Exp, scale=-0.5)
        if sample:
            vb = L["vb"]
            P.ts(vb[0:TB, 0, :], va[0:TB, 0, :], sm[0:TB, 48:49], ALU.mult)
            P.stt(L["vout"][0:TB, :], va[0:TB, 0, :], sm[0:TB, 48:49], L["gvb"][0:TB, :], ALU.mult, ALU.mult)
            P.dma(d_vgs, L["vout"][0:TB, :], is_out=True)
        else:
            vb = va
            for tb in range(NB):
                P.ts(vb[0:TB, tb, :], va[0:TB, tb, :], sm[0:TB, 48 + tb:49 + tb], ALU.mult)
        P.label = "mix.tr"
        for tb in range(NB):
            ptp = bank().bitcast(BF16)
            for c in range(2):
                P.tr(ptp[:, c * TB:(c + 1) * TB], ckv_b[0:TB, tb, c * 128:(c + 1) * 128], ident[0:TB, 0:TB])
            P.tr(ptp[0:32, 2 * TB:3 * TB], kr_b[0:TB, tb, :], ident[0:TB, 0:TB])
            P.copy(ckvT[:, :, tb * TB:(tb + 1) * TB], ptp[:, 0:2 * TB].rearrange("p (a b) -> p a b", a=2))
            P.copy(krT[0:32, tb * TB:(tb + 1) * TB], ptp[0:32, 2 * TB:3 * TB])
        if first:
            checkpoint("m_q")
        P.label = "mix.qheads"
        P.dma(L["rC"][0:32, 0:N], d_rqc[:, pos0:pos0 + N])
        P.dma(L["rS"][0:32, 0:N], d_rqs[:, pos0:pos0 + N])
        kcol0 = PAST if sample else t0
        blk0 = (PAST // 128) if sample else (t0 // 128)
        st_ = {}

        ab = {"n": 0}

        def abank():
            ab["n"] += 1
            return psb[(0, 1, 2, 5)[ab["n"] % 4]]

        def qproj(hd):
            pq, psw = psb[3], psb[4]
            for mq in range(3):
                P.mm(pq[0:96, 0:N], wq[:, mq, hd, :], qn[:, mq, 0:N], start=(mq == 0), stop=(mq == 2))
            for mq in range(3):
                P.mm(psw[0:32, 0:N], wqs[:, mq, hd, :], qn[:, mq, 0:N], start=(mq == 0), stop=(mq == 2))
            xs = L["xs96"][0]
            P.act(xs[0:96, 0:N], pq[0:96, 0:N], AF.Square)
            st_[("q", hd)] = (pq, psw, xs)

        def qfin(hd):
            pq, psw, xs = st_.pop(("q", hd))
            pn = abank()
            P.mm(pn[0:96, 0:N], ones_bf[0:96, 0:96], xs[0:96, 0:N])
            P.act(L["T0"][0:96, 0:N], pn[0:96, 0:N], AF.Ln, bias=eps_t[0:96], scale=1.0 / 96)
            P.act(L["T1"][0:96, 0:N], L["T0"][0:96, 0:N], AF.Exp, scale=-0.5)
            P.tt(L["T2"][0:32, 0:N], pq[0:32, 0:N], L["rC"][0:32, 0:N], ALU.mult)
            P.tt(L["T3"][0:32, 0:N], psw[0:32, 0:N], L["rS"][0:32, 0:N], ALU.mult)
            P.tt(L["T2"][0:32, 0:N], L["T2"][0:32, 0:N], L["T3"][0:32, 0:N], ALU.add)
            P.stt(Qh[0:96, hd, 0:N], pq[0:96, 0:N], vecs[0:96, VC_GQ:VC_GQ + 1], L["T1"][0:96, 0:N], ALU.mult, ALU.mult)
            P.stt(Qh[0:32, hd, 0:N], L["T2"][0:32, 0:N], vecs[0:32, VC_GQ:VC_GQ + 1], L["T1"][0:32, 0:N],
                  ALU.mult, ALU.mult)

        def kproj(hd):
            pk = psb[3]
            P.mm(pk[0:96, 0:N], wk[:, 0, hd, :], ckvT[:, 0, 0:N], start=True, stop=False)
            P.mm(pk[0:96, 0:N], wk[:, 1, hd, :], ckvT[:, 1, 0:N], start=False, stop=False)
            P.mm(pk[0:96, 0:N], emb[0:32, :], krT[0:32, 0:N], start=False, stop=True)
            xs = L["xs96"][1]
            P.act(xs[0:96, 0:N], pk[0:96, 0:N], AF.Square)
            st_[("k", hd)] = (pk, xs)

        def kfin(hd):
            pk, xs = st_.pop(("k", hd))
            pn = abank()
            P.mm(pn[0:96, 0:N], ones_bf[0:96, 0:96], xs[0:96, 0:N])
            P.act(L["T0"][0:96, 0:N], pn[0:96, 0:N], AF.Ln, bias=eps_t[0:96], scale=1.0 / 96)
            P.act(L["T1"][0:96, 0:N], L["T0"][0:96, 0:N], AF.Exp, scale=-0.5)
            P.stt(KT[0:96, hd, kcol0:kcol0 + N], pk[0:96, 0:N], vecs[0:96, VC_GK:VC_GK + 1], L["T1"][0:96, 0:N],
                  ALU.mult, ALU.mult)

        def head_steps(hd):
            return [lambda: qproj(hd), lambda: qfin(hd), lambda: kproj(hd), lambda: kfin(hd)]
        v_rows(ckvT, N, blk0, V)
        for stp in head_steps(0) + head_steps(1):
            stp()
        if first:
            checkpoint("m_kv")
            checkpoint("m_bkv")
        P.label = "mix.spatial"
        for g in range(8):
            ps = bank()
            for cb in range(NB):
                P.mm(ps[:, cb * TB:(cb + 1) * TB], vb[0:TB, cb, g * 128:(g + 1) * 128], wst[0:TB, g, 0:TB])
            tm = L["T5"] if g % 2 == 0 else L["T4"]
            for cb in range(NB):
                P.stt(tm[:, cb * TB:(cb + 1) * TB], ps[:, cb * TB:(cb + 1) * TB], vecs[:, VC_GV + g:VC_GV + g + 1],
                      bb[:, g, 0:TB], ALU.mult, ALU.add)
            P.tt(oa[:, g, 0:N], tm[:, 0:N], u[:, g, 0:N], ALU.mult)
        if first:
            checkpoint("m_gmlp")
        P.label = "mix.attn"
        if sample:
            jl = [(j, 128, 0) for j in range(PAST // 128)] + [(PAST // 128, TS, 0)]
        else:
            jl = []
            for j in range((t0 + N) // 128):
                a = j - t0 // 128
                jl.append((j, 128, 128 * a if a > 0 else 0))
        nj = len(jl)
        LA = 2
        cnt = {"n": 0, "d": 0}

        def att_head(hd, side, fin_prev):
            po = acc_bank()
            pend = []
            ngroups = nj if not sample else (PAST // 128 + 7) // 8 + 1
            every = max(1, ngroups // max(1, len(side))) if side else 1

            def pv_step(item):
                j, kn, qlo, pt, idx = item
                P.mm(po[0:65, qlo:N], V[0:kn, j, hd, 0:65], pt[0:kn, 0:N - qlo], start=(idx == 0), stop=(idx == nj - 1))

            gi = 0
            idx = 0
            while idx < nj:
                j, kn, qlo = jl[idx]
                if sample and kn == 128:
                    grp = [jl[idx + t] for t in range(min(8, nj - idx)) if jl[idx + t][1] == 128]
                else:
                    grp = [jl[idx]]
                ng = len(grp)
                nq = N - qlo
                pss = abank()
                for t, (jj, kk, ql) in enumerate(grp):
                    P.mm(pss[0:kk, t * nq:(t + 1) * nq], KT[0:96, hd, jj * 128:jj * 128 + kk], Qh[0:96, hd, ql:N])
                if (not sample) and j * 128 >= t0:
                    pt = L["ptd"][cnt["d"] % 3]
                    cnt["d"] += 1
                    P.act(pt[0:128, 0:nq], pss[0:128, 0:nq], AF.Exp, scale=QSCALE)
                    P.memset(pt[64:128, 0:64], 0.0, eng="dve")
                else:
                    pt = L["pt"][cnt["n"] % 3]
                    cnt["n"] += 1
                    P.act(pt[0:kn, 0:ng * nq], pss[0:kn, 0:ng * nq], AF.Exp, scale=QSCALE)
                for t, (jj, kk, ql) in enumerate(grp):
                    pend.append((jj, kk, ql, pt[:, t * nq:(t + 1) * nq], idx + t))
                idx += ng
                gi += 1
                while len(pend) > LA * ng:
                    pv_step(pend.pop(0))
                if fin_prev is not None and gi == min(2, ngroups):
                    fin_prev()
                    fin_prev = None
                if side and (gi % every == 0):
                    P.label = "mix.heads_side"
                    side.pop(0)()
                    P.label = "mix.attn"
            while pend:
                pv_step(pend.pop(0))
            if fin_prev is not None:
                fin_prev()
            while side:
                side.pop(0)()
            ri = L["T5"]
            P.act(L["T4"][64:65, 0:N], po[64:65, 0:N], AF.Ln)
            P.act(ri[0:1, 0:N], L["T4"][64:65, 0:N], AF.Exp, scale=-1.0)
            return po, ri

        def att_fin(hd, po, ri):
            pb = abank()
            P.mm(pb[0:64, 0:N], ones_f[0:1, 0:64], ri[0:1, 0:N])
            P.copy(L["T4"][0:64, 0:N], pb[0:64, 0:N])
            P.tt(ob[0:64, hd, 0:N], po[0:64, 0:N], L["T4"][0:64, 0:N], ALU.mult)
        prev = None
        for hd in range(8):
            fp = (lambda h=hd - 1, pr=prev: att_fin(h, *pr)) if prev is not None else None
            prev = att_head(hd, head_steps(hd + 2) if hd + 2 < 8 else [], fp)
        att_fin(7, *prev)
        if first:
            checkpoint("m_att")
        P.label = "mix.merge"
        mg = L["mg"]
        for m in range(8):
            w = stream(CH_MG0 + m).rearrange("p (b c) -> p b c", b=32)
            pga, pgb, pa, pbb = bank(), bank(), bank(), bank()
            for k in range(8):
                P.mm(pga[:, 0:N], w[:, k, :], h[:, k, :], start=(k == 0), stop=(k == 7))
            for k in range(8):
                P.mm(pgb[:, 0:N], w[:, 8 + k, :], h[:, k, :], start=(k == 0), stop=(k == 7))
            for k in range(8):
                P.mm(pa[:, 0:N], w[:, 16 + k, :], oa[:, k, 0:N], start=(k == 0), stop=(k == 7))
            for hd in range(8):
                P.mm(pbb[:, 0:N], w[0:64, 24 + hd, :], ob[0:64, hd, 0:N], start=(hd == 0), stop=(hd == 7))
            P.act(L["T2"][:, 0:N], pga[:, 0:N], AF.Sigmoid, bias=vecs[:, VC_BG + m:VC_BG + m + 1])
            P.act(L["T3"][:, 0:N], pgb[:, 0:N], AF.Sigmoid, bias=vecs[:, VC_BG + 8 + m:VC_BG + 9 + m])
            P.tt(L["T2"][:, 0:N], pa[:, 0:N], L["T2"][:, 0:N], ALU.mult)
            P.tt(L["T3"][:, 0:N], pbb[:, 0:N], L["T3"][:, 0:N], ALU.mult)
            P.tt(mg[:, m, 0:N], L["T2"][:, 0:N], L["T3"][:, 0:N], ALU.add)
        P.label = "mix.out"
        for half in range(2):
            w = stream(CH_O0 + half).rearrange("p (k c) -> p k c", k=8)
            for m4 in range(4):
                m = half * 4 + m4
                ps = bank()
                for k in range(8):
                    P.mm(ps[:, 0:N], w[:, k, m4 * 128:(m4 + 1) * 128], mg[:, k, 0:N], start=(k == 0), stop=(k == 7))
                P.stt(xsub[:, m, :], ps[:, 0:N], gt[:, 1, m, s:s + 1], xsub[:, m, :], ALU.mult, ALU.add)

    def main_body():
        dump('modt', modt.rearrange('p a b -> p (a b)'))
        dump('wsm', wsm)
        checkpoint('init')
        if nseq > 0:
            P.memset(Vp[:, :, :, 64:65], 1.0, eng="pool")
        tiles = [(sq, st) for sq in range(nseq) for st in range(2)]
        for ti, (sq, st) in enumerate(tiles):
            ffn(xt, 512, 2, CH_F1U, CH_F1D, 0, sq, Lp, use_pool=(ti > 0))
            if "x1" in dbg and sq == 0:
                P.dma(dbg["x1"][st], xt_flat, is_out=True)
            checkpoint("ffn1")
            for sub in range(2):
                t0 = st * 1024 + sub * 512
                mixer(xt[:, sub], 512, sq, False, t0, Lp, KTp, Vp,
                      d_ckvp[sq, t0:t0 + 512, :], d_krp[sq, t0:t0 + 512, :], t0)
            if "x2" in dbg and sq == 0:
                P.dma(dbg["x2"][st], xt_flat, is_out=True)
            checkpoint("mix")
            nxt = tiles[ti + 1] if ti + 1 < len(tiles) else None

            def after_m(m, sq=sq, st=st, nxt=nxt):
                P.dma(xdram(d_yp, sq, st)[:, :, m, :], xt[:, :, m, :], is_out=True)
                if nxt is not None:
                    P.dma(xt[:, :, m, :], xdram(d_xp, nxt[0], nxt[1])[:, :, m, :])
            ffn(xt, 512, 2, CH_F2U, CH_F2D, 2, sq, Lp, after_m=after_m)

        if CFG["sample"]:
            SA = Alloc(RES_END, ASZ)
            NK = PAST + TS
            KTs = SA.v(BF16, [8, NK])
            Vs = SA.v(BF16, [NK // 128 + 1, 8, 65])
            xs = SA.v(F32, [1, 8, TS])
            Ls = phase_layout(SA, TS, True)
            c32 = SA.v(F32, [2, 512])
            c16 = SA.v(BF16, [2, 512])
            k32 = SA.v(F32, [512])
            k16 = SA.v(BF16, [512])
            P.memset(Vs[:, :, :, 64:65], 1.0, eng="pool")
            P.dma(Ls["gvb"][0:64, :], d_gvb)
            for pc in range(PAST // 512):
                P.dma(c32, d_cckv[:, :, pc * 512:(pc + 1) * 512])
                P.dma(k32[0:32, :], d_ckr[:, pc * 512:(pc + 1) * 512])
                P.copy(c16.rearrange("p a b -> p (a b)"), c32.rearrange("p a b -> p (a b)"))
                P.copy(k16[0:32, :], k32[0:32, :])
                build_kv(c16, k16, 512, pc * 512, pc * 4, Ls, KTs, Vs)
            P.dma(xs.rearrange("p a b c -> p (a b c)"), d_xs)
            ffn(xs, TS, 1, CH_F1U, CH_F1D, 0, 4, Ls)
            mixer(xs[:, 0], TS, 4, True, 0, Ls, KTs, Vs, d_ckvs, d_krs, SEQ)
            ffn(xs, TS, 1, CH_F2U, CH_F2D, 2, 4, Ls)
            P.dma(d_ys, xs.rearrange("p a b c -> p (a b c)"), is_out=True)

    try:
        main_body()
    except _Stop:
        pass
    P.finish()
    P.emit()
    es.close()
    build_program.last_prog = P
    return nc


def _rope_tables():
    half = 16
    freqs = (np.float32(10000.0) ** (-np.arange(half, dtype=np.float32) / np.float32(half))).astype(np.float32)
    pos = np.concatenate([np.arange(SEQ, dtype=np.float32), np.arange(TS, dtype=np.float32) + np.float32(PAST)])
    ang = (pos[:, None] * freqs[None, :]).astype(np.float32)
    cos, sin = np.cos(ang).astype(np.float32), np.sin(ang).astype(np.float32)
    ck = np.concatenate([cos, cos], axis=1)
    sk = np.concatenate([-sin, sin], axis=1)
    return np.ascontiguousarray(ck.T), np.ascontiguousarray(sk.T), np.ascontiguousarray(ck), np.ascontiguousarray(sk)


def _kchunks(w, ncols):
    kk = w.shape[0] // 128
    return np.ascontiguousarray(w.reshape(kk, 128, ncols).transpose(1, 0, 2).reshape(128, kk * ncols))


def _pad(a):
    out = np.zeros((128, CW), np.float32)
    out[:a.shape[0], :a.shape[1]] = a
    return out


def _weight_chunks(w_ffn1_up, w_ffn1_down, w_in, w_branch_a, w_branch_b, w_out, w_ffn2_up, w_ffn2_down):
    ch = np.zeros((NCH, 128, CW), np.float32)

    def ffn_chunks(up, down, bu, bd):
        for gi in range(11):
            blk = np.zeros((8, 128, 2, 256), np.float32)
            upk = up.reshape(8, 128, 2 * DFF)
            for jj in range(2):
                j = 2 * gi + jj
                blk[:, :, jj, 0:128] = upk[:, :, j * 128:(j + 1) * 128]
                blk[:, :, jj, 128:256] = upk[:, :, DFF + j * 128:DFF + (j + 1) * 128]
            ch[bu + gi] = blk.transpose(1, 0, 2, 3).reshape(128, CW)
        dk = down.reshape(NJ, 128, D)
        for m in range(8):
            ch[bd + m] = _pad(dk[:, :, m * 128:(m + 1) * 128].transpose(1, 0, 2).reshape(128, NJ * 128))

    ffn_chunks(w_ffn1_up, w_ffn1_down, CH_F1U, CH_F1D)
    ffn_chunks(w_ffn2_up, w_ffn2_down, CH_F2U, CH_F2D)
    ch[CH_Q] = _pad(_kchunks(w_in[:, 2048:2432], 384))
    kr = w_in[:, 2688:2720]
    kr_sw = np.concatenate([kr[:, 16:32], kr[:, 0:16]], axis=1)
    ch[CH_KV] = _pad(_kchunks(np.concatenate([w_in[:, 2432:2720], kr_sw], axis=1), 320))
    for half in range(2):
        ch[CH_U0 + half] = _kchunks(w_in[:, half * 512:(half + 1) * 512], 512)
        ch[CH_V0 + half] = _kchunks(w_in[:, 1024 + half * 512:1024 + (half + 1) * 512], 512)
        ch[CH_O0 + half] = _kchunks(w_out[:, half * 512:(half + 1) * 512], 512)
    for m in range(8):
        blk = np.zeros((128, 32, 128), np.float32)
        blk[:, 0:8, :] = w_in[:, 2720 + m * 128:2720 + (m + 1) * 128].reshape(8, 128, 128).transpose(1, 0, 2)
        blk[:, 8:16, :] = w_in[:, 3744 + m * 128:3744 + (m + 1) * 128].reshape(8, 128, 128).transpose(1, 0, 2)
        blk[:, 16:24, :] = w_branch_a[:, m * 128:(m + 1) * 128].reshape(8, 128, 128).transpose(1, 0, 2)
        blk[0:64, 24:32, :] = w_branch_b[:, m * 128:(m + 1) * 128].reshape(8, 64, 128).transpose(1, 0, 2)
        ch[CH_MG0 + m] = blk.reshape(128, CW)
    return ch


def _head_perm():
    return np.concatenate([np.arange(64, 96), np.arange(0, 64)])


def _small_weights(w_uq, w_uk, w_uv, gmlp_ws):
    ws = np.zeros((128, WS_N), np.float32)
    perm = _head_perm()
    uq = w_uq.reshape(3, 128, 8, 96)
    ws[:, WS_Q:WS_QS] = uq[:, :, :, perm].transpose(1, 0, 2, 3).reshape(128, -1)
    swap = np.concatenate([np.arange(80, 96), np.arange(64, 80)])
    ws[:, WS_QS:WS_K] = uq[:, :, :, swap].transpose(1, 0, 2, 3).reshape(128, -1)
    uk = np.zeros((2, 128, 8, 96), np.float32)
    uk[:, :, :, 32:96] = w_uk.reshape(2, 128, 8, 64)
    ws[:, WS_K:WS_V] = uk.transpose(1, 0, 2, 3).reshape(128, -1)
    ws[:, WS_V:WS_ST] = w_uv.reshape(2, 128, 512).transpose(1, 0, 2).reshape(128, -1)
    ws[:, WS_ST:WS_EMB] = gmlp_ws.transpose(2, 0, 1).reshape(128, -1)
    ws[0:32, WS_EMB:WS_EMB + 32] = np.eye(32, dtype=np.float32)
    ws[:, WS_ID:WS_N] = np.eye(128, dtype=np.float32)
    return ws


_NC_CACHE = {}


def kernel(x_prompt, x_sample, c_prompt, c_sample, cache_ckv, cache_krope,
           w_mod, b_mod, g_ffn1, w_ffn1_up, w_ffn1_down, g_mix, w_in, g_gmlp_v, gmlp_ws, gmlp_b,
           g_q_lat, w_uq, g_kv_lat, w_uk, w_uv, g_qnorm, g_knorm, b_gate, w_branch_a, w_branch_b,
           w_out, g_ffn2, w_ffn2_up, w_ffn2_down):
    ncore = 8
    key = (CFG["nseq"], CFG["sample"], CFG.get("stop"), tuple(sorted(DEBUG.keys())))
    if key not in _NC_CACHE:
        _NC_CACHE[key] = build_program()
    nc = _NC_CACHE[key]
    in_maps = _prep(x_prompt, x_sample, c_prompt, c_sample, cache_ckv, cache_krope,
                    w_mod, b_mod, g_ffn1, w_ffn1_up, w_ffn1_down, g_mix, w_in, g_gmlp_v, gmlp_ws, gmlp_b,
                    g_q_lat, w_uq, g_kv_lat, w_uk, w_uv, g_qnorm, g_knorm, b_gate, w_branch_a, w_branch_b,
                    w_out, g_ffn2, w_ffn2_up, w_ffn2_down)
    res = run_bass_kernel_spmd(nc, in_maps, core_ids=list(range(ncore)))
    R = res.results
    kernel.last_results = R
    return _gather(R)


def _prep(x_prompt, x_sample, c_prompt, c_sample, cache_ckv, cache_krope,
          w_mod, b_mod, g_ffn1, w_ffn1_up, w_ffn1_down, g_mix, w_in, g_gmlp_v, gmlp_ws, gmlp_b,
          g_q_lat, w_uq, g_kv_lat, w_uk, w_uv, g_qnorm, g_knorm, b_gate, w_branch_a, w_branch_b,
          w_out, g_ffn2, w_ffn2_up, w_ffn2_down):
    f = lambda a: np.asarray(a, dtype=np.float32)
    x_prompt, x_sample, c_prompt, c_sample = f(x_prompt), f(x_sample), f(c_prompt), f(c_sample)
    cache_ckv, cache_krope = f(cache_ckv)[0], f(cache_krope)[0]
    ncore = 8

    wch = _weight_chunks(f(w_ffn1_up)[0], f(w_ffn1_down)[0], f(w_in)[0], f(w_branch_a)[0], f(w_branch_b)[0],
                         f(w_out)[0], f(w_ffn2_up)[0], f(w_ffn2_down)[0])
    wsmall = _small_weights(f(w_uq)[0], f(w_uk)[0], f(w_uv)[0], f(gmlp_ws)[0])
    maskT = np.triu(np.ones((128, 128), np.float32))
    perm = _head_perm()
    vecs = np.zeros((128, VC_N), np.float32)
    fm = lambda v: np.ascontiguousarray(v.reshape(-1, 128).T)
    vecs[:, VC_G1:VC_G1 + 8] = fm(f(g_ffn1)[0])
    vecs[:, VC_G2:VC_G2 + 8] = fm(f(g_mix)[0])
    vecs[:, VC_G3:VC_G3 + 8] = fm(f(g_ffn2)[0])
    vecs[:, VC_BMOD:VC_BMOD + 72] = fm(f(b_mod)[0])
    vecs[:, VC_GQL:VC_GQL + 3] = fm(f(g_q_lat)[0])
    vecs[:, VC_BG:VC_BG + 16] = fm(f(b_gate)[0])
    vecs[0:96, VC_GQ] = f(g_qnorm)[0][perm]
    vecs[0:96, VC_GK] = f(g_knorm)[0][perm]
    vecs[:, VC_GV:VC_GV + 8] = fm(f(g_gmlp_v)[0])
    vecs[:, VC_EPS] = EPS
    vecs[0:5, VC_ID5:VC_ID5 + 5] = np.eye(5, dtype=np.float32)
    gkvb = np.ascontiguousarray(np.broadcast_to(f(g_kv_lat)[0][None, :], (128, 256)))
    bbr = np.ascontiguousarray(np.broadcast_to(f(gmlp_b)[0].reshape(1, 1024), (128, 1024)))
    gvb = np.ascontiguousarray(np.broadcast_to(f(g_gmlp_v)[0][None, :], (64, 1024)))
    rqc, rqs, rkc, rks = _rope_tables()
    wmod = np.ascontiguousarray(f(w_mod)[0])

    in_maps = []
    for c in range(ncore):
        xp = x_prompt[4 * c:4 * c + 4]
        xpl = xp.reshape(4, 2, 2, 512, 8, 128).transpose(0, 1, 5, 2, 4, 3).reshape(4, 2, 128, 8192)
        xsl = x_sample[c].reshape(TS, 8, 128).transpose(2, 1, 0).reshape(128, 8 * TS)
        call = np.concatenate([c_prompt[4 * c:4 * c + 4], c_sample[c:c + 1]], axis=0)
        ct = call.reshape(5, 8, 128).transpose(2, 1, 0).reshape(128, 40)
        cckv = cache_ckv[c].reshape(PAST, 2, 128).transpose(2, 1, 0)
        ckr = cache_krope[c].T
        in_maps.append({
            "xp": np.ascontiguousarray(xpl), "xs": np.ascontiguousarray(xsl), "ct": np.ascontiguousarray(ct),
            "wmod": wmod, "wch": wch, "wsmall": wsmall, "maskT": maskT, "vecs": vecs, "gkvb": gkvb, "bb": bbr,
            "gvb": gvb, "ropeqc": rqc, "ropeqs": rqs, "ropekc": rkc, "ropeks": rks,
            "cckv": np.ascontiguousarray(cckv), "ckr": np.ascontiguousarray(ckr),
        })
    return in_maps


def _gather(R):
    ncore = 8
    y_p = np.zeros((32, SEQ, D), np.float32)
    y_s = np.zeros((8, TS, D), np.float32)
    ckv_p = np.zeros((1, 32, SEQ, 256), np.float32)
    kr_p = np.zeros((1, 32, SEQ, 32), np.float32)
    ckv_s = np.zeros((1, 8, TS, 256), np.float32)
    kr_s = np.zeros((1, 8, TS, 32), np.float32)
    vg_s = np.zeros((1, 8, TS, D), np.float32)
    for c in range(ncore):
        yp = np.asarray(R[c]["yp"]).reshape(4, 2, 128, 2, 8, 512).transpose(0, 1, 3, 5, 4, 2).reshape(4, SEQ, D)
        y_p[4 * c:4 * c + 4] = yp
        y_s[c] = np.asarray(R[c]["ys"]).reshape(128, 8, TS).transpose(2, 1, 0).reshape(TS, D)
        ckv_p[0, 4 * c:4 * c + 4] = np.asarray(R[c]["ckvp"])
        kr_p[0, 4 * c:4 * c + 4] = np.asarray(R[c]["krp"])
        ckv_s[0, c] = np.asarray(R[c]["ckvs"])
        kr_s[0, c] = np.asarray(R[c]["krs"])
        vg_s[0, c] = np.asarray(R[c]["vgs"])
    return (y_p, y_s, ckv_p, kr_p, ckv_s, kr_s, vg_s)
```

```python
import numpy as np
import concourse.bass as bass
import concourse.mybir as mybir
from concourse.bass_utils import run_bass_kernel_spmd

F32 = mybir.dt.float32
BF16 = mybir.dt.bfloat16
U8 = mybir.dt.uint8
AF = mybir.ActivationFunctionType
ALU = mybir.AluOpType
ESZ = {F32: 4, BF16: 2, U8: 1}

D = 1024
DFF = 2816
NJ = 22
SEQ = 2048
NSEQ = 4
TS = 64
PAST = 4096
EPS = 1e-6
QSCALE = 96.0 ** -0.5

CH_F1U, CH_F1D = 0, 11
CH_Q, CH_KV, CH_U0, CH_V0, CH_MG0, CH_O0 = 19, 20, 21, 23, 25, 33
CH_F2U, CH_F2D = 35, 46
NCH = 54
CW = 4096

WS_Q, WS_QS, WS_K, WS_V, WS_ST, WS_EMB, WS_ID, WS_N = 0, 2304, 3072, 4608, 5632, 6656, 6752, 6880
VC_G1, VC_G2, VC_G3, VC_BMOD, VC_GQL, VC_BG, VC_GQ, VC_GK, VC_GV, VC_EPS, VC_ID5, VC_N = 0, 8, 16, 24, 96, 99, 115, 116, 117, 125, 126, 131

PAGE = 256
DEBUG = {}
CFG = {"nseq": NSEQ, "sample": True}


class Ins:
    __slots__ = ("eng", "fn", "deps", "marked", "ticket", "isdma", "dsem", "dval", "label")

    def __init__(self, eng, fn, isdma=False):
        self.eng = eng
        self.fn = fn
        self.deps = []
        self.marked = False
        self.ticket = 0
        self.isdma = isdma
        self.dsem = None
        self.dval = 0


class Prog:
    ENGS = ("pe", "act", "dve", "pool", "sp")

    def __init__(self, nc):
        self.nc = nc
        self.streams = {e: [] for e in self.ENGS}
        self.w = {}
        self.r = {}
        self.dmacount = {e: 0 for e in self.ENGS}
        self.dmahist = {e: [] for e in self.ENGS}
        self.KRING = 8
        self.out_dmas = []
        self.kcache = {}
        self.label = ''

    def keys(self, ap):
        name = ap.tensor.name
        if name.startswith("ps"):
            return [("ps", int(name[2:]))]
        if name != "arena":
            return []
        es = ESZ[ap.dtype]
        pairs = [tuple(x) for x in ap.ap]
        ck = (ap.offset, tuple(pairs), es)
        got = self.kcache.get(ck)
        if got is not None:
            return got
        pstride = pairs[0][0]
        lo = ap.offset % pstride if pstride > 0 else ap.offset

        def expand(off, dims):
            dims = [d for d in dims if d[1] > 1]
            if not dims:
                return [(off, off)]
            st, cnt = dims[0]
            rest = dims[1:]
            ext = sum((c - 1) * s_ for s_, c in rest)
            if st <= ext + 1 or cnt > 32:
                return [(off, off + (cnt - 1) * st + ext)]
            out = []
            for i in range(cnt):
                out += expand(off + i * st, rest)
            return out

        pages = set()
        for (a, b) in expand(lo, pairs[1:]):
            for p in range(a * es // PAGE, ((b + 1) * es - 1) // PAGE + 1):
                pages.add(p)
        got = [("sb", p) for p in sorted(pages)]
        self.kcache[ck] = got
        return got

    def _dep(self, ins, prod, kind):
        if prod is None or prod is ins:
            return
        if (not prod.isdma) and (not ins.isdma) and prod.eng == ins.eng:
            if ins.eng == "pe":
                return
        if prod not in ins.deps:
            ins.deps.append(prod)
            prod.marked = True

    def add(self, eng, fn, reads=(), writes=(), isdma=False, rkeys=(), wkeys=()):
        ins = Ins(eng, fn, isdma)
        ins.label = self.label
        rk = list(rkeys)
        for ap in reads:
            if ap is not None and not isinstance(ap, (int, float)):
                rk += self.keys(ap)
        wk = list(wkeys)
        for ap in writes:
            if ap is not None:
                wk += self.keys(ap)
        for k in rk:
            self._dep(ins, self.w.get(k), "raw")
            if k[0] == "ps":
                for rd in self.r.get(k, ()):
                    if rd.eng != ins.eng:
                        self._dep(ins, rd, "rar")
        for k in wk:
            self._dep(ins, self.w.get(k), "waw")
            for rd in self.r.get(k, ()):
                self._dep(ins, rd, "war")
        for k in rk:
            self.r.setdefault(k, []).append(ins)
        for k in wk:
            self.w[k] = ins
            self.r[k] = []
        if isdma:
            i = self.dmacount[eng]
            self.dmacount[eng] += 1
            hist = self.dmahist[eng]
            if i >= self.KRING:
                prev = hist[i - self.KRING]
                if prev not in ins.deps:
                    ins.deps.append(prev)
            hist.append(ins)
            ins.dval = 16 * (i // self.KRING + 1)
            ins.dsem = (eng, i % self.KRING)
        self.streams[eng].append(ins)
        return ins

    def emit(self):
        nc = self.nc
        import contextlib
        with contextlib.ExitStack() as es:
            esem = {e: es.enter_context(nc.semaphore("done_" + e)) for e in ("pe", "act", "dve", "pool")}
            dsem = {}
            for e in self.ENGS:
                if self.dmacount[e] > 0:
                    for i in range(self.KRING):
                        dsem[(e, i)] = es.enter_context(nc.semaphore("dma_%s_%d" % (e, i)))
            for e in ("pe", "act", "dve", "pool"):
                t = 0
                for ins in self.streams[e]:
                    if ins.isdma:
                        continue
                    if ins.marked:
                        t += 1
                        ins.ticket = t
            block = es.enter_context(nc.Block())

            def run(ename, eng):
                seen = {}
                for ins in self.streams[ename]:
                    for p in ins.deps:
                        if p.isdma:
                            key, val, sem = p.dsem, p.dval, dsem[p.dsem]
                        else:
                            key, val, sem = p.eng, p.ticket, esem[p.eng]
                        if seen.get(key, 0) >= val:
                            continue
                        seen[key] = val
                        eng.wait_ge(sem, val)
                    r = ins.fn(eng)
                    if ins.isdma:
                        r.then_inc(dsem[ins.dsem], 16)
                    elif ins.marked:
                        r.then_inc(esem[ins.eng], 1)

            @block.tensor
            def _(e):
                run("pe", e)

            @block.scalar
            def _(e):
                run("act", e)

            @block.vector
            def _(e):
                run("dve", e)

            @block.gpsimd
            def _(e):
                run("pool", e)

            @block.sync
            def _(e):
                run("sp", e)

    def mm(self, out, lhsT, rhs, start=True, stop=True):
        return self.add("pe", lambda e: e.matmul(out, lhsT=lhsT, rhs=rhs, start=start, stop=stop),
                        reads=[lhsT, rhs], writes=[out])

    def tr(self, out, in_, ident):
        return self.add("pe", lambda e: e.transpose(out, in_, ident), reads=[in_, ident], writes=[out])

    def act(self, out, in_, func, bias=None, scale=1.0, accum_out=None):
        kw = {}
        if bias is not None:
            kw["bias"] = bias
        if accum_out is not None:
            kw["accum_out"] = accum_out
        return self.add("act", lambda e: e.activation(out=out, in_=in_, func=func, scale=scale, **kw),
                        reads=[in_, bias, scale], writes=[out, accum_out])

    def tt(self, out, in0, in1, op, eng="dve"):
        return self.add(eng, lambda e: e.tensor_tensor(out=out, in0=in0, in1=in1, op=op),
                        reads=[in0, in1], writes=[out])

    def ts(self, out, in0, s1, op0, s2=None, op1=None, eng="dve"):
        if op1 is None:
            fn = lambda e: e.tensor_scalar(out=out, in0=in0, scalar1=s1, scalar2=None, op0=op0)
        else:
            fn = lambda e: e.tensor_scalar(out=out, in0=in0, scalar1=s1, scalar2=s2, op0=op0, op1=op1)
        return self.add(eng, fn, reads=[in0, s1, s2], writes=[out])

    def stt(self, out, in0, scalar, in1, op0, op1):
        return self.add("dve", lambda e: e.scalar_tensor_tensor(out=out, in0=in0, scalar=scalar, in1=in1, op0=op0, op1=op1),
                        reads=[in0, scalar, in1], writes=[out])

    def copy(self, out, in_, eng="dve"):
        return self.add(eng, lambda e: e.tensor_copy(out=out, in_=in_), reads=[in_], writes=[out])

    def memset(self, out, val, eng="pool"):
        return self.add(eng, lambda e: e.memset(out, val), writes=[out])

    def dma(self, out, in_, q="pool", rkeys=(), wkeys=(), is_out=False):
        ins = self.add(q, lambda e: e.dma_start(out=out, in_=in_), reads=[in_], writes=[out], isdma=True,
                       rkeys=rkeys, wkeys=wkeys)
        if is_out:
            self.out_dmas.append(ins)
        return ins

    def finish(self):
        ins = Ins("sp", lambda e: e.nop())
        for d in self.out_dmas:
            ins.deps.append(d)
        for e in ("pe", "act", "dve", "pool"):
            if self.streams[e]:
                last = [x for x in self.streams[e] if not x.isdma]
                if last:
                    last[-1].marked = True
                    ins.deps.append(last[-1])
        self.streams["sp"].append(ins)


def build_program():
    nc = bass.Bass("TRN2", target_bir_lowering=False)
    nseq = CFG["nseq"]

    def din(name, shape, dt=F32):
        return nc.dram_tensor(name, list(shape), dt, kind="ExternalInput").ap()

    def dout(name, shape, dt=F32):
        return nc.dram_tensor(name, list(shape), dt, kind="ExternalOutput").ap()

    d_xp = din("xp", [NSEQ, 2, 128, 8192])
    d_xs = din("xs", [128, 8 * TS])
    d_ct = din("ct", [128, 40])
    d_wmod = din("wmod", [1024, 9216])
    d_wch = din("wch", [NCH, 128, CW])
    d_wsm = din("wsmall", [128, WS_N])
    d_mask = din("maskT", [128, 128])
    d_vecs = din("vecs", [128, VC_N])
    d_gkvb = din("gkvb", [128, 256])
    d_bb = din("bb", [128, 1024])
    d_gvb = din("gvb", [64, 1024])
    d_rqc = din("ropeqc", [32, SEQ + TS])
    d_rqs = din("ropeqs", [32, SEQ + TS])
    d_rkc = din("ropekc", [SEQ + TS, 32])
    d_rks = din("ropeks", [SEQ + TS, 32])
    d_cckv = din("cckv", [128, 2, PAST])
    d_ckr = din("ckr", [32, PAST])

    d_yp = dout("yp", [NSEQ, 2, 128, 8192])
    d_ys = dout("ys", [128, 8 * TS])
    d_ckvp = dout("ckvp", [NSEQ, SEQ, 256])
    d_krp = dout("krp", [NSEQ, SEQ, 32])
    d_ckvs = dout("ckvs", [TS, 256])
    d_krs = dout("krs", [TS, 32])
    d_vgs = dout("vgs", [TS, 1024])
    d_scr = nc.dram_tensor("wscr", [NCH, 128, CW], BF16, kind="Internal").ap()
    dbg = {}
    for name, shape in DEBUG.items():
        dbg[name] = dout("dbg_" + name, shape)

    import contextlib
    es = contextlib.ExitStack()
    ASZ = 212736
    arena = es.enter_context(nc.sbuf_tensor("arena", [128, ASZ], U8))
    psb = [es.enter_context(nc.psum_tensor("ps%d" % i, [128, 512], F32)) for i in range(8)]
    P = Prog(nc)

    def view(off, dt, shape):
        n = 1
        for s in shape:
            n *= s
        nbytes = n * ESZ[dt]
        assert off % 4 == 0 and off + nbytes <= ASZ, (off, nbytes)
        a = arena[:, off:off + nbytes].bitcast(dt)
        if len(shape) == 1:
            return a
        if len(shape) == 2:
            return a.rearrange("p (a b) -> p a b", a=shape[0])
        if len(shape) == 3:
            return a.rearrange("p (a b c) -> p a b c", a=shape[0], b=shape[1])
        raise ValueError

    class Alloc:
        def __init__(self, base, limit):
            self.base, self.p, self.limit = base, base, limit

        def take(self, nbytes):
            nbytes = (nbytes + PAGE - 1) // PAGE * PAGE
            o = self.p
            self.p += nbytes
            assert self.p <= self.limit, ("arena overflow", self.p, self.limit)
            return o

        def v(self, dt, shape):
            n = 1
            for s in shape:
                n *= s
            return view(self.take(n * ESZ[dt]), dt, shape)

    RA = Alloc(0, ASZ)
    wsm = RA.v(BF16, [WS_N])
    wq = wsm[:, WS_Q:WS_QS].rearrange("p (a b c) -> p a b c", a=3, b=8)
    wqs = wsm[:, WS_QS:WS_K].rearrange("p (a b c) -> p a b c", a=3, b=8)
    wk = wsm[:, WS_K:WS_V].rearrange("p (a b c) -> p a b c", a=2, b=8)
    wv = wsm[:, WS_V:WS_ST].rearrange("p (a b) -> p a b", a=2)
    wst = wsm[:, WS_ST:WS_EMB].rearrange("p (a b) -> p a b", a=8)
    emb = wsm[:, WS_EMB:WS_ID]
    ident = wsm[:, WS_ID:WS_N]
    ones_bf = RA.v(BF16, [128])
    ones_f = RA.v(F32, [64])
    vecs = RA.v(F32, [VC_N])
    eps_t = vecs[:, VC_EPS:VC_EPS + 1]
    modt = RA.v(F32, [72, 5])
    gs = RA.v(F32, [3, 8, 5])
    gt = RA.v(F32, [3, 8, 5])
    gkvb = RA.v(F32, [256])
    bb = RA.v(F32, [8, 128])
    RING0 = RA.take(4 * 8192)
    RES_END = RA.p

    ring_state = {"n": 0}

    def stream(cid):
        i = ring_state["n"]
        ring_state["n"] += 1
        off = RING0 + (i % 4) * 8192
        slot = view(off, BF16, [CW])
        P.dma(slot, d_scr[cid], q="sp", rkeys=[("scr", cid)])
        return slot

    bank_state = {"n": 0}

    def bank():
        i = bank_state["n"]
        bank_state["n"] += 1
        return psb[i % 6]

    acc_state = {"n": 0}

    def acc_bank():
        i = acc_state["n"]
        acc_state["n"] += 1
        return psb[6 + i % 2]

    def sh_ap(i, k, s):
        return modt[:, (3 * i) * 8 + k, s:s + 1]

    def sh_ap2(i, k, s):
        return modt[:, (3 * i) * 8 + k, s:s + 1]

    class _Stop(Exception):
        pass

    def checkpoint(name):
        if CFG.get('stop') == name:
            raise _Stop()

    def dump(name, ap):
        if name in dbg:
            P.dma(dbg[name], ap, is_out=True)

    def phase_layout(A, N, sample):
        L = {}
        base = A.p
        nsub = 2 if not sample else 1
        F = Alloc(base, ASZ)
        L["fh"] = F.v(BF16, [nsub, 8, N])
        fg0 = F.p
        L["fg"] = F.v(BF16, [nsub, NJ, N])
        fend = F.p
        L["xsqs"] = [view(fg0 + i * 8 * N * 2, BF16, [8, N]) for i in range(nsub)]
        M = Alloc(base, ASZ)
        L["mh"] = M.v(BF16, [8, N])
        if sample:
            L["va"] = M.v(F32, [1, 1024])
            L["vb"] = M.v(BF16, [1, 1024])
            L["vout"] = M.v(F32, [1024])
            L["gvb"] = M.v(F32, [1024])
            L["zq"] = M.v(F32, [3, N])
            L["xsq"] = M.v(BF16, [8, N])
            L["mg"] = M.v(BF16, [8, N])
        else:
            a0 = M.take(8192)
            L["va"] = view(a0, BF16, [4, 1024])
            L["zq"] = view(a0, F32, [3, N])
            L["xsq"] = view(a0, BF16, [8, N])
            L["mg"] = view(a0, BF16, [8, N])
        q0 = M.take(6 * N * 2)
        L["qn"] = view(q0, BF16, [3, N])
        L["xsq3"] = view(q0 + 3 * N * 2, BF16, [3, N])
        L["Qh"] = M.v(BF16, [8, N])
        L["u"] = M.v(BF16, [8, N])
        L["oa"] = L["u"]
        L["ob"] = M.v(BF16, [8, N])
        NBm = max(1, N // 128)
        tsz = max(256, 4 * N) + max(256, 2 * N)
        kvsz = NBm * (1024 + 512 + 128) + max(256, NBm * 64) + 2 * max(256, NBm * 128)
        atsz = 6 * 1024 + 2 * max(256, N * 4)
        ksz = tsz + max(kvsz, atsz)
        k0 = M.take(ksz)
        o = k0
        L["ckvT"] = view(o, BF16, [2, N]); o += max(256, 4 * N)
        L["krT"] = view(o, BF16, [N]); o += max(256, 2 * N)
        o1 = o
        L["ckv_o"] = view(o, F32, [NBm, 256]); o += NBm * 1024
        L["ckv_b"] = view(o, BF16, [NBm, 256]); o += NBm * 512
        L["kr_o"] = view(o, F32, [NBm, 32]); o += NBm * 128
        L["kr_b"] = view(o, BF16, [NBm, 32]); o += max(256, NBm * 64)
        L["rCt"] = view(o, F32, [NBm, 32]); o += max(256, NBm * 128)
        L["rSt"] = view(o, F32, [NBm, 32]); o += max(256, NBm * 128)
        assert o <= k0 + ksz
        o = o1
        L["pt"] = [view(o + i * 1024, BF16, [512]) for i in range(3)]
        L["ptd"] = [view(o + (3 + i) * 1024, BF16, [512]) for i in range(3)]
        o += 6144
        L["rC"] = view(o, F32, [N]); o += max(256, N * 4)
        L["rS"] = view(o, F32, [N]); o += max(256, N * 4)
        assert o <= k0 + ksz
        if 3 * N * 2 >= 2048:
            L["xs96"] = [view(q0 + 3 * N * 2 + i * 1024, BF16, [512]) for i in range(2)]
        else:
            L["xs96"] = [M.v(BF16, [512]) for _ in range(2)]
        mend = M.p
        S = Alloc(max(fend, mend), ASZ)
        for t in range(6):
            L["T%d" % t] = S.v(F32, [512])
        L["sm"] = S.v(F32, [64])
        L["base"] = base
        A.p = S.p
        return L

    PA = Alloc(RES_END, ASZ)
    KTp = PA.v(BF16, [8, SEQ])
    Vp = PA.v(BF16, [16, 8, 65])
    xt = PA.v(F32, [2, 8, 512])
    Lp = phase_layout(PA, 512, False)
    xt_flat = xt.rearrange("p a b c -> p (a b c)")

    def xdram(d, sq, st):
        return d[sq, st].rearrange("p (a b c) -> p a b c", a=2, b=8)

    PH0 = Alloc(Lp["base"], ASZ)
    P.memset(ones_bf, 1.0)
    P.memset(ones_f, 1.0)
    P.dma(vecs, d_vecs)
    P.dma(gkvb, d_gkvb)
    P.dma(bb.rearrange("p a b -> p (a b)"), d_bb)
    ctile = PH0.v(F32, [40])
    csil = PH0.v(F32, [8, 5])
    P.dma(ctile, d_ct)
    if nseq > 0:
        P.dma(xt_flat, d_xp[0, 0])
    P.act(csil.rearrange("p a b -> p (a b)"), ctile, AF.Silu)
    wst32 = PH0.v(F32, [WS_N])
    mk = PH0.v(F32, [128])
    P.dma(wst32, d_wsm)
    P.dma(mk, d_mask)
    for g in range(8):
        sl = wst32[:, WS_ST + g * 128: WS_ST + (g + 1) * 128]
        P.tt(sl, sl, mk, ALU.mult)
    P.copy(wsm[:, 0:3440], wst32[:, 0:3440], eng="dve")
    P.copy(wsm[:, 3440:WS_N], wst32[:, 3440:WS_N], eng="dve")
    wmv = d_wmod.rearrange("(k p) c -> p k c", p=128)
    mtbs = [PH0.v(F32, [512]) for _ in range(2)]
    for blk in range(18):
        stg = view(RING0 + (blk % 2) * 16384, F32, [8, 512])
        P.dma(stg, wmv[:, :, blk * 512:(blk + 1) * 512], q="sp", wkeys=[("wm", blk)])
        ps = bank()
        for k in range(8):
            P.mm(ps[0:5, 0:512], csil[:, k, :], stg[:, k, :], start=(k == 0), stop=(k == 7))
        mtb = mtbs[blk % 2]
        P.copy(mtb[0:5, :], ps[0:5, 0:512])
        pt_ = bank()
        for m4 in range(4):
            P.tr(pt_[:, m4 * 8:m4 * 8 + 5], mtb[0:5, m4 * 128:(m4 + 1) * 128], vecs[0:5, VC_ID5:VC_ID5 + 5])
        for m4 in range(4):
            col = blk * 4 + m4
            P.ts(modt[:, col, :], pt_[:, m4 * 8:m4 * 8 + 5], vecs[:, VC_BMOD + col:VC_BMOD + col + 1], ALU.add)
    for c in range(NCH):
        P.dma(d_scr[c].rearrange("p (a b) -> p a b", b=2048), d_wch[c].rearrange("p (a b) -> p a b", b=2048),
              q="pool", rkeys=[("wm", min(17, 12 + c))], wkeys=[("scr", c)])
    for i in range(3):
        gbase = (VC_G1, VC_G2, VC_G3)[i]
        for k in range(8):
            P.ts(gs[:, i, k, :], modt[:, (3 * i + 1) * 8 + k, :], 1.0, ALU.add,
                 vecs[:, gbase + k:gbase + k + 1], ALU.mult)
        P.ts(gt[:, i].rearrange("p a b -> p (a b)"),
             modt[:, (3 * i + 2) * 8:(3 * i + 3) * 8, :].rearrange("p a b -> p (a b)"),
             0.5 if i != 1 else 1.0, ALU.mult)

    def norm_multi(items, i, s, N, use_pool=True):
        for (xsub, h_out, xsq, tln, trs, t1s) in items:
            if use_pool:
                P.act(xsq[:, 0:6, 0:N], xsub[:, 0:6, :], AF.Square)
                for k in (6, 7):
                    P.tt(xsq[:, k, 0:N], xsub[:, k, :], xsub[:, k, :], ALU.mult, eng="pool")
            else:
                P.act(xsq[:, :, 0:N], xsub, AF.Square)
        pss = []
        for (xsub, h_out, xsq, tln, trs, t1s) in items:
            ps = bank()
            for k in range(8):
                P.mm(ps[:, 0:N], ones_bf, xsq[:, k, 0:N], start=(k == 0), stop=(k == 7))
            pss.append(ps)
        for idx, (xsub, h_out, xsq, tln, trs, t1s) in enumerate(items):
            P.act(tln[:, 0:N], pss[idx][:, 0:N], AF.Ln, bias=eps_t, scale=1.0 / D)
            P.act(trs[:, 0:N], tln[:, 0:N], AF.Exp, scale=-0.5)
        for (xsub, h_out, xsq, tln, trs, t1s) in items:
            for k in range(8):
                t1 = t1s[k % len(t1s)]
                P.tt(t1[:, 0:N], xsub[:, k, :], trs[:, 0:N], ALU.mult)
                P.act(h_out[:, k, :], t1[:, 0:N], AF.Identity, bias=sh_ap2(i, k, s), scale=gs[:, i, k, s:s + 1])

    def ffn(xv, N, nsub, cbase_u, cbase_d, i, s, L, after_m=None, use_pool=True):
        h, g = L["fh"], L["fg"]
        items = []
        for sub in range(nsub):
            tb_ = [(L["T0"], L["T1"], [L["T4"]]), (L["T2"], L["T3"], [L["T5"]])][sub]
            items.append((xv[:, sub], h[:, sub], L["xsqs"][sub], tb_[0], tb_[1], tb_[2]))
        P.label = "ffn%d.norm" % i
        norm_multi(items, i, s, N, use_pool=use_pool)
        P.label = "ffn%d.up" % i
        for gi in range(11):
            w = stream(cbase_u + gi).rearrange("p (k j c) -> p k j c", k=8, j=2)
            for jj in range(2):
                j = 2 * gi + jj
                for sub in range(nsub):
                    pg, pu = bank(), bank()
                    for k in range(8):
                        P.mm(pg[:, 0:N], w[:, k, jj, 0:128], h[:, sub, k, :], start=(k == 0), stop=(k == 7))
                    for k in range(8):
                        P.mm(pu[:, 0:N], w[:, k, jj, 128:256], h[:, sub, k, :], start=(k == 0), stop=(k == 7))
                    sil = L["T2"] if (j + sub) % 2 == 0 else L["T3"]
                    P.act(sil[:, 0:N], pg[:, 0:N], AF.Silu)
                    P.tt(g[:, sub, j, :], pu[:, 0:N], sil[:, 0:N], ALU.mult)
        P.label = "ffn%d.down" % i
        for m in range(8):
            w = stream(cbase_d + m)[:, 0:NJ * 128].rearrange("p (j c) -> p j c", j=NJ)
            for sub in range(nsub):
                po = bank()
                for j in range(NJ):
                    P.mm(po[:, 0:N], w[:, j, :], g[:, sub, j, :], start=(j == 0), stop=(j == NJ - 1))
                P.stt(xv[:, sub, m, :], po[:, 0:N], gt[:, i, m, s:s + 1], xv[:, sub, m, :], ALU.mult, ALU.add)
            if after_m is not None:
                after_m(m)

    def kv_heads(ckvT, krT, n, kcol0, L, KT):
        def proj(hd):
            pk = bank()
            P.mm(pk[0:96, 0:n], wk[:, 0, hd, :], ckvT[:, 0, 0:n], start=True, stop=False)
            P.mm(pk[0:96, 0:n], wk[:, 1, hd, :], ckvT[:, 1, 0:n], start=False, stop=False)
            P.mm(pk[0:96, 0:n], emb[0:32, :], krT[0:32, 0:n], start=False, stop=True)
            xs = L["xs96"][hd % 2]
            P.act(xs[0:96, 0:n], pk[0:96, 0:n], AF.Square)
            return pk, xs

        def fin(hd, pk, xs):
            pn = bank()
            ta, tb_ = (L["T0"], L["T1"]) if hd % 2 == 0 else (L["T2"], L["T3"])
            P.mm(pn[0:96, 0:n], ones_bf[0:96, 0:96], xs[0:96, 0:n])
            P.act(ta[0:96, 0:n], pn[0:96, 0:n], AF.Ln, bias=eps_t[0:96], scale=1.0 / 96)
            P.act(tb_[0:96, 0:n], ta[0:96, 0:n], AF.Exp, scale=-0.5)
            P.stt(KT[0:96, hd, kcol0:kcol0 + n], pk[0:96, 0:n], vecs[0:96, VC_GK:VC_GK + 1], tb_[0:96, 0:n],
                  ALU.mult, ALU.mult)
        prev = None
        for hd in range(8):
            cur = proj(hd)
            if prev is not None:
                fin(hd - 1, *prev)
            prev = cur
        fin(7, *prev)

    def v_rows(ckvT, n, blk0, V):
        TBk = min(128, n)
        for tb in range(n // TBk):
            pv = bank()
            for c in range(2):
                P.mm(pv[0:TBk, 0:512], ckvT[:, c, tb * TBk:(tb + 1) * TBk], wv[:, c, :], start=(c == 0), stop=(c == 1))
            P.copy(V[0:TBk, blk0 + tb, :, 0:64], pv[0:TBk, 0:512].rearrange("p (a b) -> p a b", a=8))

    def build_kv(ckvT, krT, n, kcol0, blk0, L, KT, V):
        kv_heads(ckvT, krT, n, kcol0, L, KT)
        v_rows(ckvT, n, blk0, V)

    mix_count = {"n": 0}

    def mixer(xsub, N, s, sample, t0, L, KT, V, d_ckv_rows, d_kr_rows, pos0):
        first = (mix_count["n"] == 0)
        mix_count["n"] += 1
        TB = min(128, N)
        NB = N // TB
        h = L["mh"]
        sm = L["sm"]
        zq, xsq3, qn, Qh = L["zq"], L["xsq3"], L["qn"], L["Qh"]
        ckv_o, ckv_b, kr_o, kr_b = L["ckv_o"], L["ckv_b"], L["kr_o"], L["kr_b"]
        ckvT, krT, rCt, rSt = L["ckvT"], L["krT"], L["rCt"], L["rSt"]
        u, va, oa, ob = L["u"], L["va"], L["oa"], L["ob"]
        P.label = "mix.norm"
        norm_multi([(xsub, h, L["xsq"], L["T0"], L["T1"], [L["T4"], L["T5"]])], 1, s, N)
        P.label = "mix.zq_kv"
        P.dma(rCt[0:TB, 0:NB, :], d_rkc[pos0:pos0 + N, :].rearrange("(a p) f -> p a f", p=TB))
        P.dma(rSt[0:TB, 0:NB, :], d_rks[pos0:pos0 + N, :].rearrange("(a p) f -> p a f", p=TB))
        wqc = stream(CH_Q)[:, 0:8 * 384].rearrange("p (k c) -> p k c", k=8)
        for mq in range(3):
            ps = bank()
            for k in range(8):
                P.mm(ps[:, 0:N], wqc[:, k, mq * 128:(mq + 1) * 128], h[:, k, :], start=(k == 0), stop=(k == 7))
            P.act(zq[:, mq, 0:N], ps[:, 0:N], AF.Identity)
            P.tt(xsq3[:, mq, 0:N], zq[:, mq, 0:N], zq[:, mq, 0:N], ALU.mult, eng="pool")
        wkvc = stream(CH_KV)[:, 0:8 * 320].rearrange("p (k c) -> p k c", k=8)
        pks = []
        for tb in range(NB):
            pk = bank()
            for k in range(8):
                P.mm(pk[0:TB, 0:320], h[:, k, tb * TB:(tb + 1) * TB], wkvc[:, k, :], start=(k == 0), stop=(k == 7))
            pks.append(pk)
        ps = bank()
        for mq in range(3):
            P.mm(ps[:, 0:N], ones_bf, xsq3[:, mq, 0:N], start=(mq == 0), stop=(mq == 2))
        P.act(L["T0"][:, 0:N], ps[:, 0:N], AF.Ln, bias=eps_t, scale=1.0 / 384)
        P.act(L["T1"][:, 0:N], L["T0"][:, 0:N], AF.Exp, scale=-0.5)
        for mq in range(3):
            P.stt(qn[:, mq, 0:N], zq[:, mq, 0:N], vecs[:, VC_GQL + mq:VC_GQL + mq + 1], L["T1"][:, 0:N],
                  ALU.mult, ALU.mult)
        for tb in range(NB):
            pk = pks[tb]
            P.act(L["T4"][0:TB, 0:256], pk[0:TB, 0:256], AF.Square, accum_out=sm[0:TB, tb:tb + 1])
            P.act(sm[0:TB, 8 + tb:9 + tb], sm[0:TB, tb:tb + 1], AF.Ln, bias=eps_t[0:TB], scale=1.0 / 256)
            P.act(sm[0:TB, 16 + tb:17 + tb], sm[0:TB, 8 + tb:9 + tb], AF.Exp, scale=-0.5)
            P.stt(ckv_o[0:TB, tb, :], pk[0:TB, 0:256], sm[0:TB, 16 + tb:17 + tb], gkvb[0:TB, :], ALU.mult, ALU.mult)
            P.copy(ckv_b[0:TB, tb, :], ckv_o[0:TB, tb, :], eng="pool")
            P.tt(L["T5"][0:TB, 0:32], pk[0:TB, 256:288], rCt[0:TB, tb, :], ALU.mult)
            P.tt(L["T5"][0:TB, 32:64], pk[0:TB, 288:320], rSt[0:TB, tb, :], ALU.mult)
            P.tt(kr_o[0:TB, tb, :], L["T5"][0:TB, 0:32], L["T5"][0:TB, 32:64], ALU.add)
            P.copy(kr_b[0:TB, tb, :], kr_o[0:TB, tb, :], eng="pool")
        P.dma(d_ckv_rows.rearrange("(a p) f -> p a f", p=TB), ckv_o[0:TB, 0:NB, :], is_out=True)
        P.dma(d_kr_rows.rearrange("(a p) f -> p a f", p=TB), kr_o[0:TB, 0:NB, :], is_out=True)
        P.label = "mix.vu"
        for half in range(2):
            w = stream(CH_V0 + half).rearrange("p (k c) -> p k c", k=8)
            for tb in range(NB):
                ps = bank()
                for k in range(8):
                    P.mm(ps[0:TB, 0:512], h[:, k, tb * TB:(tb + 1) * TB], w[:, k, :], start=(k == 0), stop=(k == 7))
                vsl = va[0:TB, tb, half * 512:(half + 1) * 512]
                P.act(vsl, ps[0:TB, 0:512], AF.Gelu_apprx_tanh)
                P.act(L["T4"][0:TB, 0:512], vsl, AF.Square, accum_out=sm[0:TB, 24 + 2 * tb + half:25 + 2 * tb + half])
        for half in range(2):
            w = stream(CH_U0 + half).rearrange("p (k c) -> p k c", k=8)
            for m4 in range(4):
                m = half * 4 + m4
                ps = bank()
                for k in range(8):
                    P.mm(ps[:, 0:N], w[:, k, m4 * 128:(m4 + 1) * 128], h[:, k, :], start=(k == 0), stop=(k == 7))
                P.act(u[:, m, 0:N], ps[:, 0:N], AF.Gelu_apprx_tanh)
        ssv = sm[0:TB, 24:24 + 2 * NB].rearrange("p (a b) -> p a b", b=2)
        P.tt(sm[0:TB, 32:32 + NB], ssv[:, :, 0], ssv[:, :, 1], ALU.add)
        P.act(sm[0:TB, 40:40 + NB], sm[0:TB, 32:32 + NB], AF.Ln, bias=eps_t[0:TB], scale=1.0 / D)
        P.act(sm[0:TB, 48:48 + NB], sm[0:TB, 40:40 + NB], AF.Exp, scale=-0.5)
        if sample:
            vb = L["vb"]
            P.ts(vb[0:TB, 0, :], va[0:TB, 0, :], sm[0:TB, 48:49], ALU.mult)
            P.stt(L["vout"][0:TB, :], va[0:TB, 0, :], sm[0:TB, 48:49], L["gvb"][0:TB, :], ALU.mult, ALU.mult)
            P.dma(d_vgs, L["vout"][0:TB, :], is_out=True)
        else:
            vb = va
            for tb in range(NB):
                P.ts(vb[0:TB, tb, :], va[0:TB, tb, :], sm[0:TB, 48 + tb:49 + tb], ALU.mult)
        P.label = "mix.tr"
        for tb in range(NB):
            ptp = bank().bitcast(BF16)
            for c in range(2):
                P.tr(ptp[:, c * TB:(c + 1) * TB], ckv_b[0:TB, tb, c * 128:(c + 1) * 128], ident[0:TB, 0:TB])
            P.tr(ptp[0:32, 2 * TB:3 * TB], kr_b[0:TB, tb, :], ident[0:TB, 0:TB])
            P.copy(ckvT[:, :, tb * TB:(tb + 1) * TB], ptp[:, 0:2 * TB].rearrange("p (a b) -> p a b", a=2))
            P.copy(krT[0:32, tb * TB:(tb + 1) * TB], ptp[0:32, 2 * TB:3 * TB])
        if first:
            checkpoint("m_q")
        P.label = "mix.qheads"
        P.dma(L["rC"][0:32, 0:N], d_rqc[:, pos0:pos0 + N])
        P.dma(L["rS"][0:32, 0:N], d_rqs[:, pos0:pos0 + N])
        kcol0 = PAST if sample else t0
        blk0 = (PAST // 128) if sample else (t0 // 128)
        st_ = {}

        ab = {"n": 0}

        def abank():
            ab["n"] += 1
            return psb[(0, 1, 2, 5)[ab["n"] % 4]]

        def qproj(hd):
            pq, psw = psb[3], psb[4]
            for mq in range(3):
                P.mm(pq[0:96, 0:N], wq[:, mq, hd, :], qn[:, mq, 0:N], start=(mq == 0), stop=(mq == 2))
            for mq in range(3):
                P.mm(psw[0:32, 0:N], wqs[:, mq, hd, :], qn[:, mq, 0:N], start=(mq == 0), stop=(mq == 2))
            xs = L["xs96"][0]
            P.act(xs[0:96, 0:N], pq[0:96, 0:N], AF.Square)
            st_[("q", hd)] = (pq, psw, xs)

        def qfin(hd):
            pq, psw, xs = st_.pop(("q", hd))
            pn = abank()
            P.mm(pn[0:96, 0:N], ones_bf[0:96, 0:96], xs[0:96, 0:N])
            P.act(L["T0"][0:96, 0:N], pn[0:96, 0:N], AF.Ln, bias=eps_t[0:96], scale=1.0 / 96)
            P.act(L["T1"][0:96, 0:N], L["T0"][0:96, 0:N], AF.Exp, scale=-0.5)
            P.tt(L["T2"][0:32, 0:N], pq[0:32, 0:N], L["rC"][0:32, 0:N], ALU.mult)
            P.tt(L["T3"][0:32, 0:N], psw[0:32, 0:N], L["rS"][0:32, 0:N], ALU.mult)
            P.tt(L["T2"][0:32, 0:N], L["T2"][0:32, 0:N], L["T3"][0:32, 0:N], ALU.add)
            P.stt(Qh[0:96, hd, 0:N], pq[0:96, 0:N], vecs[0:96, VC_GQ:VC_GQ + 1], L["T1"][0:96, 0:N], ALU.mult, ALU.mult)
            P.stt(Qh[0:32, hd, 0:N], L["T2"][0:32, 0:N], vecs[0:32, VC_GQ:VC_GQ + 1], L["T1"][0:32, 0:N],
                  ALU.mult, ALU.mult)

        def kproj(hd):
            pk = psb[3]
            P.mm(pk[0:96, 0:N], wk[:, 0, hd, :], ckvT[:, 0, 0:N], start=True, stop=False)
            P.mm(pk[0:96, 0:N], wk[:, 1, hd, :], ckvT[:, 1, 0:N], start=False, stop=False)
            P.mm(pk[0:96, 0:N], emb[0:32, :], krT[0:32, 0:N], start=False, stop=True)
            xs = L["xs96"][1]
            P.act(xs[0:96, 0:N], pk[0:96, 0:N], AF.Square)
            st_[("k", hd)] = (pk, xs)

        def kfin(hd):
            pk, xs = st_.pop(("k", hd))
            pn = abank()
            P.mm(pn[0:96, 0:N], ones_bf[0:96, 0:96], xs[0:96, 0:N])
            P.act(L["T0"][0:96, 0:N], pn[0:96, 0:N], AF.Ln, bias=eps_t[0:96], scale=1.0 / 96)
            P.act(L["T1"][0:96, 0:N], L["T0"][0:96, 0:N], AF.Exp, scale=-0.5)
            P.stt(KT[0:96, hd, kcol0:kcol0 + N], pk[0:96, 0:N], vecs[0:96, VC_GK:VC_GK + 1], L["T1"][0:96, 0:N],
                  ALU.mult, ALU.mult)

        def head_steps(hd):
            return [lambda: qproj(hd), lambda: qfin(hd), lambda: kproj(hd), lambda: kfin(hd)]
        v_rows(ckvT, N, blk0, V)
        for stp in head_steps(0) + head_steps(1):
            stp()
        if first:
            checkpoint("m_kv")
            checkpoint("m_bkv")
        P.label = "mix.attn"
        if sample:
            jl = [(j, 128, 0) for j in range(PAST // 128)] + [(PAST // 128, TS, 0)]
        else:
            jl = []
            for j in range((t0 + N) // 128):
                a = j - t0 // 128
                jl.append((j, 128, 128 * a if a > 0 else 0))
        nj = len(jl)
        LA = 2
        cnt = {"n": 0, "d": 0}

        def att_head(hd, side, fin_prev):
            po = acc_bank()
            pend = []
            ngroups = nj if not sample else (PAST // 128 + 7) // 8 + 1
            every = max(1, ngroups // max(1, len(side))) if side else 1

            def pv_step(item):
                j, kn, qlo, pt, idx = item
                P.mm(po[0:65, qlo:N], V[0:kn, j, hd, 0:65], pt[0:kn, 0:N - qlo], start=(idx == 0), stop=(idx == nj - 1))

            gi = 0
            idx = 0
            while idx < nj:
                j, kn, qlo = jl[idx]
                if sample and kn == 128:
                    grp = [jl[idx + t] for t in range(min(8, nj - idx)) if jl[idx + t][1] == 128]
                else:
                    grp = [jl[idx]]
                ng = len(grp)
                nq = N - qlo
                pss = abank()
                for t, (jj, kk, ql) in enumerate(grp):
                    P.mm(pss[0:kk, t * nq:(t + 1) * nq], KT[0:96, hd, jj * 128:jj * 128 + kk], Qh[0:96, hd, ql:N])
                if (not sample) and j * 128 >= t0:
                    pt = L["ptd"][cnt["d"] % 3]
                    cnt["d"] += 1
                    P.act(pt[0:128, 0:nq], pss[0:128, 0:nq], AF.Exp, scale=QSCALE)
                    P.memset(pt[64:128, 0:64], 0.0, eng="dve")
                else:
                    pt = L["pt"][cnt["n"] % 3]
                    cnt["n"] += 1
                    P.act(pt[0:kn, 0:ng * nq], pss[0:kn, 0:ng * nq], AF.Exp, scale=QSCALE)
                for t, (jj, kk, ql) in enumerate(grp):
                    pend.append((jj, kk, ql, pt[:, t * nq:(t + 1) * nq], idx + t))
                idx += ng
                gi += 1
                while len(pend) > LA * ng:
                    pv_step(pend.pop(0))
                if fin_prev is not None and gi == min(2, ngroups):
                    fin_prev()
                    fin_prev = None
                if side and (gi % every == 0):
                    P.label = "mix.heads_side"
                    side.pop(0)()
                    P.label = "mix.attn"
            while pend:
                pv_step(pend.pop(0))
            if fin_prev is not None:
                fin_prev()
            while side:
                side.pop(0)()
            ri = L["T5"]
            P.act(L["T4"][64:65, 0:N], po[64:65, 0:N], AF.Ln)
            P.act(ri[0:1, 0:N], L["T4"][64:65, 0:N], AF.Exp, scale=-1.0)
            return po, ri

        def att_fin(hd, po, ri):
            pb = abank()
            P.mm(pb[0:64, 0:N], ones_f[0:1, 0:64], ri[0:1, 0:N])
            P.copy(L["T4"][0:64, 0:N], pb[0:64, 0:N])
            P.tt(ob[0:64, hd, 0:N], po[0:64, 0:N], L["T4"][0:64, 0:N], ALU.mult)
        def spatial_step(g):
            P.label = "mix.spatial"
            ps = abank()
            for cb in range(NB):
                P.mm(ps[:, cb * TB:(cb + 1) * TB], vb[0:TB, cb, g * 128:(g + 1) * 128], wst[0:TB, g, 0:TB])
            tm = L["T0"] if g % 2 == 0 else L["T1"]
            for cb in range(NB):
                P.stt(tm[:, cb * TB:(cb + 1) * TB], ps[:, cb * TB:(cb + 1) * TB], vecs[:, VC_GV + g:VC_GV + g + 1],
                      bb[:, g, 0:TB], ALU.mult, ALU.add)
            P.tt(oa[:, g, 0:N], tm[:, 0:N], u[:, g, 0:N], ALU.mult)
            P.label = "mix.attn"

        def side_for(hd):
            if hd + 2 < 8:
                return head_steps(hd + 2)
            return [(lambda g=g: spatial_step(g)) for g in range((hd - 6) * 4, (hd - 6) * 4 + 4)]
        prev = None
        for hd in range(8):
            fp = (lambda h=hd - 1, pr=prev: att_fin(h, *pr)) if prev is not None else None
            prev = att_head(hd, side_for(hd), fp)
        att_fin(7, *prev)
        if first:
            checkpoint("m_att")
        P.label = "mix.merge"
        mg = L["mg"]
        for m in range(8):
            w = stream(CH_MG0 + m).rearrange("p (b c) -> p b c", b=32)
            pga, pgb, pa, pbb = bank(), bank(), bank(), bank()
            for k in range(8):
                P.mm(pga[:, 0:N], w[:, k, :], h[:, k, :], start=(k == 0), stop=(k == 7))
            for k in range(8):
                P.mm(pgb[:, 0:N], w[:, 8 + k, :], h[:, k, :], start=(k == 0), stop=(k == 7))
            for k in range(8):
                P.mm(pa[:, 0:N], w[:, 16 + k, :], oa[:, k, 0:N], start=(k == 0), stop=(k == 7))
            for hd in range(8):
                P.mm(pbb[:, 0:N], w[0:64, 24 + hd, :], ob[0:64, hd, 0:N], start=(hd == 0), stop=(hd == 7))
            P.act(L["T2"][:, 0:N], pga[:, 0:N], AF.Sigmoid, bias=vecs[:, VC_BG + m:VC_BG + m + 1])
            P.act(L["T3"][:, 0:N], pgb[:, 0:N], AF.Sigmoid, bias=vecs[:, VC_BG + 8 + m:VC_BG + 9 + m])
            P.tt(L["T2"][:, 0:N], pa[:, 0:N], L["T2"][:, 0:N], ALU.mult)
            P.tt(L["T3"][:, 0:N], pbb[:, 0:N], L["T3"][:, 0:N], ALU.mult)
            P.tt(mg[:, m, 0:N], L["T2"][:, 0:N], L["T3"][:, 0:N], ALU.add)
        P.label = "mix.out"
        for half in range(2):
            w = stream(CH_O0 + half).rearrange("p (k c) -> p k c", k=8)
            for m4 in range(4):
                m = half * 4 + m4
                ps = bank()
                for k in range(8):
                    P.mm(ps[:, 0:N], w[:, k, m4 * 128:(m4 + 1) * 128], mg[:, k, 0:N], start=(k == 0), stop=(k == 7))
                P.stt(xsub[:, m, :], ps[:, 0:N], gt[:, 1, m, s:s + 1], xsub[:, m, :], ALU.mult, ALU.add)

    def main_body():
        dump('modt', modt.rearrange('p a b -> p (a b)'))
        dump('wsm', wsm)
        checkpoint('init')
        if nseq > 0:
            P.memset(Vp[:, :, :, 64:65], 1.0, eng="pool")
        tiles = [(sq, st) for sq in range(nseq) for st in range(2)]
        for ti, (sq, st) in enumerate(tiles):
            ffn(xt, 512, 2, CH_F1U, CH_F1D, 0, sq, Lp, use_pool=(ti > 0))
            if "x1" in dbg and sq == 0:
                P.dma(dbg["x1"][st], xt_flat, is_out=True)
            checkpoint("ffn1")
            for sub in range(2):
                t0 = st * 1024 + sub * 512
                mixer(xt[:, sub], 512, sq, False, t0, Lp, KTp, Vp,
                      d_ckvp[sq, t0:t0 + 512, :], d_krp[sq, t0:t0 + 512, :], t0)
            if "x2" in dbg and sq == 0:
                P.dma(dbg["x2"][st], xt_flat, is_out=True)
            checkpoint("mix")
            nxt = tiles[ti + 1] if ti + 1 < len(tiles) else None

            def after_m(m, sq=sq, st=st, nxt=nxt):
                P.dma(xdram(d_yp, sq, st)[:, :, m, :], xt[:, :, m, :], is_out=True)
                if nxt is not None:
                    P.dma(xt[:, :, m, :], xdram(d_xp, nxt[0], nxt[1])[:, :, m, :])
            ffn(xt, 512, 2, CH_F2U, CH_F2D, 2, sq, Lp, after_m=after_m)

        if CFG["sample"]:
            SA = Alloc(RES_END, ASZ)
            NK = PAST + TS
            KTs = SA.v(BF16, [8, NK])
            Vs = SA.v(BF16, [NK // 128 + 1, 8, 65])
            xs = SA.v(F32, [1, 8, TS])
            Ls = phase_layout(SA, TS, True)
            c32 = SA.v(F32, [2, 512])
            c16 = SA.v(BF16, [2, 512])
            k32 = SA.v(F32, [512])
            k16 = SA.v(BF16, [512])
            P.memset(Vs[:, :, :, 64:65], 1.0, eng="pool")
            P.dma(Ls["gvb"][0:64, :], d_gvb)
            for pc in range(PAST // 512):
                P.dma(c32, d_cckv[:, :, pc * 512:(pc + 1) * 512])
                P.dma(k32[0:32, :], d_ckr[:, pc * 512:(pc + 1) * 512])
                P.copy(c16.rearrange("p a b -> p (a b)"), c32.rearrange("p a b -> p (a b)"))
                P.copy(k16[0:32, :], k32[0:32, :])
                build_kv(c16, k16, 512, pc * 512, pc * 4, Ls, KTs, Vs)
            P.dma(xs.rearrange("p a b c -> p (a b c)"), d_xs)
            ffn(xs, TS, 1, CH_F1U, CH_F1D, 0, 4, Ls)
            mixer(xs[:, 0], TS, 4, True, 0, Ls, KTs, Vs, d_ckvs, d_krs, SEQ)
            ffn(xs, TS, 1, CH_F2U, CH_F2D, 2, 4, Ls)
            P.dma(d_ys, xs.rearrange("p a b c -> p (a b c)"), is_out=True)

    try:
        main_body()
    except _Stop:
        pass
    P.finish()
    P.emit()
    es.close()
    build_program.last_prog = P
    return nc


def _rope_tables():
    half = 16
    freqs = (np.float32(10000.0) ** (-np.arange(half, dtype=np.float32) / np.float32(half))).astype(np.float32)
    pos = np.concatenate([np.arange(SEQ, dtype=np.float32), np.arange(TS, dtype=np.float32) + np.float32(PAST)])
    ang = (pos[:, None] * freqs[None, :]).astype(np.float32)
    cos, sin = np.cos(ang).astype(np.float32), np.sin(ang).astype(np.float32)
    ck = np.concatenate([cos, cos], axis=1)
    sk = np.concatenate([-sin, sin], axis=1)
    return np.ascontiguousarray(ck.T), np.ascontiguousarray(sk.T), np.ascontiguousarray(ck), np.ascontiguousarray(sk)


def _kchunks(w, ncols):
    kk = w.shape[0] // 128
    return np.ascontiguousarray(w.reshape(kk, 128, ncols).transpose(1, 0, 2).reshape(128, kk * ncols))


def _pad(a):
    out = np.zeros((128, CW), np.float32)
    out[:a.shape[0], :a.shape[1]] = a
    return out


def _weight_chunks(w_ffn1_up, w_ffn1_down, w_in, w_branch_a, w_branch_b, w_out, w_ffn2_up, w_ffn2_down):
    ch = np.zeros((NCH, 128, CW), np.float32)

    def ffn_chunks(up, down, bu, bd):
        for gi in range(11):
            blk = np.zeros((8, 128, 2, 256), np.float32)
            upk = up.reshape(8, 128, 2 * DFF)
            for jj in range(2):
                j = 2 * gi + jj
                blk[:, :, jj, 0:128] = upk[:, :, j * 128:(j + 1) * 128]
                blk[:, :, jj, 128:256] = upk[:, :, DFF + j * 128:DFF + (j + 1) * 128]
            ch[bu + gi] = blk.transpose(1, 0, 2, 3).reshape(128, CW)
        dk = down.reshape(NJ, 128, D)
        for m in range(8):
            ch[bd + m] = _pad(dk[:, :, m * 128:(m + 1) * 128].transpose(1, 0, 2).reshape(128, NJ * 128))

    ffn_chunks(w_ffn1_up, w_ffn1_down, CH_F1U, CH_F1D)
    ffn_chunks(w_ffn2_up, w_ffn2_down, CH_F2U, CH_F2D)
    ch[CH_Q] = _pad(_kchunks(w_in[:, 2048:2432], 384))
    kr = w_in[:, 2688:2720]
    kr_sw = np.concatenate([kr[:, 16:32], kr[:, 0:16]], axis=1)
    ch[CH_KV] = _pad(_kchunks(np.concatenate([w_in[:, 2432:2720], kr_sw], axis=1), 320))
    for half in range(2):
        ch[CH_U0 + half] = _kchunks(w_in[:, half * 512:(half + 1) * 512], 512)
        ch[CH_V0 + half] = _kchunks(w_in[:, 1024 + half * 512:1024 + (half + 1) * 512], 512)
        ch[CH_O0 + half] = _kchunks(w_out[:, half * 512:(half + 1) * 512], 512)
    for m in range(8):
        blk = np.zeros((128, 32, 128), np.float32)
        blk[:, 0:8, :] = w_in[:, 2720 + m * 128:2720 + (m + 1) * 128].reshape(8, 128, 128).transpose(1, 0, 2)
        blk[:, 8:16, :] = w_in[:, 3744 + m * 128:3744 + (m + 1) * 128].reshape(8, 128, 128).transpose(1, 0, 2)
        blk[:, 16:24, :] = w_branch_a[:, m * 128:(m + 1) * 128].reshape(8, 128, 128).transpose(1, 0, 2)
        blk[0:64, 24:32, :] = w_branch_b[:, m * 128:(m + 1) * 128].reshape(8, 64, 128).transpose(1, 0, 2)
        ch[CH_MG0 + m] = blk.reshape(128, CW)
    return ch


def _head_perm():
    return np.concatenate([np.arange(64, 96), np.arange(0, 64)])


def _small_weights(w_uq, w_uk, w_uv, gmlp_ws):
    ws = np.zeros((128, WS_N), np.float32)
    perm = _head_perm()
    uq = w_uq.reshape(3, 128, 8, 96)
    ws[:, WS_Q:WS_QS] = uq[:, :, :, perm].transpose(1, 0, 2, 3).reshape(128, -1)
    swap = np.concatenate([np.arange(80, 96), np.arange(64, 80)])
    ws[:, WS_QS:WS_K] = uq[:, :, :, swap].transpose(1, 0, 2, 3).reshape(128, -1)
    uk = np.zeros((2, 128, 8, 96), np.float32)
    uk[:, :, :, 32:96] = w_uk.reshape(2, 128, 8, 64)
    ws[:, WS_K:WS_V] = uk.transpose(1, 0, 2, 3).reshape(128, -1)
    ws[:, WS_V:WS_ST] = w_uv.reshape(2, 128, 512).transpose(1, 0, 2).reshape(128, -1)
    ws[:, WS_ST:WS_EMB] = gmlp_ws.transpose(2, 0, 1).reshape(128, -1)
    ws[0:32, WS_EMB:WS_EMB + 32] = np.eye(32, dtype=np.float32)
    ws[:, WS_ID:WS_N] = np.eye(128, dtype=np.float32)
    return ws


_NC_CACHE = {}


def kernel(x_prompt, x_sample, c_prompt, c_sample, cache_ckv, cache_krope,
           w_mod, b_mod, g_ffn1, w_ffn1_up, w_ffn1_down, g_mix, w_in, g_gmlp_v, gmlp_ws, gmlp_b,
           g_q_lat, w_uq, g_kv_lat, w_uk, w_uv, g_qnorm, g_knorm, b_gate, w_branch_a, w_branch_b,
           w_out, g_ffn2, w_ffn2_up, w_ffn2_down):
    ncore = 8
    key = (CFG["nseq"], CFG["sample"], CFG.get("stop"), tuple(sorted(DEBUG.keys())))
    if key not in _NC_CACHE:
        _NC_CACHE[key] = build_program()
    nc = _NC_CACHE[key]
    in_maps = _prep(x_prompt, x_sample, c_prompt, c_sample, cache_ckv, cache_krope,
                    w_mod, b_mod, g_ffn1, w_ffn1_up, w_ffn1_down, g_mix, w_in, g_gmlp_v, gmlp_ws, gmlp_b,
                    g_q_lat, w_uq, g_kv_lat, w_uk, w_uv, g_qnorm, g_knorm, b_gate, w_branch_a, w_branch_b,
                    w_out, g_ffn2, w_ffn2_up, w_ffn2_down)
    res = run_bass_kernel_spmd(nc, in_maps, core_ids=list(range(ncore)))
    R = res.results
    kernel.last_results = R
    return _gather(R)


def _prep(x_prompt, x_sample, c_prompt, c_sample, cache_ckv, cache_krope,
          w_mod, b_mod, g_ffn1, w_ffn1_up, w_ffn1_down, g_mix, w_in, g_gmlp_v, gmlp_ws, gmlp_b,
          g_q_lat, w_uq, g_kv_lat, w_uk, w_uv, g_qnorm, g_knorm, b_gate, w_branch_a, w_branch_b,
          w_out, g_ffn2, w_ffn2_up, w_ffn2_down):
    f = lambda a: np.asarray(a, dtype=np.float32)
    x_prompt, x_sample, c_prompt, c_sample = f(x_prompt), f(x_sample), f(c_prompt), f(c_sample)
    cache_ckv, cache_krope = f(cache_ckv)[0], f(cache_krope)[0]
    ncore = 8

    wch = _weight_chunks(f(w_ffn1_up)[0], f(w_ffn1_down)[0], f(w_in)[0], f(w_branch_a)[0], f(w_branch_b)[0],
                         f(w_out)[0], f(w_ffn2_up)[0], f(w_ffn2_down)[0])
    wsmall = _small_weights(f(w_uq)[0], f(w_uk)[0], f(w_uv)[0], f(gmlp_ws)[0])
    maskT = np.triu(np.ones((128, 128), np.float32))
    perm = _head_perm()
    vecs = np.zeros((128, VC_N), np.float32)
    fm = lambda v: np.ascontiguousarray(v.reshape(-1, 128).T)
    vecs[:, VC_G1:VC_G1 + 8] = fm(f(g_ffn1)[0])
    vecs[:, VC_G2:VC_G2 + 8] = fm(f(g_mix)[0])
    vecs[:, VC_G3:VC_G3 + 8] = fm(f(g_ffn2)[0])
    vecs[:, VC_BMOD:VC_BMOD + 72] = fm(f(b_mod)[0])
    vecs[:, VC_GQL:VC_GQL + 3] = fm(f(g_q_lat)[0])
    vecs[:, VC_BG:VC_BG + 16] = fm(f(b_gate)[0])
    vecs[0:96, VC_GQ] = f(g_qnorm)[0][perm]
    vecs[0:96, VC_GK] = f(g_knorm)[0][perm]
    vecs[:, VC_GV:VC_GV + 8] = fm(f(g_gmlp_v)[0])
    vecs[:, VC_EPS] = EPS
    vecs[0:5, VC_ID5:VC_ID5 + 5] = np.eye(5, dtype=np.float32)
    gkvb = np.ascontiguousarray(np.broadcast_to(f(g_kv_lat)[0][None, :], (128, 256)))
    bbr = np.ascontiguousarray(np.broadcast_to(f(gmlp_b)[0].reshape(1, 1024), (128, 1024)))
    gvb = np.ascontiguousarray(np.broadcast_to(f(g_gmlp_v)[0][None, :], (64, 1024)))
    rqc, rqs, rkc, rks = _rope_tables()
    wmod = np.ascontiguousarray(f(w_mod)[0])

    in_maps = []
    for c in range(ncore):
        xp = x_prompt[4 * c:4 * c + 4]
        xpl = xp.reshape(4, 2, 2, 512, 8, 128).transpose(0, 1, 5, 2, 4, 3).reshape(4, 2, 128, 8192)
        xsl = x_sample[c].reshape(TS, 8, 128).transpose(2, 1, 0).reshape(128, 8 * TS)
        call = np.concatenate([c_prompt[4 * c:4 * c + 4], c_sample[c:c + 1]], axis=0)
        ct = call.reshape(5, 8, 128).transpose(2, 1, 0).reshape(128, 40)
        cckv = cache_ckv[c].reshape(PAST, 2, 128).transpose(2, 1, 0)
        ckr = cache_krope[c].T
        in_maps.append({
            "xp": np.ascontiguousarray(xpl), "xs": np.ascontiguousarray(xsl), "ct": np.ascontiguousarray(ct),
            "wmod": wmod, "wch": wch, "wsmall": wsmall, "maskT": maskT, "vecs": vecs, "gkvb": gkvb, "bb": bbr,
            "gvb": gvb, "ropeqc": rqc, "ropeqs": rqs, "ropekc": rkc, "ropeks": rks,
            "cckv": np.ascontiguousarray(cckv), "ckr": np.ascontiguousarray(ckr),
        })
    return in_maps


def _gather(R):
    ncore = 8
    y_p = np.zeros((32, SEQ, D), np.float32)
    y_s = np.zeros((8, TS, D), np.float32)
    ckv_p = np.zeros((1, 32, SEQ, 256), np.float32)
    kr_p = np.zeros((1, 32, SEQ, 32), np.float32)
    ckv_s = np.zeros((1, 8, TS, 256), np.float32)
    kr_s = np.zeros((1, 8, TS, 32), np.float32)
    vg_s = np.zeros((1, 8, TS, D), np.float32)
    for c in range(ncore):
        yp = np.asarray(R[c]["yp"]).reshape(4, 2, 128, 2, 8, 512).transpose(0, 1, 3, 5, 4, 2).reshape(4, SEQ, D)
        y_p[4 * c:4 * c + 4] = yp
        y_s[c] = np.asarray(R[c]["ys"]).reshape(128, 8, TS).transpose(2, 1, 0).reshape(TS, D)
        ckv_p[0, 4 * c:4 * c + 4] = np.asarray(R[c]["ckvp"])
        kr_p[0, 4 * c:4 * c + 4] = np.asarray(R[c]["krp"])
        ckv_s[0, c] = np.asarray(R[c]["ckvs"])
        kr_s[0, c] = np.asarray(R[c]["krs"])
        vg_s[0, c] = np.asarray(R[c]["vgs"])
    return (y_p, y_s, ckv_p, kr_p, ckv_s, kr_s, vg_s)
```
